# Optimizing a Trainium2 kernel written in Bass

```python
import jax
import jax.numpy as jnp
from jax import lax
import numpy as np

D_MODEL = 1024
BATCH = 4
SEQ = 4096
DEPTH = 4

GRID_W = 64
CTX_LEN = 256
N_BRANCH = 4
MIX_W = D_MODEL // N_BRANCH
HEAD_DIM = 64
N_HEADS = MIX_W // HEAD_DIM
MLSTM_CHUNK = 128
DECAY_LORA = 32
ICLR_LORA = 32
GATE_LORA = 64
RWKV_GN_EPS = 64e-5
Q_LORA = 192
KV_LORA = 128
QK_NOPE = 64
QK_ROPE = 32
V_HEAD = HEAD_DIM
ROPE_BASE = 10000.0
Q_BLOCK = 128
LRU_C = 8.0
CONV_W = 4
LRU_BLOCKS = N_HEADS
LRU_BS = MIX_W // LRU_BLOCKS
D_FF = 4 * D_MODEL
ALPHA = (2 * DEPTH) ** 0.25
BETA = (8 * DEPTH) ** -0.25
LN_EPS = 1e-5
RMS_EPS = 1e-6
MLSTM_COLS = 4 * MIX_W + 4 * N_HEADS
RWKV_COLS = 3 * MIX_W + 2 * DECAY_LORA + 2 * ICLR_LORA + GATE_LORA
MLA_COLS = Q_LORA + KV_LORA + QK_ROPE
LRU_COLS = MIX_W
IN_WIDTHS = (N_BRANCH * D_MODEL, MLSTM_COLS, RWKV_COLS, MLA_COLS, LRU_COLS)
D_IN = N_BRANCH * D_MODEL + MLSTM_COLS + RWKV_COLS + MLA_COLS + LRU_COLS

kernel_name = 'hybrid_flow_trunk_mlstm_rwkv7_mla_rglru'


def _split(u, widths):
    parts, start = [], 0
    for w in widths:
        parts.append(u[..., start:start + w])
        start += w
    return parts


def _maybe_flip(z, axis, rev):
    return jnp.flip(z, axis) if rev else z


def _layer_norm(x, g, b):
    xf = x.astype(jnp.float32)
    mu = jnp.mean(xf, -1, keepdims=True)
    var = jnp.mean(jnp.square(xf - mu), -1, keepdims=True)
    return ((xf - mu) * lax.rsqrt(var + LN_EPS) * g + b).astype(x.dtype)


def _head_norm(y, eps):
    mu = jnp.mean(y, -1, keepdims=True)
    var = jnp.mean(jnp.square(y - mu), -1, keepdims=True)
    return (y - mu) * lax.rsqrt(var + eps)


def _rms_norm(x, g):
    xf = x.astype(jnp.float32)
    return (xf * lax.rsqrt(jnp.mean(jnp.square(xf), -1, keepdims=True) + RMS_EPS) * g).astype(x.dtype)


def _modulate(x, shift, scale):
    return x * (1.0 + scale) + shift


def _axial_rope_tables(rows):
    quarter = QK_ROPE // 4
    inv = 1.0 / (ROPE_BASE ** (jnp.arange(quarter, dtype=jnp.float32) / quarter))
    row = jnp.repeat(jnp.arange(rows, dtype=jnp.float32), GRID_W)
    col = jnp.tile(jnp.arange(GRID_W, dtype=jnp.float32), rows)
    ang_r = row[:, None] * inv
    ang_c = col[:, None] * inv
    e = lambda z: z[None, :, None, :]
    return (e(jnp.cos(ang_r)), e(jnp.sin(ang_r)), e(jnp.cos(ang_c)), e(jnp.sin(ang_c)))


def _rotate(x, cos, sin):
    x1, x2 = jnp.split(x, 2, axis=-1)
    return jnp.concatenate([x1 * cos - x2 * sin, x2 * cos + x1 * sin], -1)


def _axial_rope(x, tabs):
    cos_r, sin_r, cos_c, sin_c = tabs
    xr, xc = jnp.split(x, 2, axis=-1)
    return jnp.concatenate([_rotate(xr, cos_r, sin_r), _rotate(xc, cos_c, sin_c)], -1).astype(x.dtype)


def _attend(q, k, v, scale):
    s = jnp.einsum('bqhd,bkhd->bhqk', q, k).astype(jnp.float32) * scale
    p = jax.nn.softmax(s, axis=-1).astype(v.dtype)
    return jnp.einsum('bhqk,bkhd->bqhd', p, v)


def _mlstm_chunkwise(q, k, v, li, lf, state):
    bsz, nh, t, dh = q.shape
    nc, cl = t // MLSTM_CHUNK, MLSTM_CHUNK
    q, k, v = [z.reshape(bsz, nh, nc, cl, dh) for z in (q, k, v)]
    li, lf = li.reshape(bsz, nh, nc, cl), lf.reshape(bsz, nh, nc, cl)
    bcum = jnp.cumsum(lf, axis=-1)
    g_tot = bcum[..., -1]
    a_end = g_tot[..., None] - bcum + li
    m_loc = jnp.max(a_end, axis=-1)
    w_end = jnp.exp(a_end - m_loc[..., None])
    c_loc = jnp.einsum('bhnlk,bhnlv->bhnkv', k * w_end[..., None], v)
    n_loc = jnp.einsum('bhnl,bhnlk->bhnk', w_end, k)

    def step(carry, inp):
        c0, n0, m0 = carry
        g_i, m_i, c_i, n_i = inp
        m1 = jnp.maximum(g_i + m0, m_i)
        s_old, s_new = jnp.exp(g_i + m0 - m1), jnp.exp(m_i - m1)
        c1 = s_old[..., None, None] * c0 + s_new[..., None, None] * c_i
        n1 = s_old[..., None] * n0 + s_new[..., None] * n_i
        return (c1, n1, m1), (c0, n0, m0)

    xs = tuple(jnp.moveaxis(z, 2, 0) for z in (g_tot, m_loc, c_loc, n_loc))
    final, prev = lax.scan(step, state, xs)
    c_prev, n_prev, m_prev = (jnp.moveaxis(z, 0, 2) for z in prev)
    lower = jnp.tril(jnp.ones((cl, cl), dtype=bool))
    log_d = jnp.where(lower, bcum[..., :, None] - bcum[..., None, :] + li[..., None, :], -jnp.inf)
    log_prev = bcum + m_prev[..., None]
    m_j = jnp.maximum(log_prev, jnp.max(log_d, axis=-1))
    qk = jnp.einsum('bhnjd,bhnsd->bhnjs', q, k) * jnp.exp(log_d - m_j[..., None])
    s_prev = jnp.exp(log_prev - m_j)
    num = (jnp.einsum('bhnjs,bhnsd->bhnjd', qk, v)
           + s_prev[..., None] * jnp.einsum('bhnjk,bhnkv->bhnjv', q, c_prev))
    den = jnp.sum(qk, -1) + s_prev * jnp.einsum('bhnjk,bhnk->bhnj', q, n_prev)
    h = num / jnp.maximum(jnp.abs(den), jnp.exp(-m_j))[..., None]
    return h.reshape(bsz, nh, t, dh), final


def _mlstm_mixer(u_ctx, u_lat, need_ctx, gate_b, norm_g):
    def prep(u):
        u = u.astype(jnp.float32)
        bsz, t = u.shape[:2]
        q, k, v, o, ifg = _split(u, (MIX_W, MIX_W, MIX_W, MIX_W, 4 * N_HEADS))
        hd = lambda z: z.reshape(bsz, t, N_HEADS, HEAD_DIM).transpose(0, 2, 1, 3)
        ifg = ifg.reshape(bsz, t, 2, 2, N_HEADS) + gate_b
        li = ifg[:, :, :, 0].transpose(0, 2, 3, 1)
        lf = jax.nn.log_sigmoid(ifg[:, :, :, 1]).transpose(0, 2, 3, 1)
        return hd(q), hd(k) * HEAD_DIM ** -0.5, hd(v), li, lf, o

    def finish(h, o):
        bsz, _, t, _ = h.shape
        h = _head_norm(h.transpose(0, 2, 1, 3), LN_EPS).reshape(bsz, t, MIX_W) * norm_g
        return h * jax.nn.sigmoid(o)

    qc, kc, vc, lic, lfc, oc = prep(u_ctx)
    ql, kl, vl, lil, lfl, ol = prep(u_lat)
    bsz = u_lat.shape[0]
    zero_state = (jnp.zeros((bsz, N_HEADS, HEAD_DIM, HEAD_DIM), jnp.float32),
                  jnp.zeros((bsz, N_HEADS, HEAD_DIM), jnp.float32),
                  jnp.zeros((bsz, N_HEADS), jnp.float32))
    h_c, h_l = [], []
    for d in range(2):
        f = lambda z, rev=(d == 1): _maybe_flip(z, 2, rev)
        hc, st = _mlstm_chunkwise(f(qc), f(kc), f(vc), f(lic[:, d]), f(lfc[:, d]), zero_state)
        hl, _ = _mlstm_chunkwise(f(ql), f(kl), f(vl), f(lil[:, d]), f(lfl[:, d]), st)
        h_c.append(f(hc))
        h_l.append(f(hl))
    out_lat = finish(h_l[0] + h_l[1], ol).astype(u_lat.dtype)
    out_ctx = finish(h_c[0] + h_c[1], oc).astype(u_ctx.dtype) if need_ctx else None
    return out_ctx, out_lat


def _rwkv7_scan(r, w, k, v, kk, a, s0):
    def step(s, inp):
        r_t, w_t, k_t, v_t, kk_t, a_t = inp
        sa = jnp.einsum('bhvk,bhk->bhv', s, kk_t)
        s = (s * w_t[:, :, None, :] - sa[..., None] * (kk_t * a_t)[:, :, None, :]
             + v_t[..., None] * k_t[:, :, None, :])
        return s, jnp.einsum('bhvk,bhk->bhv', s, r_t)

    xs = tuple(jnp.swapaxes(z, 0, 1) for z in (r, w, k, v, kk, a))
    s_fin, y = lax.scan(step, s0, xs)
    return jnp.swapaxes(y, 0, 1), s_fin


def _rwkv7_mixer(u_ctx, u_lat, need_ctx, mu, w0, w2, a0, a2, g2, k_k, k_a, r_k, gn_g):
    def prep(u):
        u = u.astype(jnp.float32)
        bsz, t = u.shape[:2]
        zero = jnp.zeros_like(u[:, :1])
        prev = jnp.concatenate([zero, u[:, :-1]], 1)
        nxt = jnp.concatenate([u[:, 1:], zero], 1)
        u = u + mu * (0.5 * (prev + nxt) - u)
        r, k, v, wd, ad, gd = _split(u, (MIX_W, MIX_W, MIX_W, 2 * DECAY_LORA, 2 * ICLR_LORA, GATE_LORA))
        wd = wd.reshape(bsz, t, 2, DECAY_LORA)
        ad = ad.reshape(bsz, t, 2, ICLR_LORA)
        w_log = -jax.nn.softplus(-(w0 + jnp.einsum('btdr,drc->btdc', jnp.tanh(wd), w2))) - 0.5
        decay = jnp.exp(-jnp.exp(w_log))
        a = jax.nn.sigmoid(a0 + jnp.einsum('btdr,drc->btdc', ad, a2))
        kk = (k * k_k).reshape(bsz, t, N_HEADS, HEAD_DIM)
        kk = kk * lax.rsqrt(jnp.maximum(jnp.sum(jnp.square(kk), -1, keepdims=True), 1e-12))
        kd = k[:, :, None, :] * (1.0 + (a - 1.0) * k_a)
        g = jax.nn.sigmoid(gd) @ g2
        return r, k, v, kk, decay, a, kd, g

    def run(p, d, s0):
        r, k, v, kk, decay, a, kd, g = p
        bsz, t = r.shape[:2]
        hd = lambda z: z.reshape(bsz, t, N_HEADS, HEAD_DIM)
        f = lambda z: _maybe_flip(z, 1, d == 1)
        y, s = _rwkv7_scan(f(hd(r)), f(hd(decay[:, :, d])), f(hd(kd[:, :, d])), f(hd(v)), f(kk),
                           f(hd(a[:, :, d])), s0)
        return f(y), s

    def finish(y, p):
        r, k, v, g = p[0], p[1], p[2], p[7]
        bsz, t = r.shape[:2]
        hd = lambda z: z.reshape(bsz, t, N_HEADS, HEAD_DIM)
        y = _head_norm(y, RWKV_GN_EPS) * gn_g.reshape(N_HEADS, HEAD_DIM)
        y = y + jnp.sum(hd(r) * hd(k) * r_k, -1, keepdims=True) * hd(v)
        return y.reshape(bsz, t, MIX_W) * g

    pc, pl = prep(u_ctx), prep(u_lat)
    bsz = u_lat.shape[0]
    s_zero = jnp.zeros((bsz, N_HEADS, HEAD_DIM, HEAD_DIM), jnp.float32)
    y_c, y_l = [], []
    for d in range(2):
        yc, st = run(pc, d, s_zero)
        yl, _ = run(pl, d, st)
        y_c.append(yc)
        y_l.append(yl)
    out_lat = finish(y_l[0] + y_l[1], pl).astype(u_lat.dtype)
    out_ctx = finish(y_c[0] + y_c[1], pc).astype(u_ctx.dtype) if need_ctx else None
    return out_ctx, out_lat


def _mla_mixer(u_ctx, u_lat, need_ctx, rope_tabs, qn_g, kvn_g, wuq, wuk, wuv):
    def prep(u):
        bsz, t = u.shape[:2]
        cq, ckv, kr = _split(u, (Q_LORA, KV_LORA, QK_ROPE))
        q = (_rms_norm(cq, qn_g) @ wuq).reshape(bsz, t, N_HEADS, QK_NOPE + QK_ROPE)
        ckv = _rms_norm(ckv, kvn_g)
        k_nope = (ckv @ wuk).reshape(bsz, t, N_HEADS, QK_NOPE)
        v = (ckv @ wuv).reshape(bsz, t, N_HEADS, V_HEAD)
        return q, k_nope, kr[:, :, None, :], v

    def keys(k_nope, k_rope):
        return jnp.concatenate([k_nope, jnp.broadcast_to(k_rope, k_nope.shape[:3] + (QK_ROPE,))], -1)

    scale = (QK_NOPE + QK_ROPE) ** -0.5
    qc, knc, krc, vc = prep(u_ctx)
    ql, knl, krl, vl = prep(u_lat)
    ql = jnp.concatenate([ql[..., :QK_NOPE], _axial_rope(ql[..., QK_NOPE:], rope_tabs)], -1)
    kc = keys(knc, krc)
    kl = keys(knl, _axial_rope(krl, rope_tabs))
    k_all = jnp.concatenate([kc, kl], 1)
    v_all = jnp.concatenate([vc, vl], 1)
    bsz, t = ql.shape[:2]
    qb = ql.reshape(bsz, t // Q_BLOCK, Q_BLOCK, N_HEADS, QK_NOPE + QK_ROPE).swapaxes(0, 1)
    o_lat = lax.map(lambda q_blk: _attend(q_blk, k_all, v_all, scale), qb)
    o_lat = o_lat.swapaxes(0, 1).reshape(bsz, t, N_HEADS * V_HEAD)
    o_ctx = _attend(qc, kc, vc, scale).reshape(bsz, -1, N_HEADS * V_HEAD) if need_ctx else None
    return o_ctx, o_lat


def _linear_scan(a, b, h0):
    def comb(e1, e2):
        a1, b1 = e1
        a2, b2 = e2
        return a1 * a2, a2 * b1 + b2
    a_cum, b_cum = lax.associative_scan(comb, (a, b), axis=1)
    h = b_cum + a_cum * h0[:, None, :]
    return h, h[:, -1]


def _rglru_mixer(u_ctx, u_lat, need_ctx, conv_w, conv_b, wa, ba, wx, bx, lam):
    def conv(u):
        y = lax.conv_general_dilated(u.astype(jnp.float32), conv_w.astype(jnp.float32)[:, None, :], (1,),
                                     [(CONV_W // 2, CONV_W - 1 - CONV_W // 2)],
                                     dimension_numbers=('NWC', 'WIO', 'NWC'), feature_group_count=MIX_W)
        return y + conv_b

    def coeffs(xc, d):
        bsz, t = xc.shape[:2]
        xb = xc.reshape(bsz, t, LRU_BLOCKS, LRU_BS)
        r = jax.nn.sigmoid(jnp.einsum('btni,nij->btnj', xb, wa[d]).reshape(bsz, t, MIX_W) + ba[d])
        i = jax.nn.sigmoid(jnp.einsum('btni,nij->btnj', xb, wx[d]).reshape(bsz, t, MIX_W) + bx[d])
        log_a = -LRU_C * r * jax.nn.softplus(-lam[d])
        return jnp.exp(log_a), jnp.sqrt(-jnp.expm1(2.0 * log_a)) * (i * xc)

    xc_c, xc_l = conv(u_ctx), conv(u_lat)
    h0 = jnp.zeros((u_lat.shape[0], MIX_W), jnp.float32)
    h_c, h_l = [], []
    for d in range(2):
        f = lambda z, rev=(d == 1): _maybe_flip(z, 1, rev)
        a, b = coeffs(xc_c, d)
        hc, st = _linear_scan(f(a), f(b), h0)
        a, b = coeffs(xc_l, d)
        hl, _ = _linear_scan(f(a), f(b), st)
        h_c.append(f(hc))
        h_l.append(f(hl))
    out_lat = (h_l[0] + h_l[1]).astype(u_lat.dtype)
    out_ctx = (h_c[0] + h_c[1]).astype(u_ctx.dtype) if need_ctx else None
    return out_ctx, out_lat


def _merge(gate_u, outs, w_branch, w_out):
    bsz, t = gate_u.shape[:2]
    gates = jax.nn.sigmoid(gate_u).reshape(bsz, t, N_BRANCH, D_MODEL)
    o = jnp.stack(outs, axis=2)
    y = jnp.einsum('btnm,nmd->btnd', o, w_branch)
    return jnp.einsum('btnd,de->bte', gates * y, w_out)


def _mixing_sublayer(h_ctx, h_lat, need_ctx, rope_tabs, w_in, mlstm_gate_b, mlstm_norm_g, rwkv_mu,
                     rwkv_w0, rwkv_w2, rwkv_a0, rwkv_a2, rwkv_g2, rwkv_kk, rwkv_ka, rwkv_rk, rwkv_gn_g,
                     mla_qn_g, mla_kvn_g, mla_wuq, mla_wuk, mla_wuv, lru_conv_w, lru_conv_b, lru_wa,
                     lru_ba, lru_wx, lru_bx, lru_lambda, w_branch, w_out):
    g_c, ml_c, rw_c, mla_c, lru_c = _split(h_ctx @ w_in, IN_WIDTHS)
    g_l, ml_l, rw_l, mla_l, lru_l = _split(h_lat @ w_in, IN_WIDTHS)
    a_c, a_l = _mlstm_mixer(ml_c, ml_l, need_ctx, mlstm_gate_b, mlstm_norm_g)
    b_c, b_l = _rwkv7_mixer(rw_c, rw_l, need_ctx, rwkv_mu, rwkv_w0, rwkv_w2, rwkv_a0, rwkv_a2, rwkv_g2,
                            rwkv_kk, rwkv_ka, rwkv_rk, rwkv_gn_g)
    c_c, c_l = _mla_mixer(mla_c, mla_l, need_ctx, rope_tabs, mla_qn_g, mla_kvn_g, mla_wuq, mla_wuk, mla_wuv)
    d_c, d_l = _rglru_mixer(lru_c, lru_l, need_ctx, lru_conv_w, lru_conv_b, lru_wa, lru_ba, lru_wx, lru_bx,
                            lru_lambda)
    y_lat = _merge(g_l, (a_l, b_l, c_l, d_l), w_branch, w_out)
    y_ctx = _merge(g_c, (a_c, b_c, c_c, d_c), w_branch, w_out) if need_ctx else None
    return y_ctx, y_lat


def _sq_relu_mlp(h, w1, w2):
    return jnp.square(jax.nn.relu(h @ w1)) @ w2


def setup_inputs(seed: int = 0) -> dict:
    key = jax.random.key(seed)
    ks = iter(jax.random.split(key, 64))

    def nrm(shape, scale):
        return jax.random.normal(next(ks), shape, jnp.float32) * scale

    def near_one(shape):
        return 1.0 + nrm(shape, 0.02)

    L = DEPTH
    x = nrm((BATCH, SEQ, D_MODEL), 1.0)
    c = nrm((BATCH, D_MODEL), 1.0)
    ctx = nrm((BATCH, CTX_LEN, D_MODEL), 1.0)
    c_ctx = nrm((D_MODEL,), 1.0)
    w_mod = nrm((L, D_MODEL, 6 * D_MODEL), 0.5 * D_MODEL ** -0.5)
    b_mod = nrm((L, 6 * D_MODEL), 0.02)
    w_in = nrm((L, D_MODEL, D_IN), D_MODEL ** -0.5)
    f_base = jnp.linspace(3.0, 6.0, N_HEADS, dtype=jnp.float32)
    mlstm_gate_b = jnp.concatenate([nrm((L, 2, 1, N_HEADS), 0.1),
                                    f_base + nrm((L, 2, 1, N_HEADS), 0.1)], axis=2)
    mlstm_norm_g = near_one((L, MIX_W))
    rwkv_mu = jax.random.uniform(next(ks), (L, RWKV_COLS), jnp.float32, 0.1, 0.9)
    ramp = jnp.arange(MIX_W, dtype=jnp.float32) / (MIX_W - 1)
    rwkv_w0 = -6.5 + 5.0 * ramp ** 0.85 + nrm((L, 2, MIX_W), 0.1)
    rwkv_w2 = nrm((L, 2, DECAY_LORA, MIX_W), 0.5 * DECAY_LORA ** -0.5)
    rwkv_a0 = nrm((L, 2, MIX_W), 0.1)
    rwkv_a2 = nrm((L, 2, ICLR_LORA, MIX_W), 0.5 * ICLR_LORA ** -0.5)
    rwkv_g2 = nrm((L, GATE_LORA, MIX_W), GATE_LORA ** -0.5)
    rwkv_kk = 0.85 + nrm((L, MIX_W), 0.02)
    rwkv_ka = near_one((L, MIX_W))
    rwkv_rk = nrm((L, N_HEADS, HEAD_DIM), 0.1)
    rwkv_gn_g = near_one((L, MIX_W))
    mla_qn_g = near_one((L, Q_LORA))
    mla_kvn_g = near_one((L, KV_LORA))
    mla_wuq = nrm((L, Q_LORA, N_HEADS * (QK_NOPE + QK_ROPE)), Q_LORA ** -0.5)
    mla_wuk = nrm((L, KV_LORA, N_HEADS * QK_NOPE), KV_LORA ** -0.5)
    mla_wuv = nrm((L, KV_LORA, N_HEADS * V_HEAD), KV_LORA ** -0.5)
    lru_conv_w = nrm((L, CONV_W, MIX_W), CONV_W ** -0.5)
    lru_conv_b = nrm((L, MIX_W), 0.02)
    lru_wa = nrm((L, 2, LRU_BLOCKS, LRU_BS, LRU_BS), LRU_BS ** -0.5)
    lru_ba = nrm((L, 2, MIX_W), 0.1)
    lru_wx = nrm((L, 2, LRU_BLOCKS, LRU_BS, LRU_BS), LRU_BS ** -0.5)
    lru_bx = nrm((L, 2, MIX_W), 0.1)
    a_pow = jax.random.uniform(next(ks), (L, 2, MIX_W), jnp.float32, 0.9, 0.999)
    a_base = a_pow ** (1.0 / LRU_C)
    lru_lambda = jnp.log(a_base) - jnp.log1p(-a_base)
    w_branch = nrm((L, N_BRANCH, MIX_W, D_MODEL), MIX_W ** -0.5)
    w_out = nrm((L, D_MODEL, D_MODEL), BETA * D_MODEL ** -0.5)
    ln1_g = near_one((L, D_MODEL))
    ln1_b = nrm((L, D_MODEL), 0.02)
    w_ff1 = nrm((L, D_MODEL, D_FF), D_MODEL ** -0.5)
    w_ff2 = nrm((L, D_FF, D_MODEL), BETA * D_FF ** -0.5)
    ln2_g = near_one((L, D_MODEL))
    ln2_b = nrm((L, D_MODEL), 0.02)
    return {'x': x, 'c': c, 'ctx': ctx, 'c_ctx': c_ctx, 'w_mod': w_mod, 'b_mod': b_mod, 'w_in': w_in,
            'mlstm_gate_b': mlstm_gate_b, 'mlstm_norm_g': mlstm_norm_g, 'rwkv_mu': rwkv_mu,
            'rwkv_w0': rwkv_w0, 'rwkv_w2': rwkv_w2, 'rwkv_a0': rwkv_a0, 'rwkv_a2': rwkv_a2,
            'rwkv_g2': rwkv_g2, 'rwkv_kk': rwkv_kk, 'rwkv_ka': rwkv_ka, 'rwkv_rk': rwkv_rk,
            'rwkv_gn_g': rwkv_gn_g, 'mla_qn_g': mla_qn_g, 'mla_kvn_g': mla_kvn_g, 'mla_wuq': mla_wuq,
            'mla_wuk': mla_wuk, 'mla_wuv': mla_wuv, 'lru_conv_w': lru_conv_w, 'lru_conv_b': lru_conv_b,
            'lru_wa': lru_wa, 'lru_ba': lru_ba, 'lru_wx': lru_wx, 'lru_bx': lru_bx,
            'lru_lambda': lru_lambda, 'w_branch': w_branch, 'w_out': w_out, 'ln1_g': ln1_g,
            'ln1_b': ln1_b, 'w_ff1': w_ff1, 'w_ff2': w_ff2, 'ln2_g': ln2_g, 'ln2_b': ln2_b}


def reference(x, c, ctx, c_ctx, w_mod, b_mod, w_in, mlstm_gate_b, mlstm_norm_g, rwkv_mu, rwkv_w0, rwkv_w2,
              rwkv_a0, rwkv_a2, rwkv_g2, rwkv_kk, rwkv_ka, rwkv_rk, rwkv_gn_g, mla_qn_g, mla_kvn_g, mla_wuq,
              mla_wuk, mla_wuv, lru_conv_w, lru_conv_b, lru_wa, lru_ba, lru_wx, lru_bx, lru_lambda, w_branch,
              w_out, ln1_g, ln1_b, w_ff1, w_ff2, ln2_g, ln2_b):
    rows = x.shape[1] // GRID_W
    rope_tabs = _axial_rope_tables(rows)
    s_lat = jax.nn.silu(c)
    s_ctx = jax.nn.silu(c_ctx)
    x_lat, x_ctx = x, ctx
    for l in range(DEPTH):
        need_ctx = l < DEPTH - 1
        m_lat = jnp.split((s_lat @ w_mod[l] + b_mod[l])[:, None, :], 6, axis=-1)
        m_ctx = jnp.split((s_ctx @ w_mod[l] + b_mod[l])[None, None, :], 6, axis=-1)
        h_lat = _modulate(x_lat, m_lat[0], m_lat[1])
        h_ctx = _modulate(x_ctx, m_ctx[0], m_ctx[1])
        y_ctx, y_lat = _mixing_sublayer(
            h_ctx, h_lat, need_ctx, rope_tabs, w_in[l], mlstm_gate_b[l], mlstm_norm_g[l], rwkv_mu[l],
            rwkv_w0[l], rwkv_w2[l], rwkv_a0[l], rwkv_a2[l], rwkv_g2[l], rwkv_kk[l], rwkv_ka[l], rwkv_rk[l],
            rwkv_gn_g[l], mla_qn_g[l], mla_kvn_g[l], mla_wuq[l], mla_wuk[l], mla_wuv[l], lru_conv_w[l],
            lru_conv_b[l], lru_wa[l], lru_ba[l], lru_wx[l], lru_bx[l], lru_lambda[l], w_branch[l], w_out[l])
        x_lat = _layer_norm(ALPHA * x_lat + m_lat[2] * y_lat, ln1_g[l], ln1_b[l])
        f_lat = _sq_relu_mlp(_modulate(x_lat, m_lat[3], m_lat[4]), w_ff1[l], w_ff2[l])
        x_lat = _layer_norm(ALPHA * x_lat + m_lat[5] * f_lat, ln2_g[l], ln2_b[l])
        if need_ctx:
            x_ctx = _layer_norm(ALPHA * x_ctx + m_ctx[2] * y_ctx, ln1_g[l], ln1_b[l])
            f_ctx = _sq_relu_mlp(_modulate(x_ctx, m_ctx[3], m_ctx[4]), w_ff1[l], w_ff2[l])
            x_ctx = _layer_norm(ALPHA * x_ctx + m_ctx[5] * f_ctx, ln2_g[l], ln2_b[l])
    return x_lat
```

```python
import contextlib
import numpy as np
import concourse.bass as bass
import concourse.mybir as mybir
from concourse.bass_utils import run_bass_kernel_spmd

F32 = mybir.dt.float32
BF16 = mybir.dt.bfloat16
AF = mybir.ActivationFunctionType
ALU = mybir.AluOpType
AX = mybir.AxisListType

QUEUES = ("sync", "scalar", "gpsimd")
EPOCH = 24000
NDMASEM = 12


class Dep:
    __slots__ = ("writers", "readers")

    def __init__(self):
        self.writers = []
        self.readers = []


class Op:
    __slots__ = ("eng", "fn", "waits", "is_dma", "signal", "sem", "val")

    def __init__(self, eng, fn, is_dma):
        self.eng = eng
        self.fn = fn
        self.is_dma = is_dma
        self.waits = []
        self.signal = False
        self.sem = None
        self.val = None


def _prune(readers):
    out = []
    last = {}
    for r in readers:
        if r.is_dma:
            out.append(r)
        else:
            last[r.eng] = r
    return out + list(last.values())


class Sched:
    def __init__(self, nc):
        self.nc = nc
        self.streams = {e: [] for e in ("tensor", "vector", "scalar", "gpsimd", "sync")}
        self.dma_hist = {q: [] for q in QUEUES}
        self.final_waits = []

    def _add(self, eng, fn, reads, writes, is_dma):
        op = Op(eng, fn, is_dma)
        waits = []
        for d in reads:
            waits.extend(d.writers)
        for d in writes:
            waits.extend(d.writers)
            waits.extend(d.readers)
        seen = set()
        for w in waits:
            if id(w) in seen or w is op:
                continue
            seen.add(id(w))
            if (not w.is_dma) and (not is_dma) and w.eng == eng == "tensor":
                continue
            op.waits.append(w)
        if is_dma:
            hist = self.dma_hist[eng]
            if len(hist) >= NDMASEM:
                op.waits.append(hist[len(hist) - NDMASEM])
            hist.append(op)
        for d in reads:
            d.readers.append(op)
            if len(d.readers) > 48:
                d.readers = _prune(d.readers)
        for d in writes:
            d.writers = [op]
            d.readers = []
        self.streams[eng].append(op)
        return op

    def op(self, eng, fn, reads=(), writes=()):
        return self._add(eng, fn, reads, writes, False)

    def dma(self, q, out, in_, reads=(), writes=(), **kw):
        return self._add(q, lambda e: e.dma_start(out=out, in_=in_, **kw), reads, writes, True)

    def barrier(self):
        tails = []
        for eng in ("tensor", "vector", "scalar", "gpsimd"):
            for op in reversed(self.streams[eng]):
                if not op.is_dma and op.fn is not None:
                    tails.append(op)
                    break
        for q in QUEUES:
            tails.extend(self.dma_hist[q][-NDMASEM:])
        for eng in ("tensor", "vector", "scalar", "gpsimd", "sync"):
            op = Op(eng, None, False)
            op.waits = [w for w in tails]
            self.streams[eng].append(op)

    def finish_on(self, eng, deps):
        ops = []
        for d in deps:
            ops.extend(d.writers)
        self.final_waits.append((eng, ops))

    def emit(self):
        nc = self.nc
        for eng, ops in self.streams.items():
            for op in ops:
                for w in op.waits:
                    w.signal = True
        for eng, ops in self.final_waits:
            for w in ops:
                w.signal = True
        need = []
        seen = set()
        for eng, ops in self.streams.items():
            cnt = 0
            dcnt = 0
            for op in ops:
                if op.is_dma:
                    op.sem = ("dma", eng, dcnt % NDMASEM)
                    op.val = 16 * (dcnt // NDMASEM + 1)
                    dcnt += 1
                    op.signal = True
                elif op.signal:
                    op.sem = ("eng", eng, cnt // EPOCH)
                    op.val = cnt % EPOCH + 1
                    cnt += 1
                if op.signal and op.sem not in seen:
                    seen.add(op.sem)
                    need.append(op.sem)
        with contextlib.ExitStack() as es:
            semh = {}
            for i, key in enumerate(need):
                semh[key] = es.enter_context(nc.semaphore("s%d" % i))
            block = es.enter_context(nc.Block())

            def gen(engname):
                ops = self.streams[engname]

                def body(e):
                    known = {}

                    def wait(w):
                        if known.get(w.sem, 0) >= w.val:
                            return
                        e.wait_ge(semh[w.sem], w.val)
                        known[w.sem] = w.val
                    for op in ops:
                        for w in op.waits:
                            wait(w)
                        if op.fn is None:
                            continue
                        ins = op.fn(e)
                        if op.signal:
                            ins.then_inc(semh[op.sem], 16 if op.is_dma else 1)
                    for eng2, wops in self.final_waits:
                        if eng2 == engname:
                            for w in wops:
                                wait(w)
                return body

            for engname in ("sync", "tensor", "vector", "scalar", "gpsimd"):
                if not self.streams[engname] and not any(e == engname for e, _ in self.final_waits):
                    continue
                getattr(block, engname)(gen(engname))


D = 1024
L = 4
TC = 256
TL = 4096
T = TC + TL
NFC = 8
DIN = 6704
ALPHA = (2 * L) ** 0.25
LN_EPS = 1e-5
RMS_EPS = 1e-6
NUF = 18
NTM = 784
C_MQ, C_MK, C_MV, C_MO, C_MIF = 4096, 4352, 4608, 4864, 5120
C_RR, C_RK, C_RV, C_RWD, C_RAD, C_RGD = 5136, 5392, 5648, 5904, 5968, 6032
C_ACQ, C_ACKV, C_AKR = 6096, 6288, 6416
C_LRU = 6448
UF_CHUNKS = [
    (0, C_MQ, 128), (1, C_MQ + 128, 128), (2, C_MK, 128), (3, C_MK + 128, 128),
    (4, C_RR, 128), (5, C_RR + 128, 128), (6, C_RK, 128), (7, C_RK + 128, 128),
    (8, C_RV, 128), (9, C_RV + 128, 128), (10, C_RWD, 128), (11, C_RGD, 64),
    (12, C_ACQ, 128), (13, C_ACQ + 128, 64), (14, C_ACKV, 128), (15, C_AKR, 32),
    (16, C_LRU, 128), (17, C_LRU + 128, 128),
]
BLOCKS = [(0, 256)] + [(256 + 512 * i, 512) for i in range(8)]


WEIGHT_SHAPES = {
    "w_mod": [4, 1024, 6144], "b_mod": [4, 6144], "w_in": [4, 1024, 6704], "mlstm_gate_b": [4, 2, 2, 4],
    "mlstm_norm_g": [4, 256], "rwkv_mu": [4, 960], "rwkv_w0": [4, 2, 256], "rwkv_w2": [4, 2, 32, 256],
    "rwkv_a0": [4, 2, 256], "rwkv_a2": [4, 2, 32, 256], "rwkv_g2": [4, 64, 256], "rwkv_kk": [4, 256],
    "rwkv_ka": [4, 256], "rwkv_rk": [4, 4, 64], "rwkv_gn_g": [4, 256], "mla_qn_g": [4, 192], "mla_kvn_g": [4, 128],
    "mla_wuq": [4, 192, 384], "mla_wuk": [4, 128, 256], "mla_wuv": [4, 128, 256], "lru_conv_w": [4, 4, 256],
    "lru_conv_b": [4, 256], "lru_wa": [4, 2, 4, 64, 64], "lru_ba": [4, 2, 256], "lru_wx": [4, 2, 4, 64, 64],
    "lru_bx": [4, 2, 256], "lru_lambda": [4, 2, 256], "w_branch": [4, 4, 256, 1024], "w_out": [4, 1024, 1024],
    "ln1_g": [4, 1024], "ln1_b": [4, 1024], "w_ff1": [4, 1024, 4096], "w_ff2": [4, 4096, 1024], "ln2_g": [4, 1024],
    "ln2_b": [4, 1024],
}
CONST_SHAPES = {"ident": [128, 128], "trif": [128, 128], "trib": [128, 128], "ones": [128, 128], "cosT": [32, 4096], "sinT": [32, 4096], "m_su": [128, 64], "m_sl": [128, 64], "m_u": [128, 64], "m_l": [128, 64], "idb": [128, 64], "bones": [128, 128], "rmask": [128, 512]}

class Tl:
    __slots__ = ("t", "d")

    def __init__(self, t):
        self.t = t
        self.d = Dep()

    def __getitem__(self, k):
        return self.t[k]


class _HoutAll:
    def __init__(self, tl, deps):
        self.t = tl.t
        self.deps = deps

    def __getitem__(self, k):
        return self.t[k]


class Builder:
    def __init__(self, kinds=None, layers=(0, 1, 2, 3)):
        self.nc = bass.Bass("TRN2", target_bir_lowering=False)
        self.S = Sched(self.nc)
        self.kinds = kinds or {}
        self.layers = layers
        self.dr = {}
        self.es = contextlib.ExitStack()
        self.es.enter_context(self.nc.allow_non_contiguous_dma(reason="small strided parameter loads"))
        self.outdeps = []

    def dram(self, name, shape, dtype, kind=None):
        k = self.kinds.get(name, kind or "Internal")
        ap = self.nc.dram_tensor(name, list(shape), dtype, kind=k).ap()
        self.dr[name] = ap
        return ap

    def sb(self, es, name, shape, dtype=F32):
        self.uid = getattr(self, "uid", 0) + 1
        return Tl(es.enter_context(self.nc.sbuf_tensor("%s_%d" % (name, self.uid), list(shape), dtype)))

    def ps(self, es, name, shape, dtype=F32):
        self.uid = getattr(self, "uid", 0) + 1
        return Tl(es.enter_context(self.nc.psum_tensor("%s_%d" % (name, self.uid), list(shape), dtype)))

    def op(self, eng, fn, reads=(), writes=()):
        return self.S.op(eng, fn, [r.d if isinstance(r, Tl) else r for r in reads],
                         [w.d if isinstance(w, Tl) else w for w in writes])

    def dma(self, q, out, in_, reads=(), writes=(), **kw):
        return self.S.dma(q, out, in_, [r.d if isinstance(r, Tl) else r for r in reads],
                          [w.d if isinstance(w, Tl) else w for w in writes], **kw)

    def declare(self):
        d = self.dram
        EI = "ExternalInput"
        self.x = d("x", [TL, D], F32, EI)
        self.ctx = d("ctx", [TC, D], F32, EI)
        self.c = d("c", [D], F32, EI)
        self.c_ctx = d("c_ctx", [D], F32, EI)
        self.w = {}
        for name, shape in WEIGHT_SHAPES.items():
            self.w[name] = d(name, shape, F32, EI)
        self.cst = {}
        for name, shape in CONST_SHAPES.items():
            self.cst[name] = d("k_" + name, shape, F32, EI)
        self.out = d("out", [TL, D], F32, "ExternalOutput")
        self.xresT = d("xresT", [NFC, 128, T], F32)
        self.xres_src = d("xresT_in", [NFC, 128, T], F32) if "xresT_in" in self.kinds else self.xresT
        self.hT = d("hT", [NFC, 128, T], BF16)
        self.G = d("G", [32, 128, T], BF16)
        self.UF = d("UF", [NUF, 128, T], F32)
        self.UT = d("UT", [T, NTM], F32)
        self.OT = d("OT", [8, 128, T], BF16)
        self.Wb1 = d("Wb1", [32, 128, 8, 128], BF16)
        self.Wb2 = d("Wb2", [8, 128, 32, 128], BF16)

    def globals_(self):
        es = self.es
        self.modT = self.sb(es, "modT", [128, L, 48, 2])
        self.gA = self.sb(es, "gA", [128, L, 4, 8, 2])
        self.lnp = self.sb(es, "lnp", [128, L, 4, 8])
        self.ident = self.sb(es, "ident", [128, 128])
        self.onesb = self.sb(es, "onesb", [128, 128], BF16)
        self.one_sc = self.sb(es, "one_sc", [128, L, 8, 2])
        self.gsc = self.sb(es, "gsc", [128, L, 2, 8, 2])
        self.epsc = {}
        for i, v in enumerate((LN_EPS, RMS_EPS, 64e-5, 0.0, LN_EPS / (ALPHA * ALPHA))):
            t = self.sb(es, "epsc%d" % i, [128, 1])
            self.op("vector", (lambda e, t=t, v=v: e.memset(t[:], v)), writes=[t])
            self.epsc[v] = t
        self.dma("sync", self.ident[:], self.cst["ident"][:, :], writes=[self.ident])
        self.op("vector", lambda e: e.memset(self.onesb[:], 1.0 / 1024.0), writes=[self.onesb])

    def I(self, eng, meth, reads, writes, *args, **kw):
        return self.op(eng, lambda e: getattr(e, meth)(*args, **kw), reads, writes)

    def rsqrt(self, in_tl, in_ap, out_tl, out_ap, eps, scale=1.0):
        self.I("scalar", "activation", [in_tl], [out_tl], out=out_ap, in_=in_ap, func=AF.Sqrt, bias=self.epsc[eps][:in_ap.shape[0], 0:1], scale=scale)
        self.I("vector", "reciprocal", [out_tl], [out_tl], out=out_ap, in_=out_ap)

    def mm(self, out_tl, out_ap, l_tl, l_ap, r_tl, r_ap, start=True, stop=True, extra=()):
        return self.op("tensor", lambda e: e.matmul(out_ap, lhsT=l_ap, rhs=r_ap, start=start, stop=stop),
                       [l_tl, r_tl] + list(extra), [out_tl])

    def phase_mod(self):
        with contextlib.ExitStack() as es:
            craw = self.sb(es, "craw", [128, 8, 2])
            sT = self.sb(es, "sT", [128, 8, 2])
            sg = self.sb(es, "sg", [128, 8, 2])
            bm = self.sb(es, "bm", [128, L, 48])
            wm = [self.sb(es, "wm%d" % i, [128, 8, 512], BF16) for i in range(3)]
            sTb = self.sb(es, "sTb", [128, 8, 2], BF16)
            pm = self.ps(es, "pm", [128, L, 48, 2])
            self.dma("sync", craw[:, :, 0], self.c.rearrange("(k p) -> p k", p=128), writes=[craw])
            self.dma("sync", craw[:, :, 1], self.c_ctx.rearrange("(k p) -> p k", p=128), writes=[craw])
            self.dma("sync", bm[:], self.w["b_mod"].rearrange("l (j p) -> p l j", p=128), writes=[bm])
            for i, nm in enumerate(("ln1_g", "ln1_b", "ln2_g", "ln2_b")):
                for l in range(L):
                    self.dma("sync", self.lnp[:, l, i, :], self.w[nm][l].rearrange("(j p) -> p j", p=128), writes=[self.lnp])
            self.I("scalar", "activation", [craw], [sg], out=sg[:], in_=craw[:], func=AF.Sigmoid)
            self.I("vector", "tensor_mul", [craw, sg], [sT], out=sT[:], in0=craw[:], in1=sg[:])
            self.I("vector", "tensor_copy", [sT], [sTb], out=sTb[:], in_=sT[:])
            n = 0
            for l in range(L):
                for cb in range(12):
                    w_ = wm[n % 3]
                    n += 1
                    self.dma("gpsimd", w_[:], self.w["w_mod"][l, :, cb * 512:(cb + 1) * 512].rearrange("(k p) c -> p k c", p=128),
                             writes=[w_])
                    for m in range(4):
                        j = cb * 4 + m
                        for kc in range(8):
                            self.mm(pm, pm[:, l, j, :], w_, w_[:, kc, m * 128:(m + 1) * 128], sTb, sTb[:, kc, :],
                                    start=(kc == 0), stop=(kc == 7))
            for s in range(2):
                self.I("vector", "tensor_tensor", [pm, bm], [self.modT], out=self.modT[:, :, :, s], in0=pm[:, :, :, s], in1=bm[:], op=ALU.add)
            self.I("vector", "tensor_scalar_add", [self.modT], [self.one_sc], out=self.one_sc[:], in0=self.modT[:, :, 8:16, :], scalar1=1.0)
            for gi, c0 in ((0, 16), (1, 40)):
                self.I("vector", "tensor_scalar_mul", [self.modT], [self.gsc], out=self.gsc[:, :, gi, :, :], in0=self.modT[:, :, c0:c0 + 8, :], scalar1=1.0 / ALPHA)
            for l in range(L):
                for s in range(2):
                    self._fuse(l, s, 0, self.lnp[:, l, 0, :], self.lnp[:, l, 1, :], l, 3, 4)
                    if l + 1 < L:
                        self._fuse(l, s, 2, self.lnp[:, l, 2, :], self.lnp[:, l, 3, :], l + 1, 0, 1)
            self.S.barrier()

    def _fuse(self, l, s, slot, g, b, lm, ish, isc):
        m = self.modT
        gA = self.gA
        sc = m[:, lm, isc * 8:(isc + 1) * 8, s]
        sh = m[:, lm, ish * 8:(ish + 1) * 8, s]
        self.I("vector", "scalar_tensor_tensor", [m, self.lnp], [gA], out=gA[:, l, slot, :, s], in0=sc, scalar=1.0, in1=g, op0=ALU.add, op1=ALU.mult)
        self.I("vector", "scalar_tensor_tensor", [m, self.lnp], [gA], out=gA[:, l, slot + 1, :, s], in0=sc, scalar=1.0, in1=b, op0=ALU.add, op1=ALU.mult)
        self.I("vector", "tensor_add", [m, gA], [gA], out=gA[:, l, slot + 1, :, s], in0=gA[:, l, slot + 1, :, s], in1=sh)

    def phase_x0(self):
        with contextlib.ExitStack() as es:
            xin = [self.sb(es, "xin%d" % i, [128, 4, D]) for i in range(2)]
            xo = [self.sb(es, "xo%d" % i, [128, 8, 512]) for i in range(2)]
            ho = [self.sb(es, "ho%d" % i, [128, 8, 512], BF16) for i in range(2)]
            pt = [self.ps(es, "pt%d" % i, [128, 512]) for i in range(4)]
            npt = 0
            m = self.modT
            for bi, (t0, n) in enumerate(BLOCKS):
                s = 1 if t0 < TC else 0
                xi, xo_, ho_ = xin[bi % 2], xo[bi % 2], ho[bi % 2]
                nt = n // 128
                src = self.ctx if s == 1 else self.x
                r0 = t0 if s == 1 else t0 - TC
                self.dma("sync", xi[:, 0:nt, :], src[r0:r0 + n, :].rearrange("(a p) f -> p a f", p=128), writes=[xi])
                for fc in range(8):
                    p_ = pt[npt % 4]
                    npt += 1
                    for a in range(nt):
                        self.mm(p_, p_[:, a * 128:(a + 1) * 128], xi, xi[:, a, fc * 128:(fc + 1) * 128], self.ident, self.ident[:])
                    self.I("scalar", "copy", [p_], [xo_], out=xo_[:, fc, 0:n], in_=p_[:, 0:n])
                    self.I("vector", "tensor_scalar", [xo_, m, self.one_sc], [ho_], out=ho_[:, fc, 0:n], in0=xo_[:, fc, 0:n],
                           scalar1=self.one_sc[:, 0, fc, s:s + 1], scalar2=m[:, 0, fc, s:s + 1], op0=ALU.mult, op1=ALU.add)
                self.dma("sync", self.xresT[:, :, t0:t0 + n].rearrange("f p t -> p f t"), xo_[:, :, 0:n], reads=[xo_])
                self.dma("sync", self.hT[:, :, t0:t0 + n].rearrange("f p t -> p f t"), ho_[:, :, 0:n], reads=[ho_])
            self.S.barrier()

    def phase1(self, l):
        w_in = self.w["w_in"]
        with contextlib.ExitStack() as es:
            hT = self.sb(es, "hTr", [128, 8, T], BF16)
            slabs = [self.sb(es, "slab%d" % i, [128, 8, 128], BF16) for i in range(3)]
            stf = [self.sb(es, "stf%d" % i, [128, T]) for i in range(2)]
            stb = [self.sb(es, "stb%d" % i, [128, T], BF16) for i in range(2)]
            tsl = self.sb(es, "tsl", [128, 8, NTM], BF16)
            tst = [self.sb(es, "tst%d" % i, [128, NTM]) for i in range(2)]
            pp = [self.ps(es, "pp%d" % i, [128, 512]) for i in range(6)]
            for fc in range(8):
                self.dma("sync", hT[:, fc, :], self.hT[fc], writes=[hT])
            for (d0, c0, ncol) in ((0, C_MV, 256), (256, C_MO, 256), (512, C_MIF, 16), (528, C_MK, 256)):
                self.dma("gpsimd", tsl[:, :, d0:d0 + ncol], w_in[l, :, c0:c0 + ncol].rearrange("(k p) c -> p k c", p=128), writes=[tsl])
            chunks = [("g", j, j * 128, 128) for j in range(32)] + [("u", i, c0, nc_) for (i, c0, nc_) in UF_CHUNKS]
            npp = 0
            nf = nb = 0
            for ci, (kind, idx, c0, ncol) in enumerate(chunks):
                sl = slabs[ci % 3]
                self.dma("gpsimd", sl[:, :, 0:ncol], w_in[l, :, c0:c0 + ncol].rearrange("(k p) c -> p k c", p=128), writes=[sl])
                mrows = ncol
                if kind == "u" and idx == 15:
                    for (dst, srcc, sgn) in ((32, 8, -1.0), (40, 0, 1.0), (48, 24, -1.0), (56, 16, 1.0)):
                        self.I("vector", "tensor_scalar_mul", [sl], [sl], out=sl[:, :, dst:dst + 8], in0=sl[:, :, srcc:srcc + 8], scalar1=sgn)
                    mrows = 64
                if kind == "g":
                    st = stb[nb % 2]
                    nb += 1
                else:
                    st = stf[nf % 2]
                    nf += 1
                for bi, (t0, n) in enumerate(BLOCKS):
                    p_ = pp[npp % 6]
                    npp += 1
                    for kc in range(8):
                        self.mm(p_, p_[0:mrows, 0:n], sl, sl[:, kc, 0:mrows], hT, hT[:, kc, t0:t0 + n], start=(kc == 0), stop=(kc == 7))
                    if kind == "g":
                        self.I("scalar", "activation", [p_], [st], out=st[:, t0:t0 + n], in_=p_[:, 0:n], func=AF.Sigmoid)
                    else:
                        self.I("vector", "tensor_copy", [p_], [st], out=st[0:mrows, t0:t0 + n], in_=p_[0:mrows, 0:n])
                if kind == "g":
                    self.dma("sync", self.G[idx], st[:], reads=[st])
                else:
                    self.dma("sync", self.UF[idx, 0:mrows, :], st[0:mrows, :], reads=[st])
            for tt in range(T // 128):
                p1 = pp[npp % 6]
                p2 = pp[(npp + 1) % 6]
                npp += 2
                to = tst[tt % 2]
                for kc in range(8):
                    self.mm(p1, p1[:, 0:512], hT, hT[:, kc, tt * 128:(tt + 1) * 128], tsl, tsl[:, kc, 0:512], start=(kc == 0), stop=(kc == 7))
                for kc in range(8):
                    self.mm(p2, p2[:, 0:272], hT, hT[:, kc, tt * 128:(tt + 1) * 128], tsl, tsl[:, kc, 512:784], start=(kc == 0), stop=(kc == 7))
                self.I("vector", "tensor_copy", [p1], [to], out=to[:, 0:512], in_=p1[:, 0:512])
                self.I("scalar", "copy", [p2], [to], out=to[:, 512:784], in_=p2[:, 0:272])
                self.dma("sync", self.UT[tt * 128:(tt + 1) * 128, :], to[:], reads=[to])
            self.S.barrier()

    def barrier(self):
        self.S.barrier()

    def weight_prep(self, l):
        w1 = self.w["w_ff1"][l].rearrange("(k p) (j c) -> j p k c", p=128, c=128)
        w2 = self.w["w_ff2"][l].rearrange("(k p) (j c) -> j p k c", p=128, c=128)
        for j in range(32):
            self.dma("gpsimd", self.Wb1[j], w1[j])
        for c in range(8):
            for hh in range(2):
                self.dma("gpsimd", self.Wb2[c, :, hh * 16:(hh + 1) * 16, :], w2[c, :, hh * 16:(hh + 1) * 16, :])

    def ln_stats_chunk(self, Xtl, Xd, fc, n, xb, xbd, sq, sqd, xoff, soff):
        self.op("scalar", lambda e: e.activation(out=sq[:, soff + fc, 0:n], in_=Xtl[:, fc, 0:n], func=AF.Square), [Xd[fc]], [sqd[soff + fc]])
        self.op("scalar", lambda e: e.copy(out=xb[:, xoff + fc, 0:n], in_=Xtl[:, fc, 0:n]), [Xd[fc]], [xbd[xoff + fc]])

    def ln_stats_mm(self, fc, n, xb, xbd, sq, sqd, xoff, soff, pmean, pmsq):
        self.op("tensor", lambda e: e.matmul(pmean[:, 0:n], lhsT=self.onesb[:], rhs=xb[:, xoff + fc, 0:n], start=(fc == 0), stop=(fc == 7)),
                [self.onesb.d, xbd[xoff + fc]], [pmean.d])
        self.op("tensor", lambda e: e.matmul(pmsq[:, 0:n], lhsT=self.onesb[:], rhs=sq[:, soff + fc, 0:n], start=(fc == 0), stop=(fc == 7)),
                [self.onesb.d, sqd[soff + fc]], [pmsq.d])

    def ln_finish(self, X, Xd, n, g_ap, b_ap, gA_ap, bA_ap, Hout, Hdeps, pmean, pmsq, st, eps):
        rstd, nmr, tmp, mean = st
        self.I("vector", "tensor_copy", [pmean], [mean], out=mean[:, 0:n], in_=pmean[:, 0:n])
        self.I("vector", "tensor_tensor", [mean], [tmp], out=tmp[:, 0:n], in0=mean[:, 0:n], in1=mean[:, 0:n], op=ALU.mult)
        self.I("vector", "tensor_tensor", [pmsq, tmp], [tmp], out=tmp[:, 0:n], in0=pmsq[:, 0:n], in1=tmp[:, 0:n], op=ALU.subtract)
        self.rsqrt(tmp, tmp[:, 0:n], rstd, rstd[:, 0:n], eps)
        self.I("vector", "scalar_tensor_tensor", [mean, rstd], [nmr], out=nmr[:, 0:n], in0=mean[:, 0:n], scalar=-1.0, in1=rstd[:, 0:n], op0=ALU.mult, op1=ALU.mult)
        def stage_a(fc):
            xs = X[:, fc, 0:n]
            self.op("vector", (lambda e, xs=xs: e.tensor_tensor(out=xs, in0=xs, in1=rstd[:, 0:n], op=ALU.mult)), [Xd[fc], rstd.d], [Xd[fc]])
            self.op("gpsimd", (lambda e, xs=xs: e.tensor_tensor(out=xs, in0=xs, in1=nmr[:, 0:n], op=ALU.add)), [Xd[fc], nmr.d], [Xd[fc]])

        def stage_b(fc):
            xs = X[:, fc, 0:n]
            if Hout is not None:
                self.op("vector", (lambda e, xs=xs, fc=fc: e.tensor_scalar(out=Hout[:, fc, 0:n], in0=xs, scalar1=gA_ap[:, fc:fc + 1], scalar2=bA_ap[:, fc:fc + 1],
                                                                            op0=ALU.mult, op1=ALU.add)), [Xd[fc], self.gA.d], [Hdeps[fc]])
            self.op("scalar", (lambda e, xs=xs, fc=fc: e.activation(out=xs, in_=xs, func=AF.Identity, scale=g_ap[:, fc:fc + 1], bias=b_ap[:, fc:fc + 1])),
                    [Xd[fc], self.lnp.d], [Xd[fc]])
        stage_a(0)
        for fc in range(8):
            if fc + 1 < 8:
                stage_a(fc + 1)
            stage_b(fc)

    def phase3(self, l):
        last = (l == L - 1)
        m = self.modT
        gs = self.gsc
        EPS2 = LN_EPS / (ALPHA * ALPHA)
        with contextlib.ExitStack() as es:
            Pw = self.sb(es, "Pw", [128, 4, 2, D], BF16)
            Wo = self.sb(es, "Wo", [128, 8, D], BF16)
            w1s = [self.sb(es, "w1s%d" % i, [128, 8, 128], BF16) for i in range(3)]
            w2s = [self.sb(es, "w2s%d" % i, [128, 32, 128], BF16) for i in range(2)]
            gt = [self.sb(es, "gt%d" % i, [128, 4, 512], BF16) for i in range(2)]
            oT = [self.sb(es, "oT%d" % i, [128, 8, 512], BF16) for i in range(2)]
            Abuf = [self.sb(es, "bA%d" % i, [128, 8, 512]) for i in range(1 if last else 2)]
            Adeps = [[Dep() for _ in range(8)] for _ in range(1 if last else 2)]
            if last:
                Abuf, Adeps = [Abuf[0], Abuf[0]], [Adeps[0], Adeps[0]]
            B = self.sb(es, "bB", [128, 8, 512])
            Bd = [Dep() for _ in range(8)]
            E = self.sb(es, "bE", [128, 8, 512], BF16)
            Ed = [Dep() for _ in range(8)]
            Fb = self.sb(es, "bF", [128, 8, 512], BF16) if not last else None
            Fd = [Dep() for _ in range(8)]
            if last:
                ot = [self.sb(es, "ot%d" % i, [128, D]) for i in range(2)]
                Fb = self.sb(es, "bF", [128, 8, 512], BF16)
            hid = self.sb(es, "hid", [128, 32, 512], BF16)
            Hd = [Dep() for _ in range(32)]
            ysb = [self.sb(es, "ysb%d" % i, [128, 4, 512], BF16) for i in range(2)]
            tq = [self.sb(es, "tq%d" % i, [128, 4, 512], BF16) for i in range(2)]
            st = [self.sb(es, "st%d" % i, [128, 512]) for i in range(4)]
            py = [self.ps(es, "py%d" % i, [128, 512]) for i in range(4)]
            pa = [self.ps(es, "pa%d" % i, [128, 512]) for i in range(2)]
            pmean = self.ps(es, "pmean", [128, 512])
            pmsq = self.ps(es, "pmsq", [128, 512])
            self.dma("gpsimd", Pw[:], self.w["w_branch"][l].rearrange("n (mc p) d -> p n mc d", p=128), writes=[Pw])
            self.dma("gpsimd", Wo[:], self.w["w_out"][l].rearrange("(k p) d -> p k d", p=128), writes=[Wo])
            Gv = self.G.rearrange("(nb dc) p t -> dc p nb t", nb=4)
            blocks = BLOCKS[1:] if last else BLOCKS
            pysets = [py, [pa[0], pa[1], pmean, pmsq]]
            cnt = {"npa": 0, "nw1": 0, "nw2": 0, "ng": 0, "ntb": 0}

            def do_block(bi, t0, n):
                s = 1 if t0 < TC else 0
                o_ = oT[bi % 2]
                A = Abuf[bi % 2]
                Ad = Adeps[bi % 2]
                self.dma("sync", o_[:, :, 0:n], self.OT[:, :, t0:t0 + n].rearrange("c p t -> p c t"), writes=[o_])
                self.dma("sync", A[:, :, 0:n], self.xres_src[:, :, t0:t0 + n].rearrange("f p t -> p f t"), writes=Ad)
                for dc in range(8):
                    g_ = gt[cnt["ng"] % 2]
                    cnt["ng"] += 1
                    self.dma("sync", g_[:, :, 0:n], Gv[dc, :, :, t0:t0 + n], writes=[g_])
                    pys = pysets[dc % 2]
                    for nb in range(4):
                        for mc in range(2):
                            self.mm(pys[nb], pys[nb][:, 0:n], Pw, Pw[:, nb, mc, dc * 128:(dc + 1) * 128], o_, o_[:, nb * 2 + mc, 0:n],
                                    start=(mc == 0), stop=(mc == 1))
                    ys_ = ysb[cnt["ntb"] % 2]
                    tq_ = tq[cnt["ntb"] % 2]
                    cnt["ntb"] += 1
                    for nb in range(4):
                        self.I("scalar", "copy", [pys[nb]], [ys_], out=ys_[:, nb, 0:n], in_=pys[nb][:, 0:n])
                    self.I("vector", "tensor_tensor", [ys_, g_], [tq_], out=tq_[:, :, 0:n], in0=ys_[:, :, 0:n], in1=g_[:, :, 0:n], op=ALU.mult)
                    self.I("gpsimd", "tensor_tensor", [tq_], [tq_], out=tq_[:, 0, 0:n], in0=tq_[:, 0, 0:n], in1=tq_[:, 1, 0:n], op=ALU.add)
                    self.I("vector", "tensor_tensor", [tq_], [tq_], out=tq_[:, 2, 0:n], in0=tq_[:, 2, 0:n], in1=tq_[:, 3, 0:n], op=ALU.add)
                    self.op("gpsimd", (lambda e, tq_=tq_, dc=dc: e.tensor_tensor(out=E[:, dc, 0:n], in0=tq_[:, 0, 0:n], in1=tq_[:, 2, 0:n], op=ALU.add)),
                            [tq_.d], [Ed[dc]])
                for d2 in range(8):
                    p_ = pa[cnt["npa"] % 2]
                    cnt["npa"] += 1
                    for dc in range(8):
                        self.op("tensor", (lambda e, p_=p_, dc=dc, d2=d2: e.matmul(p_[:, 0:n], lhsT=Wo[:, dc, d2 * 128:(d2 + 1) * 128], rhs=E[:, dc, 0:n],
                                                                                  start=(dc == 0), stop=(dc == 7))), [Wo.d, Ed[dc]], [p_.d])
                    self.op("vector", (lambda e, p_=p_, d2=d2: e.scalar_tensor_tensor(out=B[:, d2, 0:n], in0=p_[:, 0:n], scalar=gs[:, l, 0, d2, s:s + 1],
                                                                                     in1=A[:, d2, 0:n], op0=ALU.mult, op1=ALU.add)), [p_.d, gs.d, Ad[d2]], [Bd[d2]])
                    self.ln_stats_chunk(B, Bd, d2, n, hid, Hd, hid, Hd, 0, 8)
                    if d2 >= 2:
                        self.ln_stats_mm(d2 - 2, n, hid, Hd, hid, Hd, 0, 8, pmean, pmsq)
                for fc in (6, 7):
                    self.ln_stats_mm(fc, n, hid, Hd, hid, Hd, 0, 8, pmean, pmsq)
                self.ln_finish(B, Bd, n, self.lnp[:, l, 0, :], self.lnp[:, l, 1, :], self.gA[:, l, 0, :, s], self.gA[:, l, 1, :, s], E, Ed, pmean, pmsq, st, EPS2)
                for j in range(32):
                    ws = w1s[cnt["nw1"] % 3]
                    cnt["nw1"] += 1
                    self.dma("sync", ws[:], self.Wb1[j], writes=[ws])
                    p_ = pa[cnt["npa"] % 2]
                    cnt["npa"] += 1
                    for kc in range(8):
                        self.op("tensor", (lambda e, p_=p_, ws=ws, kc=kc: e.matmul(p_[:, 0:n], lhsT=ws[:, kc, :], rhs=E[:, kc, 0:n], start=(kc == 0), stop=(kc == 7))),
                                [ws.d, Ed[kc]], [p_.d])
                    self.op("scalar", (lambda e, p_=p_, j=j: e.activation(out=hid[:, j, 0:n], in_=p_[:, 0:n], func=AF.Relu)), [p_.d], [Hd[j]])
                    self.op("gpsimd", (lambda e, j=j: e.tensor_tensor(out=hid[:, j, 0:n], in0=hid[:, j, 0:n], in1=hid[:, j, 0:n], op=ALU.mult)), [Hd[j]], [Hd[j]])
                for c in range(8):
                    ws = w2s[cnt["nw2"] % 2]
                    cnt["nw2"] += 1
                    self.dma("sync", ws[:], self.Wb2[c], writes=[ws])
                    p_ = pa[cnt["npa"] % 2]
                    cnt["npa"] += 1
                    for j in range(32):
                        self.op("tensor", (lambda e, p_=p_, ws=ws, j=j: e.matmul(p_[:, 0:n], lhsT=ws[:, j, :], rhs=hid[:, j, 0:n], start=(j == 0), stop=(j == 31))),
                                [ws.d, Hd[j]], [p_.d])
                    self.op("vector", (lambda e, p_=p_, c=c: e.scalar_tensor_tensor(out=A[:, c, 0:n], in0=p_[:, 0:n], scalar=gs[:, l, 1, c, s:s + 1],
                                                                                   in1=B[:, c, 0:n], op0=ALU.mult, op1=ALU.add)), [p_.d, gs.d, Bd[c]], [Ad[c]])
                    self.ln_stats_chunk(A, Ad, c, n, E, Ed, Fb, Fd, 0, 0)
                    if c >= 2:
                        self.ln_stats_mm(c - 2, n, E, Ed, Fb, Fd, 0, 0, pmean, pmsq)
                for fc in (6, 7):
                    self.ln_stats_mm(fc, n, E, Ed, Fb, Fd, 0, 0, pmean, pmsq)
                if not last:
                    self.ln_finish(A, Ad, n, self.lnp[:, l, 2, :], self.lnp[:, l, 3, :], self.gA[:, l, 2, :, s], self.gA[:, l, 3, :, s], Fb, Fd, pmean, pmsq, st, EPS2)
                    self.S.dma("scalar", self.xresT[:, :, t0:t0 + n].rearrange("f p t -> p f t"), A[:, :, 0:n], Ad, [])
                    self.S.dma("scalar", self.hT[:, :, t0:t0 + n].rearrange("f p t -> p f t"), Fb[:, :, 0:n], Fd, [])
                else:
                    self.ln_finish(A, Ad, n, self.lnp[:, l, 2, :], self.lnp[:, l, 3, :], None, None, None, None, pmean, pmsq, st, EPS2)
                    for a in range(n // 128):
                        o2 = ot[a % 2]
                        for half in range(2):
                            p_ = py[(a * 2 + half) % 4]
                            for f4 in range(4):
                                fc = half * 4 + f4
                                self.op("tensor", (lambda e, p_=p_, f4=f4, fc=fc, a=a: e.matmul(p_[:, f4 * 128:(f4 + 1) * 128], lhsT=A[:, fc, a * 128:(a + 1) * 128],
                                                                                               rhs=self.ident[:], start=True, stop=True)), [Ad[fc], self.ident.d], [p_.d])
                            self.I("scalar" if half else "vector", "copy" if half else "tensor_copy", [p_], [o2],
                                   out=o2[:, half * 512:(half + 1) * 512], in_=p_[:, :])
                        r0 = t0 - TC + a * 128
                        dop = self.dma("scalar", self.out[r0:r0 + 128, :], o2[:], reads=[o2])
                        dd = Dep()
                        dd.writers = [dop]
                        self.outdeps.append(dd)
            for bi, (t0, n) in enumerate(blocks):
                do_block(bi, t0, n)
            self.S.barrier()

    def _xb(self, es):
        if not hasattr(self, "_xbt") or self._xbt_es is not es:
            self._xbt = self.sb(es, "xbt", [128, 8, 512], BF16)
            self._xbt_es = es
        return self._xbt
    def mixer_lru(self, l):
        W = self.w
        with contextlib.ExitStack() as es:
            ubuf = self.sb(es, "l_u", [128, T + 8])
            xc = self.sb(es, "l_xc", [128, T])
            r = self.sb(es, "l_r", [128, T])
            ig = self.sb(es, "l_i", [128, T])
            tmp = self.sb(es, "l_t", [128, T])
            hf = self.sb(es, "l_hf", [128, T])
            hb = self.sb(es, "l_hb", [128, T])
            ob = self.sb(es, "l_ob", [128, T], BF16)
            cw = self.sb(es, "l_cw", [128, 4])
            cb = self.sb(es, "l_cb", [128, 1])
            prm = self.sb(es, "l_prm", [128, 3, 2])
            cd = self.sb(es, "l_cd", [128, 2, 2])
            e1 = self.sb(es, "l_e1", [128, 2])
            wbd = [self.sb(es, "l_wbd%d" % i, [128, 128]) for i in range(2)]
            pr = [self.ps(es, "l_pr%d" % i, [128, 512]) for i in range(4)]
            npr = 0
            for cp in range(2):
                ch = slice(cp * 128, (cp + 1) * 128)
                self.I("gpsimd", "memset", [], [ubuf], ubuf[:], 0.0)
                self.dma("sync", ubuf[:, 2:2 + TC], self.UF[16 + cp, :, 0:TC], writes=[ubuf])
                self.dma("sync", ubuf[:, 261:261 + TL], self.UF[16 + cp, :, TC:T], writes=[ubuf])
                self.dma("sync", cw[:], W["lru_conv_w"][l, :, ch].rearrange("j c -> c j"), writes=[cw])
                self.dma("sync", cb[:], W["lru_conv_b"][l, ch].rearrange("(c o) -> c o", o=1), writes=[cb])
                for i, nm in enumerate(("lru_ba", "lru_bx", "lru_lambda")):
                    self.dma("sync", prm[:, i, :], W[nm][l, :, ch].rearrange("d c -> c d"), writes=[prm])
                self.I("scalar", "activation", [prm], [e1], out=e1[:], in_=prm[:, 2, :], func=AF.Exp, scale=-1.0)
                self.I("scalar", "activation", [e1], [e1], out=e1[:], in_=e1[:], func=AF.Ln, bias=1.0)
                self.I("vector", "tensor_scalar_mul", [e1], [cd], out=cd[:, 0, :], in0=e1[:], scalar1=-8.0)
                self.I("vector", "tensor_scalar_mul", [e1], [cd], out=cd[:, 1, :], in0=e1[:], scalar1=-16.0)
                for (o0, u0, n) in ((0, 0, TC), (TC, 259, TL)):
                    self.I("vector", "tensor_scalar", [ubuf, cw, cb], [xc], out=xc[:, o0:o0 + n], in0=ubuf[:, u0:u0 + n],
                           scalar1=cw[:, 0:1], scalar2=cb[:, 0:1], op0=ALU.mult, op1=ALU.add)
                    for j in range(1, 4):
                        self.I("vector", "scalar_tensor_tensor", [ubuf, cw, xc], [xc], out=xc[:, o0:o0 + n], in0=ubuf[:, u0 + j:u0 + j + n],
                               scalar=cw[:, j:j + 1], in1=xc[:, o0:o0 + n], op0=ALU.mult, op1=ALU.add)
                for d in range(2):
                    for gi, (nm, dst) in enumerate((("lru_wa", r), ("lru_wx", ig))):
                        wb = wbd[gi]
                        self.I("gpsimd", "memset", [], [wb], wb[:], 0.0)
                        for i in range(2):
                            self.dma("sync", wb[i * 64:(i + 1) * 64, i * 64:(i + 1) * 64], W[nm][l, d, 2 * cp + i], writes=[wb])
                        for (t0, n) in BLOCKS:
                            p_ = pr[npr % 4]
                            npr += 1
                            self.mm(p_, p_[:, 0:n], wb, wb[:], xc, xc[:, t0:t0 + n])
                            self.I("scalar", "activation", [p_, prm], [dst], out=dst[:, t0:t0 + n], in_=p_[:, 0:n], func=AF.Sigmoid,
                                   bias=prm[:, gi, d:d + 1])
                    self.I("scalar", "activation", [r, cd], [tmp], out=tmp[:], in_=r[:], func=AF.Exp, scale=cd[:, 1, d:d + 1])
                    self.I("scalar", "activation", [r, cd], [r], out=r[:], in_=r[:], func=AF.Exp, scale=cd[:, 0, d:d + 1])
                    self.I("scalar", "activation", [tmp], [tmp], out=tmp[:], in_=tmp[:], func=AF.Sqrt, scale=-1.0, bias=1.0)
                    self.I("gpsimd", "tensor_tensor", [tmp, ig], [tmp], out=tmp[:], in0=tmp[:], in1=ig[:], op=ALU.mult)
                    self.I("gpsimd", "tensor_tensor", [tmp, xc], [tmp], out=tmp[:], in0=tmp[:], in1=xc[:], op=ALU.mult)
                    if d == 0:
                        self.I("vector", "tensor_tensor_scan", [r, tmp], [hf], out=hf[:], data0=r[:], data1=tmp[:], initial=0.0,
                               op0=ALU.mult, op1=ALU.add)
                    else:
                        rv = lambda tl, a, b: tl[:, a:b][:, ::-1]
                        self.I("vector", "tensor_tensor_scan", [r, tmp], [hb], out=rv(hb, 0, TC), data0=rv(r, 0, TC), data1=rv(tmp, 0, TC),
                               initial=0.0, op0=ALU.mult, op1=ALU.add)
                        self.I("vector", "tensor_tensor_scan", [r, tmp, hb], [hb], out=rv(hb, TC, T), data0=rv(r, TC, T), data1=rv(tmp, TC, T),
                               initial=hb[:, 0:1], op0=ALU.mult, op1=ALU.add)
                self.I("vector", "tensor_tensor", [hf, hb], [ob], out=ob[:], in0=hf[:], in1=hb[:], op=ALU.add)
                self.dma("sync", self.OT[6 + cp], ob[:], reads=[ob])
            self.S.barrier()

    def load_const(self, es, name, shape=(128, 128), dtype=F32):
        t = self.sb(es, "c_" + name, list(shape), dtype)
        self.dma("sync", t[:], self.cst[name], writes=[t])
        return t

    def mixer_mlstm(self, l):
        W = self.w
        NCH = T // 128
        LN8 = float(np.log(8.0))
        with contextlib.ExitStack() as es:
            trif = self.load_const(es, "trif")
            trib = self.load_const(es, "trib")
            ones = self.load_const(es, "ones")
            gb = self.sb(es, "m_gb", [128, 16])
            z = self.sb(es, "m_z", [128, NCH, 16])
            nlf = self.sb(es, "m_nlf", [128, NCH, 16])
            ncF = self.sb(es, "m_ncF", [128, NCH, 16])
            ncB = self.sb(es, "m_ncB", [128, NCH, 16])
            es_ = [self.sb(es, "m_es%d" % d, [128, NCH, 4]) for d in range(2)]
            eb_ = [self.sb(es, "m_eb%d" % d, [128, NCH, 4]) for d in range(2)]
            eg_ = [self.sb(es, "m_eg%d" % d, [128, NCH, 4]) for d in range(2)]
            ng = self.sb(es, "m_ng", [128, 256])
            qT = self.sb(es, "m_qT", [128, T])
            kTm = [self.sb(es, "m_kTm%d" % i, [128, T]) for i in range(2)]
            vt = self.sb(es, "m_vt", [128, NCH, 128])
            kt = self.sb(es, "m_kt", [128, NCH, 128])
            hd = [self.sb(es, "m_h%d" % d, [128, NCH, 128]) for d in range(2)]
            Cst = self.sb(es, "m_C", [128, 2, 65])
            AT = [self.sb(es, "m_AT%d" % i, [128, 2, 128]) for i in range(2)]
            Vt = [self.sb(es, "m_Vt%d" % i, [128, 2, 65]) for i in range(2)]
            sm = [self.sb(es, "m_sm%d" % i, [128, 4, 2]) for i in range(2)]
            red = self.sb(es, "m_red", [128, NCH * 2])
            red2 = self.sb(es, "m_red2", [128, NCH * 2])
            ob = self.sb(es, "m_ob", [128, T], BF16)
            ps_s = [self.ps(es, "m_pss%d" % i, [128, 2, 128]) for i in range(2)]
            ps_n = [self.ps(es, "m_psn%d" % i, [128, 2, 65]) for i in range(2)]
            ps_k = [self.ps(es, "m_psk%d" % i, [128, 2, 65]) for i in range(2)]
            ps_g = [self.ps(es, "m_psg%d" % i, [128, 512]) for i in range(2)]
            self.dma("sync", gb[:], W["mlstm_gate_b"][l].rearrange("a b c -> (a b c)").partition_broadcast(128), writes=[gb])
            self.dma("sync", ng[:], W["mlstm_norm_g"][l].partition_broadcast(128), writes=[ng])
            self.dma("sync", z[:], self.UT[:, 512:528].rearrange("(c p) g -> p c g", p=128), writes=[z])
            self.I("vector", "tensor_tensor", [z, gb], [z], out=z[:], in0=z[:], in1=gb[:].unsqueeze(1).to_broadcast([128, NCH, 16]), op=ALU.add)
            self.I("scalar", "activation", [z], [nlf], out=nlf[:], in_=z[:], func=AF.Exp, scale=-1.0)
            self.I("scalar", "activation", [nlf], [nlf], out=nlf[:], in_=nlf[:], func=AF.Ln, bias=1.0)
            for (tri, dst) in ((trif, ncF), (trib, ncB)):
                for hf_ in range(2):
                    p_ = ps_g[hf_]
                    self.mm(p_, p_[:, 0:272], tri, tri[:], nlf, nlf[:, hf_ * 17:(hf_ + 1) * 17, :])
                    self.I("vector", "tensor_copy", [p_], [dst], out=dst[:, hf_ * 17:(hf_ + 1) * 17, :], in_=p_[:, 0:272])
            for d, ncx in ((0, ncF), (1, ncB)):
                ci, cf = d * 8, d * 8 + 4
                self.I("vector", "tensor_tensor", [z, ncx], [es_[d]], out=es_[d][:], in0=z[:, :, ci:ci + 4], in1=ncx[:, :, cf:cf + 4], op=ALU.add)
                self.I("scalar", "activation", [es_[d]], [es_[d]], out=es_[d][:], in_=es_[d][:], func=AF.Exp, bias=-LN8)
                self.I("scalar", "activation", [ncx], [eb_[d]], out=eb_[d][:], in_=ncx[:, :, cf:cf + 4], func=AF.Exp, scale=-1.0)
            for hf_ in range(2):
                p_ = ps_g[hf_]
                self.mm(p_, p_[:, 0:272], ones, ones[:], nlf, nlf[:, hf_ * 17:(hf_ + 1) * 17, :])
                for d in range(2):
                    self.I("scalar", "activation", [p_], [eg_[d]], out=eg_[d][:, hf_ * 17:(hf_ + 1) * 17, :],
                           in_=p_[:, 0:272].rearrange("p (c g) -> p c g", g=16)[:, :, d * 8 + 4:d * 8 + 8], func=AF.Exp, scale=-1.0)
            order = [list(range(NCH)), [1, 0] + list(range(NCH - 1, 1, -1))]
            nb = 0
            for hp in range(2):
                self.dma("sync", qT[:], self.UF[hp], writes=[qT])
                for hh in range(2):
                    self.I("gpsimd", "memset", [], [kTm[hh]], kTm[hh][:], 0.0)
                    self.dma("sync", kTm[hh][hh * 64:(hh + 1) * 64, :], self.UF[2 + hp, hh * 64:(hh + 1) * 64, :], writes=[kTm[hh]])
                self.dma("sync", vt[:], self.UT[:, hp * 128:(hp + 1) * 128].rearrange("(c p) g -> p c g", p=128), writes=[vt])
                self.dma("sync", kt[:], self.UT[:, 528 + hp * 128:528 + (hp + 1) * 128].rearrange("(c p) g -> p c g", p=128), writes=[kt])
                for d in range(2):
                    mask = trif if d == 0 else trib
                    self.I("vector", "memset", [], [Cst], Cst[:], 0.0)
                    seq = order[d]

                    def stage_a(c, k):
                        cs = slice(c * 128, (c + 1) * 128)
                        pS, A_, V_ = ps_s[k % 2], AT[k % 2], Vt[k % 2]
                        for hh in range(2):
                            self.mm(pS, pS[:, hh, :], kTm[hh], kTm[hh][:, cs], qT, qT[:, cs])
                        self.I("vector", "tensor_tensor", [pS, mask], [A_], out=A_[:], in0=pS[:], in1=mask[:].unsqueeze(1).to_broadcast([128, 2, 128]), op=ALU.mult)
                        esl = es_[d][:, c, 2 * hp:2 * hp + 2]
                        self.I("gpsimd", "tensor_tensor", [vt, es_[d]], [V_], out=V_[:, :, 0:64], in0=vt[:, c, :].rearrange("p (h e) -> p h e", h=2),
                               in1=esl.unsqueeze(2).to_broadcast([128, 2, 64]), op=ALU.mult)
                        self.I("gpsimd", "tensor_copy", [es_[d]], [V_], out=V_[:, :, 64], in_=esl)

                    stage_a(seq[0], nb)
                    for ci, c in enumerate(seq):
                        if ci + 1 < len(seq):
                            stage_a(seq[ci + 1], nb + 1)
                        cs = slice(c * 128, (c + 1) * 128)
                        pN, pK = ps_n[nb % 2], ps_k[nb % 2]
                        A_, V_, s_ = AT[nb % 2], Vt[nb % 2], sm[nb % 2]
                        nb += 1
                        for hh in range(2):
                            self.mm(pN, pN[:, hh, :], A_, A_[:, hh, :], V_, V_[:, hh, :], start=True, stop=False)
                            self.mm(pN, pN[:, hh, :], qT, qT[:, cs], Cst, Cst[:, hh, :], start=False, stop=True)
                        for hh in range(2):
                            self.mm(pK, pK[:, hh, :], kt, kt[:, c, :], V_, V_[:, hh, :])
                        for hh in range(2):
                            rows = slice(hh * 64, (hh + 1) * 64)
                            self.I("vector", "tensor_tensor", [pK, Cst], [Cst], out=Cst[rows, hh, :], in0=pK[rows, hh, :], in1=Cst[rows, hh, :], op=ALU.add)
                            self.I("vector", "tensor_scalar_mul", [Cst, eg_[d]], [Cst], out=Cst[rows, hh, :], in0=Cst[rows, hh, :],
                                   scalar1=eg_[d][rows, c, 2 * hp + hh:2 * hp + hh + 1])
                        ebl = eb_[d][:, c, 2 * hp:2 * hp + 2]
                        self.I("vector", "tensor_tensor", [pN, eb_[d]], [s_], out=s_[:, 0, :], in0=pN[:, :, 64], in1=ebl, op=ALU.mult)
                        self.I("scalar", "activation", [s_], [s_], out=s_[:, 1, :], in_=s_[:, 0, :], func=AF.Abs)
                        self.I("vector", "tensor_scalar_max", [s_], [s_], out=s_[:, 1, :], in0=s_[:, 1, :], scalar1=1.0)
                        self.I("vector", "reciprocal", [s_], [s_], out=s_[:, 2, :], in_=s_[:, 1, :])
                        self.I("gpsimd", "tensor_tensor", [s_, eb_[d]], [s_], out=s_[:, 3, :], in0=s_[:, 2, :], in1=ebl, op=ALU.mult)
                        self.I("vector", "tensor_tensor", [pN, s_], [hd[d]], out=hd[d][:, c, :].rearrange("p (h e) -> p h e", h=2), in0=pN[:, :, 0:64],
                               in1=s_[:, 3, :].unsqueeze(2).to_broadcast([128, 2, 64]), op=ALU.mult)
                hs, sq = hd[0], hd[1]
                hs4 = hs[:].rearrange("p c (h e) -> p (c h) e", h=2)
                sq4 = sq[:].rearrange("p c (h e) -> p (c h) e", h=2)
                NG = NCH * 2
                self.I("vector", "tensor_tensor", [hd[0], hd[1]], [hs], out=hs[:], in0=hd[0][:], in1=hd[1][:], op=ALU.add)
                self.I("vector", "tensor_reduce", [hs], [red], out=red[:], in_=hs4, axis=AX.X, op=ALU.add)
                self.I("vector", "tensor_scalar_mul", [red], [red], out=red[:], in0=red[:], scalar1=-1.0 / 64.0)
                self.I("vector", "tensor_tensor", [hs, red], [hs], out=hs4, in0=hs4, in1=red[:].unsqueeze(2).to_broadcast([128, NG, 64]), op=ALU.add)
                self.I("gpsimd", "tensor_tensor", [hs], [sq], out=sq[:], in0=hs[:], in1=hs[:], op=ALU.mult)
                self.I("vector", "tensor_reduce", [sq], [red2], out=red2[:], in_=sq4, axis=AX.X, op=ALU.add)
                self.rsqrt(red2, red2[:], red2, red2[:], LN_EPS, scale=1.0 / 64.0)
                self.I("vector", "tensor_tensor", [hs, red2], [hs], out=hs4, in0=hs4, in1=red2[:].unsqueeze(2).to_broadcast([128, NG, 64]), op=ALU.mult)
                self.I("gpsimd", "tensor_tensor", [hs, ng], [hs], out=hs[:], in0=hs[:], in1=ng[:, hp * 128:(hp + 1) * 128].unsqueeze(1).to_broadcast([128, NCH, 128]), op=ALU.mult)
                self.dma("sync", sq[:], self.UT[:, 256 + hp * 128:256 + (hp + 1) * 128].rearrange("(c p) g -> p c g", p=128), reads=[], writes=[sq])
                self.I("scalar", "activation", [sq], [sq], out=sq[:], in_=sq[:], func=AF.Sigmoid)
                self.I("vector", "tensor_tensor", [hs, sq], [hs], out=hs[:], in0=hs[:], in1=sq[:], op=ALU.mult)
                for c in range(NCH):
                    p_ = ps_g[(c // 4) % 2]
                    self.mm(p_, p_[:, (c % 4) * 128:(c % 4 + 1) * 128], hs, hs[:, c, :], self.ident, self.ident[:])
                    if c % 4 == 3 or c == NCH - 1:
                        c0 = (c // 4) * 4
                        n = (c - c0 + 1) * 128
                        self.I("scalar", "copy", [p_], [ob], out=ob[:, c0 * 128:c0 * 128 + n], in_=p_[:, 0:n])
                self.dma("sync", self.OT[0 + hp], ob[:], reads=[ob])
            self.S.barrier()

    def mixer_mla(self, l):
        W = self.w
        NCH = T // 128
        SCALE = 96.0 ** -0.5
        RP = slice(64, 96)
        with contextlib.ExitStack() as es:
            ones = self.load_const(es, "ones")
            cqA = self.sb(es, "a_cqA", [128, T])
            cqB = self.sb(es, "a_cqB", [128, T])
            ckv = self.sb(es, "a_ckv", [128, T])
            rq = self.sb(es, "a_rq", [128, T])
            rkv = self.sb(es, "a_rkv", [128, T])
            rkvt = self.sb(es, "a_rkvt", [128, NCH])
            sqt = [self.sb(es, "a_sq%d" % i, [128, 512]) for i in range(2)]
            gq = self.sb(es, "a_gq", [128, 2])
            gkv = self.sb(es, "a_gkv", [128, 1])
            wqA = self.sb(es, "a_wqA", [128, 384])
            wqB = self.sb(es, "a_wqB", [128, 384])
            wqRA = self.sb(es, "a_wqRA", [128, 4, 96])
            wqRB = self.sb(es, "a_wqRB", [128, 4, 96])
            wk = self.sb(es, "a_wk", [128, 256])
            wv = self.sb(es, "a_wv", [128, 256])
            qf = self.sb(es, "a_qf", [128, T], BF16)
            kf = self.sb(es, "a_kf", [128, T], BF16)
            kRs = self.sb(es, "a_kRs", [128, T], BF16)
            Vf = self.sb(es, "a_V", [128, NCH, 4 * 65 + 64], BF16)
            cs_ = [self.sb(es, "a_cs%d" % i, [128, 2, 512]) for i in range(2)]
            tr = [self.sb(es, "a_tr%d" % i, [128, 2, 512]) for i in range(2)]
            PT = [self.sb(es, "a_PT%d" % i, [128, 512], BF16) for i in range(3)]
            oTs = [self.sb(es, "a_oT%d" % i, [128, 512]) for i in range(2)]
            rc = [self.sb(es, "a_rc%d" % i, [128, 4]) for i in range(2)]
            psc = [self.ps(es, "a_psc%d" % i, [128, 512]) for i in range(2)]
            ppo = [self.ps(es, "a_ppo%d" % i, [128, 512]) for i in range(2)]
            ptk = self.ps(es, "a_ptk", [128, 512])
            pj = [self.ps(es, "a_pj%d" % i, [128, 512]) for i in range(2)]
            es2 = contextlib.ExitStack()
            kr = self.sb(es2, "a_kr", [128, T])
            krR = self.sb(es2, "a_krR", [128, T])
            self.dma("sync", cqA[:], self.UF[12], writes=[cqA])
            self.dma("sync", cqB[0:64, :], self.UF[13, 0:64, :], writes=[cqB])
            self.dma("sync", ckv[:], self.UF[14], writes=[ckv])
            self.dma("sync", kr[RP, :], self.UF[15, 0:32, :], writes=[kr])
            self.dma("sync", krR[RP, :], self.UF[15, 32:64, :], writes=[krR])
            self.dma("sync", gq[:, 0:1], W["mla_qn_g"][l, 0:128].rearrange("(c o) -> c o", o=1), writes=[gq])
            self.dma("sync", gq[0:64, 1:2], W["mla_qn_g"][l, 128:192].rearrange("(c o) -> c o", o=1), writes=[gq])
            self.dma("sync", gkv[:], W["mla_kvn_g"][l].rearrange("(c o) -> c o", o=1), writes=[gkv])
            self.dma("sync", wqA[:], W["mla_wuq"][l, 0:128, :], writes=[wqA])
            self.dma("sync", wqB[0:64, :], W["mla_wuq"][l, 128:192, :], writes=[wqB])
            self.dma("sync", wk[:], W["mla_wuk"][l], writes=[wk])
            self.dma("sync", wv[:], W["mla_wuv"][l], writes=[wv])
            self.I("vector", "tensor_scalar_mul", [wqA, gq], [wqA], out=wqA[:], in0=wqA[:], scalar1=gq[:, 0:1])
            self.I("vector", "tensor_scalar_mul", [wqB, gq], [wqB], out=wqB[0:64, :], in0=wqB[0:64, :], scalar1=gq[0:64, 1:2])
            self.I("vector", "tensor_scalar_mul", [wk, gkv], [wk], out=wk[:], in0=wk[:], scalar1=gkv[:, 0:1])
            self.I("vector", "tensor_scalar_mul", [wv, gkv], [wv], out=wv[:], in0=wv[:], scalar1=gkv[:, 0:1])
            for (wsrc, wdst, rows) in ((wqA, wqRA, 128), (wqB, wqRB, 64)):
                self.I("gpsimd", "memset", [], [wdst], wdst[:], 0.0)
                for h in range(4):
                    b0 = h * 96 + 64
                    for (dst, srcc, sgn) in ((0, 8, -1.0), (8, 0, 1.0), (16, 24, -1.0), (24, 16, 1.0)):
                        self.I("vector", "tensor_scalar_mul", [wsrc], [wdst], out=wdst[0:rows, h, 64 + dst:64 + dst + 8],
                               in0=wsrc[0:rows, b0 + srcc:b0 + srcc + 8], scalar1=sgn)
            nj = 0
            for (t0, n) in BLOCKS:
                for which in range(2):
                    p_ = pj[nj % 2]
                    nj += 1
                    if which == 0:
                        s1, s2 = sqt[0], sqt[1]
                        self.I("scalar", "activation", [cqA], [s1], out=s1[:, 0:n], in_=cqA[:, t0:t0 + n], func=AF.Square)
                        self.I("scalar", "activation", [cqB], [s2], out=s2[0:64, 0:n], in_=cqB[0:64, t0:t0 + n], func=AF.Square)
                        self.mm(p_, p_[:, 0:n], ones, ones[:], s1, s1[:, 0:n], start=True, stop=False)
                        self.mm(p_, p_[:, 0:n], ones, ones[0:64, :], s2, s2[0:64, 0:n], start=False, stop=True)
                        self.I("vector", "tensor_scalar", [p_], [rq], out=rq[:, t0:t0 + n], in0=p_[:, 0:n], scalar1=1.0 / 192.0, scalar2=RMS_EPS, op0=ALU.mult, op1=ALU.add)
                    else:
                        s1 = sqt[0]
                        self.I("scalar", "activation", [ckv], [s1], out=s1[:, 0:n], in_=ckv[:, t0:t0 + n], func=AF.Square)
                        self.mm(p_, p_[:, 0:n], ones, ones[:], s1, s1[:, 0:n])
                        self.I("vector", "tensor_scalar", [p_], [rkv], out=rkv[:, t0:t0 + n], in0=p_[:, 0:n], scalar1=1.0 / 128.0, scalar2=RMS_EPS, op0=ALU.mult, op1=ALU.add)
                        p2 = pj[nj % 2]
                        nj += 1
                        for a in range(n // 128):
                            self.mm(p2, p2[:, a:a + 1], s1, s1[:, a * 128:(a + 1) * 128], ones, ones[:, 0:1])
                        c0 = t0 // 128
                        self.I("vector", "tensor_scalar", [p2], [rkvt], out=rkvt[:, c0:c0 + n // 128], in0=p2[:, 0:n // 128], scalar1=1.0 / 128.0, scalar2=RMS_EPS, op0=ALU.mult, op1=ALU.add)
            for tl_ in (rq, rkv, rkvt):
                self.rsqrt(tl_, tl_[:], tl_, tl_[:], 0.0)
            self.I("gpsimd", "memset", [], [Vf], Vf[:], 1.0)
            self.I("gpsimd", "memset", [], [qf], qf[96:128, :], 0.0)
            self.I("gpsimd", "memset", [], [kf], kf[96:128, :], 0.0)
            for c in range(NCH):
                p_ = pj[nj % 2]
                nj += 1
                self.mm(p_, p_[:, 0:256], ckv, ckv[:, c * 128:(c + 1) * 128], wv, wv[:])
                self.I("vector", "tensor_scalar_mul", [p_, rkvt], [Vf], out=Vf[:, c, 0:260].rearrange("p (h e) -> p h e", h=4)[:, :, 0:64],
                       in0=p_[:, 0:256].rearrange("p (h e) -> p h e", h=4), scalar1=rkvt[:, c:c + 1])
            for bi, (t0, n) in enumerate(BLOCKS):
                if t0 < TC:
                    self.I("vector", "tensor_copy", [kr], [kRs], out=kRs[RP, t0:t0 + n], in_=kr[RP, t0:t0 + n])
                    continue
                c_ = cs_[bi % 2]
                t_ = tr[bi % 2]
                self.dma("sync", c_[RP, 0, 0:n], self.cst["cosT"][:, t0 - TC:t0 - TC + n], writes=[c_])
                self.dma("sync", c_[RP, 1, 0:n], self.cst["sinT"][:, t0 - TC:t0 - TC + n], writes=[c_])
                self.I("vector", "tensor_tensor", [kr, c_], [t_], out=t_[RP, 0, 0:n], in0=kr[RP, t0:t0 + n], in1=c_[RP, 0, 0:n], op=ALU.mult)
                self.I("gpsimd", "tensor_tensor", [krR, c_], [t_], out=t_[RP, 1, 0:n], in0=krR[RP, t0:t0 + n], in1=c_[RP, 1, 0:n], op=ALU.mult)
                self.I("vector", "tensor_tensor", [t_], [kRs], out=kRs[RP, t0:t0 + n], in0=t_[RP, 0, 0:n], in1=t_[RP, 1, 0:n], op=ALU.add)
            es2.close()
            otok = self.sb(es, "a_otok", [128, NCH, 128])
            ob = self.sb(es, "a_ob", [128, T], BF16)
            nb = 0
            for hp in range(2):
                for hh in range(2):
                    h = hp * 2 + hh
                    b0 = h * 96
                    for bi, (t0, n) in enumerate(BLOCKS):
                        pq, pk = pj[0], pj[1]
                        self.mm(pq, pq[0:96, 0:n], wqA, wqA[:, b0:b0 + 96], cqA, cqA[:, t0:t0 + n], start=True, stop=False)
                        self.mm(pq, pq[0:96, 0:n], wqB, wqB[0:64, b0:b0 + 96], cqB, cqB[0:64, t0:t0 + n], start=False, stop=True)
                        self.I("vector", "tensor_tensor", [pq, rq], [qf], out=qf[0:64, t0:t0 + n], in0=pq[0:64, 0:n], in1=rq[0:64, t0:t0 + n], op=ALU.mult)
                        self.mm(pk, pk[0:64, 0:n], wk, wk[:, h * 64:(h + 1) * 64], ckv, ckv[:, t0:t0 + n])
                        self.I("vector", "tensor_tensor", [pk, rkv], [kf], out=kf[0:64, t0:t0 + n], in0=pk[0:64, 0:n], in1=rkv[0:64, t0:t0 + n], op=ALU.mult)
                        if t0 < TC:
                            self.I("vector", "tensor_tensor", [pq, rq], [qf], out=qf[RP, t0:t0 + n], in0=pq[RP, 0:n], in1=rq[RP, t0:t0 + n], op=ALU.mult)
                            continue
                        c_ = cs_[bi % 2]
                        t_ = tr[bi % 2]
                        self.dma("sync", c_[RP, 0, 0:n], self.cst["cosT"][:, t0 - TC:t0 - TC + n], writes=[c_])
                        self.dma("sync", c_[RP, 1, 0:n], self.cst["sinT"][:, t0 - TC:t0 - TC + n], writes=[c_])
                        self.I("vector", "tensor_tensor", [pq, c_], [t_], out=t_[RP, 0, 0:n], in0=pq[RP, 0:n], in1=c_[RP, 0, 0:n], op=ALU.mult)
                        p2 = pj[1]
                        self.mm(p2, p2[0:96, 0:n], wqRA, wqRA[:, h, :], cqA, cqA[:, t0:t0 + n], start=True, stop=False)
                        self.mm(p2, p2[0:96, 0:n], wqRB, wqRB[0:64, h, :], cqB, cqB[0:64, t0:t0 + n], start=False, stop=True)
                        self.I("vector", "tensor_tensor", [p2, c_], [t_], out=t_[RP, 1, 0:n], in0=p2[RP, 0:n], in1=c_[RP, 1, 0:n], op=ALU.mult)
                        self.I("gpsimd", "tensor_tensor", [t_], [t_], out=t_[RP, 0, 0:n], in0=t_[RP, 0, 0:n], in1=t_[RP, 1, 0:n], op=ALU.add)
                        self.I("gpsimd", "tensor_tensor", [t_, rq], [qf], out=qf[RP, t0:t0 + n], in0=t_[RP, 0, 0:n], in1=rq[RP, t0:t0 + n], op=ALU.mult)
                    self.I("gpsimd", "tensor_copy", [kRs], [kf], out=kf[RP, :], in_=kRs[RP, :])
                    for bi, (t0, n) in enumerate(BLOCKS):
                        nkt = 2 if t0 < TC else NCH
                        nqs = n // 128
                        po = ppo[bi % 2]
                        def score(kt_):
                            ps__ = psc[(nb + kt_) % 2]
                            self.mm(ps__, ps__[:, 0:n], kf, kf[:, kt_ * 128:(kt_ + 1) * 128], qf, qf[:, t0:t0 + n])
                        score(0)
                        for kt in range(nkt):
                            if kt + 1 < nkt:
                                score(kt + 1)
                            ps_ = psc[(nb + kt) % 2]
                            pt_ = PT[(nb + kt) % 3]
                            self.I("scalar", "activation", [ps_], [pt_], out=pt_[:, 0:n], in_=ps_[:, 0:n], func=AF.Exp, scale=SCALE)
                            self.mm(po, po[:, 0:n], Vf, Vf[:, kt, h * 65:h * 65 + 128], pt_, pt_[:, 0:n], start=(kt == 0), stop=(kt == nkt - 1))
                        nb += nkt
                        oT_ = oTs[bi % 2]
                        self.I("scalar", "copy", [po], [oT_], out=oT_[0:65, 0:n], in_=po[0:65, 0:n])
                        for qs in range(nqs):
                            self.mm(ptk, ptk[:, qs * 65:(qs + 1) * 65], oT_, oT_[0:65, qs * 128:(qs + 1) * 128], self.ident, self.ident[0:65, 0:65])
                        r_ = rc[bi % 2]
                        pk3 = ptk[:, 0:nqs * 65].rearrange("p (q e) -> p q e", e=65)
                        c0 = t0 // 128
                        self.I("vector", "reciprocal", [ptk], [r_], out=r_[:, 0:nqs], in_=pk3[:, :, 64])
                        self.I("vector", "tensor_tensor", [ptk, r_], [otok], out=otok[:, c0:c0 + nqs, hh * 64:(hh + 1) * 64], in0=pk3[:, :, 0:64],
                               in1=r_[:, 0:nqs].unsqueeze(2).to_broadcast([128, nqs, 64]), op=ALU.mult)
                for c in range(NCH):
                    p_ = pj[(c // 4) % 2]
                    self.mm(p_, p_[:, (c % 4) * 128:(c % 4 + 1) * 128], otok, otok[:, c, :], self.ident, self.ident[:])
                    if c % 4 == 3 or c == NCH - 1:
                        c0 = (c // 4) * 4
                        n = (c - c0 + 1) * 128
                        self.I("scalar", "copy", [p_], [ob], out=ob[:, c0 * 128:c0 * 128 + n], in_=p_[:, 0:n])
                self.dma("sync", self.OT[4 + hp], ob[:], reads=[ob])
            self.S.barrier()

    def mixer_rwkv(self, l):
        W = self.w
        NCA = T // 64
        DEC = float(np.exp(-0.5))
        with contextlib.ExitStack() as es:
            su = self.load_const(es, "m_su", (128, 64))
            sl = self.load_const(es, "m_sl", (128, 64))
            uu = self.load_const(es, "m_u", (128, 64))
            ll = self.load_const(es, "m_l", (128, 64))
            idb = self.load_const(es, "idb", (128, 64))
            bones = self.load_const(es, "bones")
            rmask = self.load_const(es, "rmask", (128, 512))
            ones = self.load_const(es, "ones")
            sbf = lambda name, shape, dt=F32: self.sb(es, "r_" + name, shape, dt)
            prm = sbf("prm", [128, 20])
            plw = sbf("plw", [128, 3])
            pla = sbf("pla", [128, 3])
            plg = sbf("plg", [128, 3])
            gnb = sbf("gnb", [128, 64])
            W2p = [sbf("W2p%d" % d, [64, 256]) for d in range(2)]
            A2p = [sbf("A2p%d" % d, [64, 256]) for d in range(2)]
            g2r = sbf("g2r", [128, 256])
            yacc = sbf("yacc", [128, NCA, 64])
            vtk = sbf("vtk", [128, NCA, 64])
            gat = sbf("gat", [128, NCA, 64])
            rks = sbf("rks", [128, NCA])
            red = sbf("red", [128, NCA])
            red2 = sbf("red2", [128, NCA])
            ob = sbf("ob", [128, T], BF16)
            xb = [sbf("xb%d" % i, [128, 516]) for i in range(2)]
            xs = [sbf("xs%d" % i, [128, 512]) for i in range(2)]
            names = ["rs", "ks", "vs", "wl", "al", "gl", "td", "ld", "a", "sg", "cum", "Ep", "Em", "Eq", "kkr", "sq", "rn", "kk", "b", "kd",
                     "Kap", "Kt", "Bt", "Rt", "rk"]
            F = {nm: sbf(nm, [128, 512]) for nm in names}
            X = sbf("X", [128, 8, 128])
            W2 = sbf("W2", [128, 8, 128])
            M = {nm: sbf(nm, [128, 8, 64]) for nm in ["Btok", "NBtok", "Kttok", "Vtok", "A0", "A1", "At0", "At1", "TT0", "TT1", "AkT", "ArT", "AbT",
                                                       "NAbT", "G2", "Hp", "QpT", "Yloc", "ytmp"]}
            ST = [sbf("ST%d" % i, [128, 64]) for i in range(3)]
            pg = [self.ps(es, "r_pg%d" % i, [128, 512]) for i in range(4)]
            pW = [self.ps(es, "r_pW%d" % i, [128, 4, 128]) for i in range(2)]
            pZ = self.ps(es, "r_pZ", [128, 512])
            pY = self.ps(es, "r_pY", [128, 8, 64])
            self._npg = 0

            def nextpg():
                self._npg += 1
                return pg[self._npg % 4]

            def col(ap1d):
                return ap1d.rearrange("(c o) -> c o", o=1)
            HR = [slice(0, 64), slice(64, 128)]
            for d in range(2):
                for (tl_, nm) in ((W2p[d], "rwkv_w2"), (A2p[d], "rwkv_a2")):
                    self.I("gpsimd", "memset", [], [tl_], tl_[:], 0.0)
                    self.dma("sync", tl_[d * 32:(d + 1) * 32, :], W[nm][l, d], writes=[tl_])
            for hh in range(2):
                self.dma("sync", g2r[HR[hh], :], W["rwkv_g2"][l], writes=[g2r])
            mu = W["rwkv_mu"][l]
            for (tl_, rows, off) in ((plw, 64, 768), (pla, 64, 832), (plg, 64, 896)):
                self.I("vector", "memset", [], [tl_], tl_[:], 0.0)
                self.dma("sync", tl_[0:64, 0:1], col(mu[off:off + 64]), writes=[tl_])
                if tl_ is plg:
                    self.dma("sync", tl_[64:128, 0:1], col(mu[off:off + 64]), writes=[tl_])
                self.I("vector", "tensor_scalar", [tl_], [tl_], out=tl_[:, 1:2], in0=tl_[:, 0:1], scalar1=-1.0, scalar2=1.0, op0=ALU.mult, op1=ALU.add)
                self.I("vector", "tensor_scalar_mul", [tl_], [tl_], out=tl_[:, 2:3], in0=tl_[:, 0:1], scalar1=0.5)

            def shift_load(dst, rows, src, t0, n, p3, k):
                has_prev = t0 not in (0, TC)
                has_next = (t0 + n) not in (TC, T)
                lo, hi = int(has_prev), int(has_next)
                x_, s_ = xb[k % 2], xs[k % 2]
                self.dma("sync", x_[0:rows, 1 - lo:n + 1 + hi], src[:, t0 - lo:t0 + n + hi], writes=[x_])
                if not has_prev:
                    self.I("gpsimd", "memset", [], [x_], x_[0:rows, 0:1], 0.0)
                if not has_next:
                    self.I("gpsimd", "memset", [], [x_], x_[0:rows, n + 1:n + 2], 0.0)
                self.I("gpsimd", "tensor_tensor", [x_], [s_], out=s_[0:rows, 0:n], in0=x_[0:rows, 0:n], in1=x_[0:rows, 2:n + 2], op=ALU.add)
                self.I("scalar", "mul", [s_, p3[0]], [s_], out=s_[0:rows, 0:n], in_=s_[0:rows, 0:n], mul=p3[2])
                self.I("vector", "scalar_tensor_tensor", [x_, s_, p3[0]], [dst], out=dst[0:rows, 0:n], in0=x_[0:rows, 1:n + 1], scalar=p3[1],
                       in1=s_[0:rows, 0:n], op0=ALU.mult, op1=ALU.add)

            nld = 0
            for hp in range(2):
                chs = slice(hp * 128, (hp + 1) * 128)
                for j, off in enumerate((0, 256, 512)):
                    self.dma("sync", prm[:, j:j + 1], col(mu[off + hp * 128:off + (hp + 1) * 128]), writes=[prm])
                self.I("vector", "tensor_scalar", [prm], [prm], out=prm[:, 3:6], in0=prm[:, 0:3], scalar1=-1.0, scalar2=1.0, op0=ALU.mult, op1=ALU.add)
                self.I("vector", "tensor_scalar_mul", [prm], [prm], out=prm[:, 6:9], in0=prm[:, 0:3], scalar1=0.5)
                for j, (nm, d) in enumerate((("rwkv_w0", 0), ("rwkv_w0", 1), ("rwkv_a0", 0), ("rwkv_a0", 1))):
                    self.dma("sync", prm[:, 9 + j:10 + j], col(W[nm][l, d, chs]), writes=[prm])
                self.dma("sync", prm[:, 13:14], col(W["rwkv_kk"][l, chs]), writes=[prm])
                self.dma("sync", prm[:, 14:15], col(W["rwkv_ka"][l, chs]), writes=[prm])
                self.dma("sync", prm[:, 16:17], col(W["rwkv_rk"][l].rearrange("h e -> (h e)")[chs]), writes=[prm])
                self.I("vector", "tensor_scalar", [prm], [prm], out=prm[:, 15:16], in0=prm[:, 14:15], scalar1=-1.0, scalar2=1.0, op0=ALU.mult, op1=ALU.add)
                for hh in range(2):
                    h = hp * 2 + hh
                    self.dma("sync", gnb[HR[hh], :], W["rwkv_gn_g"][l, h * 64:(h + 1) * 64].partition_broadcast(64), writes=[gnb])
                for d in range(2):
                    self.I("vector", "memset", [], [ST[0]], ST[0][:], 0.0)
                    nst = 0
                    blocks = BLOCKS if d == 0 else [BLOCKS[0]] + BLOCKS[:0:-1]
                    for (t0, n) in blocks:
                        ncb = n // 64
                        gc0 = t0 // 64
                        v3 = lambda tl_, a=0, b=64: tl_[:, 0:n].rearrange("p (c t) -> p c t", t=64)
                        for (nm, idx, j) in (("rs", 4 + hp, 0), ("ks", 6 + hp, 1), ("vs", 8 + hp, 2)):
                            shift_load(F[nm], 128, self.UF[idx], t0, n, (prm, prm[:, 3 + j:4 + j], prm[:, 6 + j:7 + j]), nld)
                            nld += 1
                        shift_load(F["wl"], 64, self.UF[10, 0:64, :], t0, n, (plw, plw[0:64, 1:2], plw[0:64, 2:3]), nld)
                        nld += 1
                        shift_load(F["al"], 64, self.UF[10, 64:128, :], t0, n, (pla, pla[0:64, 1:2], pla[0:64, 2:3]), nld)
                        nld += 1
                        if d == 0:
                            for hh in range(2):
                                x_ = xb[nld % 2]
                                pass
                            has_prev = t0 not in (0, TC)
                            has_next = (t0 + n) not in (TC, T)
                            lo, hi = int(has_prev), int(has_next)
                            x_, s_ = xb[nld % 2], xs[nld % 2]
                            nld += 1
                            for hh in range(2):
                                self.dma("sync", x_[HR[hh], 1 - lo:n + 1 + hi], self.UF[11, 0:64, t0 - lo:t0 + n + hi], writes=[x_])
                            if not has_prev:
                                self.I("gpsimd", "memset", [], [x_], x_[:, 0:1], 0.0)
                            if not has_next:
                                self.I("gpsimd", "memset", [], [x_], x_[:, n + 1:n + 2], 0.0)
                            self.I("gpsimd", "tensor_tensor", [x_], [s_], out=s_[:, 0:n], in0=x_[:, 0:n], in1=x_[:, 2:n + 2], op=ALU.add)
                            self.I("scalar", "mul", [s_, plg], [s_], out=s_[:, 0:n], in_=s_[:, 0:n], mul=plg[:, 2:3])
                            self.I("vector", "scalar_tensor_tensor", [x_, s_, plg], [F["gl"]], out=F["gl"][:, 0:n], in0=x_[:, 1:n + 1], scalar=plg[:, 1:2],
                                   in1=s_[:, 0:n], op0=ALU.mult, op1=ALU.add)
                            self.I("scalar", "activation", [F["gl"]], [F["sg"]], out=F["sg"][:, 0:n], in_=F["gl"][:, 0:n], func=AF.Sigmoid)
                        self.I("scalar", "activation", [F["wl"]], [F["td"]], out=F["td"][0:64, 0:n], in_=F["wl"][0:64, 0:n], func=AF.Tanh)
                        p_ = nextpg()
                        self.mm(p_, p_[:, 0:n], W2p[d], W2p[d][:, chs], F["td"], F["td"][0:64, 0:n])
                        self.I("scalar", "activation", [p_, prm], [F["ld"]], out=F["ld"][:, 0:n], in_=p_[:, 0:n], func=AF.Sigmoid, bias=prm[:, 9 + d:10 + d])
                        self.I("vector", "tensor_scalar_mul", [F["ld"]], [F["ld"]], out=F["ld"][:, 0:n], in0=F["ld"][:, 0:n], scalar1=-DEC)
                        p_ = nextpg()
                        self.mm(p_, p_[:, 0:n], A2p[d], A2p[d][:, chs], F["al"], F["al"][0:64, 0:n])
                        self.I("scalar", "activation", [p_, prm], [F["a"]], out=F["a"][:, 0:n], in_=p_[:, 0:n], func=AF.Sigmoid, bias=prm[:, 11 + d:12 + d])
                        rv = (lambda ap: ap) if d == 0 else (lambda ap: ap[:, ::-1])
                        self.I("vector", "tensor_tensor_scan", [F["ld"], rmask], [F["cum"]], out=rv(F["cum"][:, 0:n]), data0=rmask[:, 0:n], data1=rv(F["ld"][:, 0:n]),
                               initial=0.0, op0=ALU.mult, op1=ALU.add)
                        self.I("vector", "tensor_tensor", [F["cum"], F["ld"]], [F["Eq"]], out=F["Eq"][:, 0:n], in0=F["cum"][:, 0:n], in1=F["ld"][:, 0:n], op=ALU.subtract)
                        self.I("scalar", "activation", [F["cum"]], [F["Ep"]], out=F["Ep"][:, 0:n], in_=F["cum"][:, 0:n], func=AF.Exp)
                        self.I("scalar", "activation", [F["cum"]], [F["Em"]], out=F["Em"][:, 0:n], in_=F["cum"][:, 0:n], func=AF.Exp, scale=-1.0)
                        self.I("scalar", "activation", [F["Eq"]], [F["Eq"]], out=F["Eq"][:, 0:n], in_=F["Eq"][:, 0:n], func=AF.Exp)
                        self.I("vector", "tensor_scalar_mul", [F["ks"], prm], [F["kkr"]], out=F["kkr"][:, 0:n], in0=F["ks"][:, 0:n], scalar1=prm[:, 13:14])
                        self.I("gpsimd", "tensor_tensor", [F["kkr"]], [F["sq"]], out=F["sq"][:, 0:n], in0=F["kkr"][:, 0:n], in1=F["kkr"][:, 0:n], op=ALU.mult)
                        p_ = nextpg()
                        self.mm(p_, p_[:, 0:n], bones, bones[:], F["sq"], F["sq"][:, 0:n])
                        self.I("vector", "tensor_scalar_max", [p_], [F["rn"]], out=F["rn"][:, 0:n], in0=p_[:, 0:n], scalar1=1e-12)
                        self.rsqrt(F["rn"], F["rn"][:, 0:n], F["rn"], F["rn"][:, 0:n], 0.0)
                        self.I("gpsimd", "tensor_tensor", [F["kkr"], F["rn"]], [F["kk"]], out=F["kk"][:, 0:n], in0=F["kkr"][:, 0:n], in1=F["rn"][:, 0:n], op=ALU.mult)
                        self.I("gpsimd", "tensor_tensor", [F["kk"], F["a"]], [F["b"]], out=F["b"][:, 0:n], in0=F["kk"][:, 0:n], in1=F["a"][:, 0:n], op=ALU.mult)
                        self.I("vector", "tensor_scalar", [F["a"], prm], [F["kd"]], out=F["kd"][:, 0:n], in0=F["a"][:, 0:n], scalar1=prm[:, 14:15], scalar2=prm[:, 15:16],
                               op0=ALU.mult, op1=ALU.add)
                        self.I("gpsimd", "tensor_tensor", [F["kd"], F["ks"]], [F["kd"]], out=F["kd"][:, 0:n], in0=F["kd"][:, 0:n], in1=F["ks"][:, 0:n], op=ALU.mult)
                        self.I("vector", "tensor_tensor", [F["kk"], F["Eq"]], [F["Kap"]], out=F["Kap"][:, 0:n], in0=F["kk"][:, 0:n], in1=F["Eq"][:, 0:n], op=ALU.mult)
                        self.I("gpsimd", "tensor_tensor", [F["kd"], F["Em"]], [F["Kt"]], out=F["Kt"][:, 0:n], in0=F["kd"][:, 0:n], in1=F["Em"][:, 0:n], op=ALU.mult)
                        self.I("vector", "tensor_tensor", [F["b"], F["Em"]], [F["Bt"]], out=F["Bt"][:, 0:n], in0=F["b"][:, 0:n], in1=F["Em"][:, 0:n], op=ALU.mult)
                        self.I("gpsimd", "tensor_tensor", [F["rs"], F["Ep"]], [F["Rt"]], out=F["Rt"][:, 0:n], in0=F["rs"][:, 0:n], in1=F["Ep"][:, 0:n], op=ALU.mult)

                        def gram(dst_ps, L_, R_, nn=64):
                            for c in range(ncb):
                                for hh in range(2):
                                    self.mm(dst_ps, dst_ps[HR[hh], c * 64:c * 64 + 64], L_, L_[HR[hh], c * 64:(c + 1) * 64], R_, R_[HR[hh], c * 64:(c + 1) * 64])

                        def tmm(dst_ps, width, L_, lw_, R_, rw_, off_l=0, off_r=0, start=True, stop=True):
                            for c in range(ncb):
                                for hh in range(2):
                                    self.mm(dst_ps, dst_ps[HR[hh], c * width:c * width + rw_], L_, L_[HR[hh], c, off_l:off_l + lw_], R_, R_[HR[hh], c, off_r:off_r + rw_],
                                            start=start, stop=stop)
                        def tmm2(dst_ps, L1, R1, o1, L2, R2, o2):
                            for c in range(ncb):
                                for hh in range(2):
                                    self.mm(dst_ps, dst_ps[HR[hh], c * 64:c * 64 + 64], L1, L1[HR[hh], c, 0:64], R1, R1[HR[hh], c, o1:o1 + 64], start=True, stop=False)
                                    self.mm(dst_ps, dst_ps[HR[hh], c * 64:c * 64 + 64], L2, L2[HR[hh], c, 0:64], R2, R2[HR[hh], c, o2:o2 + 64], start=False, stop=True)
                        p3 = lambda p_: p_[:, 0:ncb * 64].rearrange("p (c t) -> p c t", t=64)
                        bc = lambda tl_: tl_[:].unsqueeze(1).to_broadcast([128, ncb, 64])
                        m3 = lambda nm: M[nm][:, 0:ncb, :]
                        for (src, kind) in ((F["Kap"], "X"), (F["Bt"], "B"), (F["Kt"], "Kt"), (F["vs"], "V")):
                            p_ = nextpg()
                            for c in range(ncb):
                                for hh in range(2):
                                    self.mm(p_, p_[HR[hh], c * 64:(c + 1) * 64], src, src[HR[hh], c * 64:(c + 1) * 64], self.ident, self.ident[HR[hh], HR[hh]])
                            if kind == "X":
                                self.I("vector", "tensor_copy", [p_], [X], out=X[:, 0:ncb, 0:64], in_=p3(p_))
                            elif kind == "B":
                                self.I("vector", "tensor_copy", [p_], [M["Btok"]], out=m3("Btok"), in_=p3(p_))
                                self.I("scalar", "mul", [M["Btok"]], [M["NBtok"]], out=m3("NBtok"), in_=m3("Btok"), mul=-1.0)
                            elif kind == "Kt":
                                self.I("vector", "tensor_copy", [p_], [M["Kttok"]], out=m3("Kttok"), in_=p3(p_))
                            else:
                                self.I("vector", "tensor_copy", [p_], [M["Vtok"]], out=m3("Vtok"), in_=p3(p_))
                                if d == 0:
                                    self.I("gpsimd", "tensor_copy", [M["Vtok"]], [vtk], out=vtk[:, gc0:gc0 + ncb, :], in_=m3("Vtok"))
                        if d == 0:
                            self.I("gpsimd", "tensor_tensor", [F["rs"], F["ks"]], [F["rk"]], out=F["rk"][:, 0:n], in0=F["rs"][:, 0:n], in1=F["ks"][:, 0:n], op=ALU.mult)
                            self.I("vector", "tensor_scalar_mul", [F["rk"], prm], [F["rk"]], out=F["rk"][:, 0:n], in0=F["rk"][:, 0:n], scalar1=prm[:, 16:17])
                            p_ = nextpg()
                            for c in range(ncb):
                                for hh in range(2):
                                    self.mm(p_, p_[HR[hh], c:c + 1], F["rk"], F["rk"][HR[hh], c * 64:(c + 1) * 64], ones, ones[HR[hh], 0:1])
                            self.I("vector", "tensor_copy", [p_], [rks], out=rks[:, gc0:gc0 + ncb], in_=p_[:, 0:ncb])
                            p_ = nextpg()
                            for c in range(ncb):
                                for hh in range(2):
                                    h = hp * 2 + hh
                                    self.mm(p_, p_[HR[hh], c * 64:(c + 1) * 64], F["sg"], F["sg"][HR[hh], c * 64:(c + 1) * 64], g2r, g2r[HR[hh], h * 64:(h + 1) * 64])
                            self.I("vector", "tensor_copy", [p_], [gat], out=gat[:, gc0:gc0 + ncb, :], in_=p3(p_))
                        su_, sl_, uu_ = (su, sl, uu) if d == 0 else (sl, su, ll)
                        p_ = nextpg()
                        gram(p_, F["Bt"], F["Kap"])
                        self.I("vector", "scalar_tensor_tensor", [p_, su_], [M["At0"]], out=m3("At0"), in0=p3(p_), scalar=-1.0, in1=bc(su_), op0=ALU.mult, op1=ALU.mult)
                        p_ = nextpg()
                        gram(p_, F["Kap"], F["Bt"])
                        self.I("vector", "scalar_tensor_tensor", [p_, sl_], [M["A0"]], out=m3("A0"), in0=p3(p_), scalar=-1.0, in1=bc(sl_), op0=ALU.mult, op1=ALU.mult)
                        p_ = nextpg()
                        gram(p_, F["Kt"], F["Kap"])
                        self.I("vector", "tensor_tensor", [p_, su_], [M["AkT"]], out=m3("AkT"), in0=p3(p_), in1=bc(su_), op=ALU.mult)
                        p_ = nextpg()
                        gram(p_, F["Kt"], F["Rt"])
                        self.I("vector", "tensor_tensor", [p_, uu_], [M["ArT"]], out=m3("ArT"), in0=p3(p_), in1=bc(uu_), op=ALU.mult)
                        p_ = nextpg()
                        gram(p_, F["Bt"], F["Rt"])
                        self.I("vector", "tensor_tensor", [p_, uu_], [M["AbT"]], out=m3("AbT"), in0=p3(p_), in1=bc(uu_), op=ALU.mult)
                        self.I("scalar", "mul", [M["AbT"]], [M["NAbT"]], out=m3("NAbT"), in_=m3("AbT"), mul=-1.0)
                        self.I("gpsimd", "tensor_tensor", [M["At0"], idb], [M["TT0"]], out=m3("TT0"), in0=m3("At0"), in1=bc(idb), op=ALU.add)
                        Ac, Atc, Tc = "A0", "At0", "TT0"
                        for j in range(5):
                            An, Atn, Tn = ("A1", "At1", "TT1") if j % 2 == 0 else ("A0", "At0", "TT0")
                            p_ = nextpg()
                            tmm(p_, 64, M[Atc], 64, M[Ac], 64)
                            self.I("vector", "tensor_copy", [p_], [M[An]], out=m3(An), in_=p3(p_))
                            if j < 4:
                                p_ = nextpg()
                                tmm(p_, 64, M[Ac], 64, M[Atc], 64)
                                self.I("scalar", "copy", [p_], [M[Atn]], out=m3(Atn), in_=p3(p_))
                            p_ = nextpg()
                            tmm(p_, 64, M[An], 64, M[Tc], 64)
                            self.I("vector", "tensor_tensor", [p_, M[Tc]], [M[Tn]], out=m3(Tn), in0=p3(p_), in1=m3(Tc), op=ALU.add)
                            Ac, Atc, Tc = An, Atn, Tn
                        p_ = nextpg()
                        tmm(p_, 64, M["AkT"], 64, M["Vtok"], 64)
                        self.I("vector", "tensor_copy", [p_], [X], out=X[:, 0:ncb, 64:128], in_=p3(p_))
                        for c in range(ncb):
                            for hh in range(2):
                                pw_ = pW[c // 4]
                                self.mm(pw_, pw_[HR[hh], c % 4, :], M[Tc], M[Tc][HR[hh], c, :], X, X[HR[hh], c, :])
                        for half in range((ncb + 3) // 4):
                            cc = min(4, ncb - half * 4)
                            self.I("vector" if half == 0 else "scalar", "tensor_copy" if half == 0 else "copy", [pW[half]], [W2],
                                   out=W2[:, half * 4:half * 4 + cc, :], in_=pW[half][:, 0:cc, :])
                        p_ = nextpg()
                        tmm(p_, 64, W2, 64, M["Btok"], 64)
                        self.I("vector", "scalar_tensor_tensor", [p_, idb], [M["G2"]], out=m3("G2"), in0=p3(p_), scalar=-1.0, in1=bc(idb), op0=ALU.mult, op1=ALU.add)
                        p_ = nextpg()
                        tmm2(p_, M["Kttok"], M["Vtok"], 0, M["NBtok"], W2, 64)
                        pos = 63 if d == 0 else 0
                        pC = F["Ep"][:, 0:n].rearrange("p (c t) -> p c t", t=64)[:, :, pos]
                        self.I("vector", "tensor_tensor", [p_, F["Ep"]], [M["Hp"]], out=m3("Hp"), in0=p3(p_), in1=pC.unsqueeze(2).to_broadcast([128, ncb, 64]), op=ALU.mult)
                        p_ = nextpg()
                        tmm(p_, 64, W2, 64, M["AbT"], 64)
                        self.I("vector", "scalar_tensor_tensor", [p_, F["Rt"]], [M["QpT"]], out=m3("QpT"), in0=p3(p_), scalar=-1.0, in1=v3(F["Rt"]), op0=ALU.mult, op1=ALU.add)
                        p_ = nextpg()
                        tmm2(p_, M["ArT"], M["Vtok"], 0, M["NAbT"], W2, 64)
                        self.I("scalar", "copy", [p_], [M["Yloc"]], out=m3("Yloc"), in_=p3(p_))
                        corder = range(ncb) if d == 0 else range(ncb - 1, -1, -1)
                        for c in corder:
                            S0 = ST[nst % 3]
                            S1 = ST[(nst + 1) % 3]
                            nst += 1
                            for hh in range(2):
                                self.mm(pZ, pZ[HR[hh], 0:64], M["G2"], M["G2"][HR[hh], c, :], S0, S0[HR[hh], :])
                            for hh in range(2):
                                self.mm(pY, pY[HR[hh], c, :], M["QpT"], M["QpT"][HR[hh], c, :], S0, S0[HR[hh], :])
                            self.I("vector", "scalar_tensor_tensor", [pZ, F["Ep"], M["Hp"]], [S1], out=S1[:], in0=pZ[:, 0:64], scalar=F["Ep"][:, c * 64 + pos:c * 64 + pos + 1],
                                   in1=M["Hp"][:, c, :], op0=ALU.mult, op1=ALU.add)
                        if d == 0:
                            self.I("vector", "tensor_tensor", [pY, M["Yloc"]], [yacc], out=yacc[:, gc0:gc0 + ncb, :], in0=pY[:, 0:ncb, :], in1=m3("Yloc"), op=ALU.add)
                        else:
                            self.I("vector", "tensor_tensor", [pY, M["Yloc"]], [M["ytmp"]], out=m3("ytmp"), in0=pY[:, 0:ncb, :], in1=m3("Yloc"), op=ALU.add)
                            self.I("gpsimd", "tensor_tensor", [yacc, M["ytmp"]], [yacc], out=yacc[:, gc0:gc0 + ncb, :], in0=yacc[:, gc0:gc0 + ncb, :], in1=m3("ytmp"), op=ALU.add)
                yb = lambda ap: ap.unsqueeze(2).to_broadcast([128, NCA, 64])
                self.I("vector", "tensor_reduce", [yacc], [red], out=red[:], in_=yacc[:], axis=AX.X, op=ALU.add)
                self.I("vector", "tensor_scalar_mul", [red], [red], out=red[:], in0=red[:], scalar1=-1.0 / 64.0)
                self.I("vector", "tensor_tensor", [yacc, red], [yacc], out=yacc[:], in0=yacc[:], in1=yb(red[:]), op=ALU.add)
                ysq = yacc
                self.I("gpsimd", "tensor_tensor", [yacc], [vtk if False else self._tmp_big(es)], out=self._tmp_big(es)[:], in0=yacc[:], in1=yacc[:], op=ALU.mult)
                tb = self._tmp_big(es)
                self.I("vector", "tensor_reduce", [tb], [red2], out=red2[:], in_=tb[:], axis=AX.X, op=ALU.add)
                self.rsqrt(red2, red2[:], red2, red2[:], 64e-5, scale=1.0 / 64.0)
                self.I("vector", "tensor_tensor", [yacc, red2], [yacc], out=yacc[:], in0=yacc[:], in1=yb(red2[:]), op=ALU.mult)
                self.I("gpsimd", "tensor_tensor", [yacc, gnb], [yacc], out=yacc[:], in0=yacc[:], in1=gnb[:].unsqueeze(1).to_broadcast([128, NCA, 64]), op=ALU.mult)
                self.I("vector", "tensor_tensor", [vtk, rks], [tb], out=tb[:], in0=vtk[:], in1=yb(rks[:]), op=ALU.mult)
                self.I("gpsimd", "tensor_tensor", [yacc, tb], [yacc], out=yacc[:], in0=yacc[:], in1=tb[:], op=ALU.add)
                self.I("vector", "tensor_tensor", [yacc, gat], [yacc], out=yacc[:], in0=yacc[:], in1=gat[:], op=ALU.mult)
                for g8 in range((NCA + 7) // 8):
                    p_ = nextpg()
                    cc = min(8, NCA - g8 * 8)
                    for c in range(cc):
                        gc = g8 * 8 + c
                        for hh in range(2):
                            self.mm(p_, p_[HR[hh], c * 64:(c + 1) * 64], yacc, yacc[HR[hh], gc, :], self.ident, self.ident[HR[hh], HR[hh]])
                    self.I("scalar", "copy", [p_], [ob], out=ob[:, g8 * 512:g8 * 512 + cc * 64], in_=p_[:, 0:cc * 64])
                self.dma("sync", self.OT[2 + hp], ob[:], reads=[ob])
            self.S.barrier()

    def _tmp_big(self, es):
        if getattr(self, "_tb_es", None) is not es:
            self._tb = self.sb(es, "r_tb", [128, T // 64, 64])
            self._tb_es = es
        return self._tb


def build_program(kinds=None, plan=None, layers=(0, 1, 2, 3)):
    b = Builder(kinds=kinds, layers=layers)
    b.declare()
    with b.es:
        b.globals_()
        plan = plan or ["full"]
        if plan == ["full"]:
            b.phase_mod()
            b.phase_x0()
            for l in layers:
                b.phase1(l)
                b.weight_prep(l)
                b.mixer_lru(l)
                b.mixer_mlstm(l)
                b.mixer_mla(l)
                b.mixer_rwkv(l)
                b.phase3(l)
        else:
            for item in plan:
                name, *args = item if isinstance(item, (tuple, list)) else (item,)
                getattr(b, name)(*args)
        b.S.finish_on("sync", b.outdeps)
        b.S.emit()
    return b.nc


def make_consts():
    c = {}
    c["ident"] = np.eye(128, dtype=np.float32)
    c["trif"] = np.triu(np.ones((128, 128), np.float32))
    c["trib"] = np.tril(np.ones((128, 128), np.float32))
    c["ones"] = np.ones((128, 128), np.float32)
    e64 = np.eye(64, dtype=np.float32)
    o64 = np.ones((64, 64), np.float32)
    c["m_su"] = np.concatenate([np.triu(o64, 1)] * 2, 0)
    c["m_sl"] = np.concatenate([np.tril(o64, -1)] * 2, 0)
    c["m_u"] = np.concatenate([np.triu(o64, 0)] * 2, 0)
    c["m_l"] = np.concatenate([np.tril(o64, 0)] * 2, 0)
    c["idb"] = np.concatenate([e64, e64], 0)
    c["bones"] = np.kron(np.eye(2, dtype=np.float32), o64)
    rm = np.ones((128, 512), np.float32)
    rm[:, ::64] = 0.0
    c["rmask"] = rm
    inv = (1.0 / (10000.0 ** (np.arange(8, dtype=np.float32) / 8.0))).astype(np.float32)
    tt = np.arange(4096)
    ang_r = ((tt // 64).astype(np.float32)[:, None] * inv).astype(np.float32)
    ang_c = ((tt % 64).astype(np.float32)[:, None] * inv).astype(np.float32)
    c["cosT"] = np.ascontiguousarray(np.concatenate([np.cos(ang_r), np.cos(ang_r), np.cos(ang_c), np.cos(ang_c)], 1).T.astype(np.float32))
    c["sinT"] = np.ascontiguousarray(np.concatenate([np.sin(ang_r), np.sin(ang_r), np.sin(ang_c), np.sin(ang_c)], 1).T.astype(np.float32))
    return {"k_" + k: v for k, v in c.items()}


_PROG = {}


def kernel(**inputs):
    consts = make_consts()
    if "full" not in _PROG:
        _PROG["full"] = build_program()
    nc = _PROG["full"]
    in_maps = []
    for b in range(4):
        m = {"x": np.ascontiguousarray(inputs["x"][b]), "ctx": np.ascontiguousarray(inputs["ctx"][b]),
             "c": np.ascontiguousarray(inputs["c"][b]), "c_ctx": np.ascontiguousarray(inputs["c_ctx"])}
        for k in WEIGHT_SHAPES:
            m[k] = np.ascontiguousarray(inputs[k])
        m.update(consts)
        in_maps.append(m)
    res = run_bass_kernel_spmd(nc, in_maps, core_ids=list(range(4)))
    return np.stack([r["out"] for r in res.results], 0).astype(np.float32)
```

```python
import contextlib
import numpy as np
import concourse.bass as bass
import concourse.mybir as mybir
from concourse.bass_utils import run_bass_kernel_spmd

F32 = mybir.dt.float32
BF16 = mybir.dt.bfloat16
AF = mybir.ActivationFunctionType
ALU = mybir.AluOpType
AX = mybir.AxisListType

QUEUES = ("sync", "scalar", "gpsimd")
EPOCH = 24000
NDMASEM = 12


class Dep:
    __slots__ = ("writers", "readers")

    def __init__(self):
        self.writers = []
        self.readers = []


class Op:
    __slots__ = ("eng", "fn", "waits", "is_dma", "signal", "sem", "val")

    def __init__(self, eng, fn, is_dma):
        self.eng = eng
        self.fn = fn
        self.is_dma = is_dma
        self.waits = []
        self.signal = False
        self.sem = None
        self.val = None


def _prune(readers):
    out = []
    last = {}
    for r in readers:
        if r.is_dma:
            out.append(r)
        else:
            last[r.eng] = r
    return out + list(last.values())


class Sched:
    def __init__(self, nc):
        self.nc = nc
        self.streams = {e: [] for e in ("tensor", "vector", "scalar", "gpsimd", "sync")}
        self.dma_hist = {q: [] for q in QUEUES}
        self.final_waits = []

    def _add(self, eng, fn, reads, writes, is_dma):
        op = Op(eng, fn, is_dma)
        waits = []
        for d in reads:
            waits.extend(d.writers)
        for d in writes:
            waits.extend(d.writers)
            waits.extend(d.readers)
        seen = set()
        for w in waits:
            if id(w) in seen or w is op:
                continue
            seen.add(id(w))
            if (not w.is_dma) and (not is_dma) and w.eng == eng == "tensor":
                continue
            op.waits.append(w)
        if is_dma:
            hist = self.dma_hist[eng]
            if len(hist) >= NDMASEM:
                op.waits.append(hist[len(hist) - NDMASEM])
            hist.append(op)
        for d in reads:
            d.readers.append(op)
            if len(d.readers) > 48:
                d.readers = _prune(d.readers)
        for d in writes:
            d.writers = [op]
            d.readers = []
        self.streams[eng].append(op)
        return op

    def op(self, eng, fn, reads=(), writes=()):
        return self._add(eng, fn, reads, writes, False)

    def dma(self, q, out, in_, reads=(), writes=(), **kw):
        return self._add(q, lambda e: e.dma_start(out=out, in_=in_, **kw), reads, writes, True)

    def barrier(self):
        tails = []
        for eng in ("tensor", "vector", "scalar", "gpsimd"):
            for op in reversed(self.streams[eng]):
                if not op.is_dma and op.fn is not None:
                    tails.append(op)
                    break
        for q in QUEUES:
            tails.extend(self.dma_hist[q][-NDMASEM:])
        for eng in ("tensor", "vector", "scalar", "gpsimd", "sync"):
            op = Op(eng, None, False)
            op.waits = [w for w in tails]
            self.streams[eng].append(op)

    def finish_on(self, eng, deps):
        ops = []
        for d in deps:
            ops.extend(d.writers)
        self.final_waits.append((eng, ops))

    def emit(self):
        nc = self.nc
        for eng, ops in self.streams.items():
            for op in ops:
                for w in op.waits:
                    w.signal = True
        for eng, ops in self.final_waits:
            for w in ops:
                w.signal = True
        need = []
        seen = set()
        for eng, ops in self.streams.items():
            cnt = 0
            dcnt = 0
            for op in ops:
                if op.is_dma:
                    op.sem = ("dma", eng, dcnt % NDMASEM)
                    op.val = 16 * (dcnt // NDMASEM + 1)
                    dcnt += 1
                    op.signal = True
                elif op.signal:
                    op.sem = ("eng", eng, cnt // EPOCH)
                    op.val = cnt % EPOCH + 1
                    cnt += 1
                if op.signal and op.sem not in seen:
                    seen.add(op.sem)
                    need.append(op.sem)
        with contextlib.ExitStack() as es:
            semh = {}
            for i, key in enumerate(need):
                semh[key] = es.enter_context(nc.semaphore("s%d" % i))
            block = es.enter_context(nc.Block())

            def gen(engname):
                ops = self.streams[engname]

                def body(e):
                    known = {}

                    def wait(w):
                        if known.get(w.sem, 0) >= w.val:
                            return
                        e.wait_ge(semh[w.sem], w.val)
                        known[w.sem] = w.val
                    for op in ops:
                        for w in op.waits:
                            wait(w)
                        if op.fn is None:
                            continue
                        ins = op.fn(e)
                        if op.signal:
                            ins.then_inc(semh[op.sem], 16 if op.is_dma else 1)
                    for eng2, wops in self.final_waits:
                        if eng2 == engname:
                            for w in wops:
                                wait(w)
                return body

            for engname in ("sync", "tensor", "vector", "scalar", "gpsimd"):
                if not self.streams[engname] and not any(e == engname for e, _ in self.final_waits):
                    continue
                getattr(block, engname)(gen(engname))


D = 1024
L = 4
TC = 256
TL = 4096
T = TC + TL
NFC = 8
DIN = 6704
ALPHA = (2 * L) ** 0.25
LN_EPS = 1e-5
RMS_EPS = 1e-6
NUF = 18
NTM = 784
C_MQ, C_MK, C_MV, C_MO, C_MIF = 4096, 4352, 4608, 4864, 5120
C_RR, C_RK, C_RV, C_RWD, C_RAD, C_RGD = 5136, 5392, 5648, 5904, 5968, 6032
C_ACQ, C_ACKV, C_AKR = 6096, 6288, 6416
C_LRU = 6448
UF_CHUNKS = [
    (0, C_MQ, 128), (1, C_MQ + 128, 128), (2, C_MK, 128), (3, C_MK + 128, 128),
    (4, C_RR, 128), (5, C_RR + 128, 128), (6, C_RK, 128), (7, C_RK + 128, 128),
    (8, C_RV, 128), (9, C_RV + 128, 128), (10, C_RWD, 128), (11, C_RGD, 64),
    (12, C_ACQ, 128), (13, C_ACQ + 128, 64), (14, C_ACKV, 128), (15, C_AKR, 32),
    (16, C_LRU, 128), (17, C_LRU + 128, 128),
]
BLOCKS = [(0, 256)] + [(256 + 512 * i, 512) for i in range(8)]


WEIGHT_SHAPES = {
    "w_mod": [4, 1024, 6144], "b_mod": [4, 6144], "w_in": [4, 1024, 6704], "mlstm_gate_b": [4, 2, 2, 4],
    "mlstm_norm_g": [4, 256], "rwkv_mu": [4, 960], "rwkv_w0": [4, 2, 256], "rwkv_w2": [4, 2, 32, 256],
    "rwkv_a0": [4, 2, 256], "rwkv_a2": [4, 2, 32, 256], "rwkv_g2": [4, 64, 256], "rwkv_kk": [4, 256],
    "rwkv_ka": [4, 256], "rwkv_rk": [4, 4, 64], "rwkv_gn_g": [4, 256], "mla_qn_g": [4, 192], "mla_kvn_g": [4, 128],
    "mla_wuq": [4, 192, 384], "mla_wuk": [4, 128, 256], "mla_wuv": [4, 128, 256], "lru_conv_w": [4, 4, 256],
    "lru_conv_b": [4, 256], "lru_wa": [4, 2, 4, 64, 64], "lru_ba": [4, 2, 256], "lru_wx": [4, 2, 4, 64, 64],
    "lru_bx": [4, 2, 256], "lru_lambda": [4, 2, 256], "w_branch": [4, 4, 256, 1024], "w_out": [4, 1024, 1024],
    "ln1_g": [4, 1024], "ln1_b": [4, 1024], "w_ff1": [4, 1024, 4096], "w_ff2": [4, 4096, 1024], "ln2_g": [4, 1024],
    "ln2_b": [4, 1024],
}
CONST_SHAPES = {"ident": [128, 128], "trif": [128, 128], "trib": [128, 128], "ones": [128, 128], "cosT": [32, 4096], "sinT": [32, 4096], "m_su": [128, 64], "m_sl": [128, 64], "m_u": [128, 64], "m_l": [128, 64], "idb": [128, 64], "bones": [128, 128], "rmask": [128, 512]}

class Tl:
    __slots__ = ("t", "d")

    def __init__(self, t):
        self.t = t
        self.d = Dep()

    def __getitem__(self, k):
        return self.t[k]


class _HoutAll:
    def __init__(self, tl, deps):
        self.t = tl.t
        self.deps = deps

    def __getitem__(self, k):
        return self.t[k]


class Builder:
    def __init__(self, kinds=None, layers=(0, 1, 2, 3)):
        self.nc = bass.Bass("TRN2", target_bir_lowering=False)
        self.S = Sched(self.nc)
        self.kinds = kinds or {}
        self.layers = layers
        self.dr = {}
        self.es = contextlib.ExitStack()
        self.es.enter_context(self.nc.allow_non_contiguous_dma(reason="small strided parameter loads"))
        self.outdeps = []

    def dram(self, name, shape, dtype, kind=None):
        k = self.kinds.get(name, kind or "Internal")
        ap = self.nc.dram_tensor(name, list(shape), dtype, kind=k).ap()
        self.dr[name] = ap
        return ap

    def sb(self, es, name, shape, dtype=F32):
        self.uid = getattr(self, "uid", 0) + 1
        return Tl(es.enter_context(self.nc.sbuf_tensor("%s_%d" % (name, self.uid), list(shape), dtype)))

    def ps(self, es, name, shape, dtype=F32):
        self.uid = getattr(self, "uid", 0) + 1
        return Tl(es.enter_context(self.nc.psum_tensor("%s_%d" % (name, self.uid), list(shape), dtype)))

    def op(self, eng, fn, reads=(), writes=()):
        return self.S.op(eng, fn, [r.d if isinstance(r, Tl) else r for r in reads],
                         [w.d if isinstance(w, Tl) else w for w in writes])

    def dma(self, q, out, in_, reads=(), writes=(), **kw):
        return self.S.dma(q, out, in_, [r.d if isinstance(r, Tl) else r for r in reads],
                          [w.d if isinstance(w, Tl) else w for w in writes], **kw)

    def declare(self):
        d = self.dram
        EI = "ExternalInput"
        self.x = d("x", [TL, D], F32, EI)
        self.ctx = d("ctx", [TC, D], F32, EI)
        self.c = d("c", [D], F32, EI)
        self.c_ctx = d("c_ctx", [D], F32, EI)
        self.w = {}
        for name, shape in WEIGHT_SHAPES.items():
            self.w[name] = d(name, shape, F32, EI)
        self.cst = {}
        for name, shape in CONST_SHAPES.items():
            self.cst[name] = d("k_" + name, shape, F32, EI)
        self.out = d("out", [TL, D], F32, "ExternalOutput")
        self.xresT = d("xresT", [NFC, 128, T], F32)
        self.xres_src = d("xresT_in", [NFC, 128, T], F32) if "xresT_in" in self.kinds else self.xresT
        self.hT = d("hT", [NFC, 128, T], BF16)
        self.G = d("G", [32, 128, T], BF16)
        self.UF = d("UF", [NUF, 128, T], F32)
        self.UT = d("UT", [T, NTM], F32)
        self.OT = d("OT", [8, 128, T], BF16)
        self.Wb1 = d("Wb1", [32, 128, 8, 128], BF16)
        self.Wb2 = d("Wb2", [8, 128, 32, 128], BF16)

    def globals_(self):
        es = self.es
        self.modT = self.sb(es, "modT", [128, L, 48, 2])
        self.gA = self.sb(es, "gA", [128, L, 4, 8, 2])
        self.lnp = self.sb(es, "lnp", [128, L, 4, 8])
        self.ident = self.sb(es, "ident", [128, 128])
        self.onesb = self.sb(es, "onesb", [128, 128], BF16)
        self.one_sc = self.sb(es, "one_sc", [128, L, 8, 2])
        self.gsc = self.sb(es, "gsc", [128, L, 2, 8, 2])
        self.epsc = {}
        for i, v in enumerate((LN_EPS, RMS_EPS, 64e-5, 0.0, LN_EPS / (ALPHA * ALPHA))):
            t = self.sb(es, "epsc%d" % i, [128, 1])
            self.op("vector", (lambda e, t=t, v=v: e.memset(t[:], v)), writes=[t])
            self.epsc[v] = t
        self.dma("sync", self.ident[:], self.cst["ident"][:, :], writes=[self.ident])
        self.op("vector", lambda e: e.memset(self.onesb[:], 1.0 / 1024.0), writes=[self.onesb])

    def I(self, eng, meth, reads, writes, *args, **kw):
        return self.op(eng, lambda e: getattr(e, meth)(*args, **kw), reads, writes)

    def rsqrt(self, in_tl, in_ap, out_tl, out_ap, eps, scale=1.0):
        self.I("scalar", "activation", [in_tl], [out_tl], out=out_ap, in_=in_ap, func=AF.Sqrt, bias=self.epsc[eps][:in_ap.shape[0], 0:1], scale=scale)
        self.I("vector", "reciprocal", [out_tl], [out_tl], out=out_ap, in_=out_ap)

    def mm(self, out_tl, out_ap, l_tl, l_ap, r_tl, r_ap, start=True, stop=True, extra=()):
        return self.op("tensor", lambda e: e.matmul(out_ap, lhsT=l_ap, rhs=r_ap, start=start, stop=stop),
                       [l_tl, r_tl] + list(extra), [out_tl])

    def phase_mod(self):
        with contextlib.ExitStack() as es:
            craw = self.sb(es, "craw", [128, 8, 2])
            sT = self.sb(es, "sT", [128, 8, 2])
            sg = self.sb(es, "sg", [128, 8, 2])
            bm = self.sb(es, "bm", [128, L, 48])
            wm = [self.sb(es, "wm%d" % i, [128, 8, 512], BF16) for i in range(3)]
            sTb = self.sb(es, "sTb", [128, 8, 2], BF16)
            pm = self.ps(es, "pm", [128, L, 48, 2])
            self.dma("sync", craw[:, :, 0], self.c.rearrange("(k p) -> p k", p=128), writes=[craw])
            self.dma("sync", craw[:, :, 1], self.c_ctx.rearrange("(k p) -> p k", p=128), writes=[craw])
            self.dma("sync", bm[:], self.w["b_mod"].rearrange("l (j p) -> p l j", p=128), writes=[bm])
            for i, nm in enumerate(("ln1_g", "ln1_b", "ln2_g", "ln2_b")):
                for l in range(L):
                    self.dma("sync", self.lnp[:, l, i, :], self.w[nm][l].rearrange("(j p) -> p j", p=128), writes=[self.lnp])
            self.I("scalar", "activation", [craw], [sg], out=sg[:], in_=craw[:], func=AF.Sigmoid)
            self.I("vector", "tensor_mul", [craw, sg], [sT], out=sT[:], in0=craw[:], in1=sg[:])
            self.I("vector", "tensor_copy", [sT], [sTb], out=sTb[:], in_=sT[:])
            n = 0
            for l in range(L):
                for cb in range(12):
                    w_ = wm[n % 3]
                    n += 1
                    self.dma("gpsimd", w_[:], self.w["w_mod"][l, :, cb * 512:(cb + 1) * 512].rearrange("(k p) c -> p k c", p=128),
                             writes=[w_])
                    for m in range(4):
                        j = cb * 4 + m
                        for kc in range(8):
                            self.mm(pm, pm[:, l, j, :], w_, w_[:, kc, m * 128:(m + 1) * 128], sTb, sTb[:, kc, :],
                                    start=(kc == 0), stop=(kc == 7))
            for s in range(2):
                self.I("vector", "tensor_tensor", [pm, bm], [self.modT], out=self.modT[:, :, :, s], in0=pm[:, :, :, s], in1=bm[:], op=ALU.add)
            self.I("vector", "tensor_scalar_add", [self.modT], [self.one_sc], out=self.one_sc[:], in0=self.modT[:, :, 8:16, :], scalar1=1.0)
            for gi, c0 in ((0, 16), (1, 40)):
                self.I("vector", "tensor_scalar_mul", [self.modT], [self.gsc], out=self.gsc[:, :, gi, :, :], in0=self.modT[:, :, c0:c0 + 8, :], scalar1=1.0 / ALPHA)
            for l in range(L):
                for s in range(2):
                    self._fuse(l, s, 0, self.lnp[:, l, 0, :], self.lnp[:, l, 1, :], l, 3, 4)
                    if l + 1 < L:
                        self._fuse(l, s, 2, self.lnp[:, l, 2, :], self.lnp[:, l, 3, :], l + 1, 0, 1)
            self.S.barrier()

    def _fuse(self, l, s, slot, g, b, lm, ish, isc):
        m = self.modT
        gA = self.gA
        sc = m[:, lm, isc * 8:(isc + 1) * 8, s]
        sh = m[:, lm, ish * 8:(ish + 1) * 8, s]
        self.I("vector", "scalar_tensor_tensor", [m, self.lnp], [gA], out=gA[:, l, slot, :, s], in0=sc, scalar=1.0, in1=g, op0=ALU.add, op1=ALU.mult)
        self.I("vector", "scalar_tensor_tensor", [m, self.lnp], [gA], out=gA[:, l, slot + 1, :, s], in0=sc, scalar=1.0, in1=b, op0=ALU.add, op1=ALU.mult)
        self.I("vector", "tensor_add", [m, gA], [gA], out=gA[:, l, slot + 1, :, s], in0=gA[:, l, slot + 1, :, s], in1=sh)

    def phase_x0(self):
        with contextlib.ExitStack() as es:
            xin = [self.sb(es, "xin%d" % i, [128, 4, D]) for i in range(2)]
            xo = [self.sb(es, "xo%d" % i, [128, 8, 512]) for i in range(2)]
            ho = [self.sb(es, "ho%d" % i, [128, 8, 512], BF16) for i in range(2)]
            pt = [self.ps(es, "pt%d" % i, [128, 512]) for i in range(4)]
            npt = 0
            m = self.modT
            for bi, (t0, n) in enumerate(BLOCKS):
                s = 1 if t0 < TC else 0
                xi, xo_, ho_ = xin[bi % 2], xo[bi % 2], ho[bi % 2]
                nt = n // 128
                src = self.ctx if s == 1 else self.x
                r0 = t0 if s == 1 else t0 - TC
                self.dma("sync", xi[:, 0:nt, :], src[r0:r0 + n, :].rearrange("(a p) f -> p a f", p=128), writes=[xi])
                for fc in range(8):
                    p_ = pt[npt % 4]
                    npt += 1
                    for a in range(nt):
                        self.mm(p_, p_[:, a * 128:(a + 1) * 128], xi, xi[:, a, fc * 128:(fc + 1) * 128], self.ident, self.ident[:])
                    self.I("scalar", "copy", [p_], [xo_], out=xo_[:, fc, 0:n], in_=p_[:, 0:n])
                    self.I("vector", "tensor_scalar", [xo_, m, self.one_sc], [ho_], out=ho_[:, fc, 0:n], in0=xo_[:, fc, 0:n],
                           scalar1=self.one_sc[:, 0, fc, s:s + 1], scalar2=m[:, 0, fc, s:s + 1], op0=ALU.mult, op1=ALU.add)
                self.dma("sync", self.xresT[:, :, t0:t0 + n].rearrange("f p t -> p f t"), xo_[:, :, 0:n], reads=[xo_])
                self.dma("sync", self.hT[:, :, t0:t0 + n].rearrange("f p t -> p f t"), ho_[:, :, 0:n], reads=[ho_])
            self.S.barrier()

    def phase1(self, l):
        w_in = self.w["w_in"]
        with contextlib.ExitStack() as es:
            hT = self.sb(es, "hTr", [128, 8, T], BF16)
            slabs = [self.sb(es, "slab%d" % i, [128, 8, 128], BF16) for i in range(3)]
            stf = [self.sb(es, "stf%d" % i, [128, T]) for i in range(2)]
            stb = [self.sb(es, "stb%d" % i, [128, T], BF16) for i in range(2)]
            tsl = self.sb(es, "tsl", [128, 8, NTM], BF16)
            tst = [self.sb(es, "tst%d" % i, [128, NTM]) for i in range(2)]
            pp = [self.ps(es, "pp%d" % i, [128, 512]) for i in range(6)]
            for fc in range(8):
                self.dma("sync", hT[:, fc, :], self.hT[fc], writes=[hT])
            for (d0, c0, ncol) in ((0, C_MV, 256), (256, C_MO, 256), (512, C_MIF, 16), (528, C_MK, 256)):
                self.dma("gpsimd", tsl[:, :, d0:d0 + ncol], w_in[l, :, c0:c0 + ncol].rearrange("(k p) c -> p k c", p=128), writes=[tsl])
            chunks = [("g", j, j * 128, 128) for j in range(32)] + [("u", i, c0, nc_) for (i, c0, nc_) in UF_CHUNKS]
            npp = 0
            nf = nb = 0
            for ci, (kind, idx, c0, ncol) in enumerate(chunks):
                sl = slabs[ci % 3]
                self.dma("gpsimd", sl[:, :, 0:ncol], w_in[l, :, c0:c0 + ncol].rearrange("(k p) c -> p k c", p=128), writes=[sl])
                mrows = ncol
                if kind == "u" and idx == 15:
                    for (dst, srcc, sgn) in ((32, 8, -1.0), (40, 0, 1.0), (48, 24, -1.0), (56, 16, 1.0)):
                        self.I("vector", "tensor_scalar_mul", [sl], [sl], out=sl[:, :, dst:dst + 8], in0=sl[:, :, srcc:srcc + 8], scalar1=sgn)
                    mrows = 64
                if kind == "g":
                    st = stb[nb % 2]
                    nb += 1
                else:
                    st = stf[nf % 2]
                    nf += 1
                for bi, (t0, n) in enumerate(BLOCKS):
                    p_ = pp[npp % 6]
                    npp += 1
                    for kc in range(8):
                        self.mm(p_, p_[0:mrows, 0:n], sl, sl[:, kc, 0:mrows], hT, hT[:, kc, t0:t0 + n], start=(kc == 0), stop=(kc == 7))
                    if kind == "g":
                        self.I("scalar", "activation", [p_], [st], out=st[:, t0:t0 + n], in_=p_[:, 0:n], func=AF.Sigmoid)
                    else:
                        self.I("vector", "tensor_copy", [p_], [st], out=st[0:mrows, t0:t0 + n], in_=p_[0:mrows, 0:n])
                if kind == "g":
                    self.dma("sync", self.G[idx], st[:], reads=[st])
                else:
                    self.dma("sync", self.UF[idx, 0:mrows, :], st[0:mrows, :], reads=[st])
            for tt in range(T // 128):
                p1 = pp[npp % 6]
                p2 = pp[(npp + 1) % 6]
                npp += 2
                to = tst[tt % 2]
                for kc in range(8):
                    self.mm(p1, p1[:, 0:512], hT, hT[:, kc, tt * 128:(tt + 1) * 128], tsl, tsl[:, kc, 0:512], start=(kc == 0), stop=(kc == 7))
                for kc in range(8):
                    self.mm(p2, p2[:, 0:272], hT, hT[:, kc, tt * 128:(tt + 1) * 128], tsl, tsl[:, kc, 512:784], start=(kc == 0), stop=(kc == 7))
                self.I("vector", "tensor_copy", [p1], [to], out=to[:, 0:512], in_=p1[:, 0:512])
                self.I("scalar", "copy", [p2], [to], out=to[:, 512:784], in_=p2[:, 0:272])
                self.dma("sync", self.UT[tt * 128:(tt + 1) * 128, :], to[:], reads=[to])
            self.S.barrier()

    def barrier(self):
        self.S.barrier()

    def weight_prep(self, l):
        w1 = self.w["w_ff1"][l].rearrange("(k p) (j c) -> j p k c", p=128, c=128)
        w2 = self.w["w_ff2"][l].rearrange("(k p) (j c) -> j p k c", p=128, c=128)
        for j in range(32):
            self.dma("gpsimd", self.Wb1[j], w1[j])
        for c in range(8):
            for hh in range(2):
                self.dma("gpsimd", self.Wb2[c, :, hh * 16:(hh + 1) * 16, :], w2[c, :, hh * 16:(hh + 1) * 16, :])

    def ln_stats_chunk(self, Xtl, Xd, fc, n, xb, xbd, sq, sqd, xoff, soff):
        self.op("scalar", lambda e: e.activation(out=sq[:, soff + fc, 0:n], in_=Xtl[:, fc, 0:n], func=AF.Square), [Xd[fc]], [sqd[soff + fc]])
        self.op("scalar", lambda e: e.copy(out=xb[:, xoff + fc, 0:n], in_=Xtl[:, fc, 0:n]), [Xd[fc]], [xbd[xoff + fc]])

    def ln_stats_mm(self, fc, n, xb, xbd, sq, sqd, xoff, soff, pmean, pmsq):
        self.op("tensor", lambda e: e.matmul(pmean[:, 0:n], lhsT=self.onesb[:], rhs=xb[:, xoff + fc, 0:n], start=(fc == 0), stop=(fc == 7)),
                [self.onesb.d, xbd[xoff + fc]], [pmean.d])
        self.op("tensor", lambda e: e.matmul(pmsq[:, 0:n], lhsT=self.onesb[:], rhs=sq[:, soff + fc, 0:n], start=(fc == 0), stop=(fc == 7)),
                [self.onesb.d, sqd[soff + fc]], [pmsq.d])

    def ln_finish(self, X, Xd, n, g_ap, b_ap, gA_ap, bA_ap, Hout, Hdeps, pmean, pmsq, st, eps):
        rstd, nmr, tmp, mean = st
        self.I("vector", "tensor_copy", [pmean], [mean], out=mean[:, 0:n], in_=pmean[:, 0:n])
        self.I("vector", "tensor_tensor", [mean], [tmp], out=tmp[:, 0:n], in0=mean[:, 0:n], in1=mean[:, 0:n], op=ALU.mult)
        self.I("vector", "tensor_tensor", [pmsq, tmp], [tmp], out=tmp[:, 0:n], in0=pmsq[:, 0:n], in1=tmp[:, 0:n], op=ALU.subtract)
        self.rsqrt(tmp, tmp[:, 0:n], rstd, rstd[:, 0:n], eps)
        self.I("vector", "scalar_tensor_tensor", [mean, rstd], [nmr], out=nmr[:, 0:n], in0=mean[:, 0:n], scalar=-1.0, in1=rstd[:, 0:n], op0=ALU.mult, op1=ALU.mult)
        def stage_a(fc):
            xs = X[:, fc, 0:n]
            self.op("vector", (lambda e, xs=xs: e.tensor_tensor(out=xs, in0=xs, in1=rstd[:, 0:n], op=ALU.mult)), [Xd[fc], rstd.d], [Xd[fc]])
            self.op("gpsimd", (lambda e, xs=xs: e.tensor_tensor(out=xs, in0=xs, in1=nmr[:, 0:n], op=ALU.add)), [Xd[fc], nmr.d], [Xd[fc]])

        def stage_b(fc):
            xs = X[:, fc, 0:n]
            if Hout is not None:
                self.op("vector", (lambda e, xs=xs, fc=fc: e.tensor_scalar(out=Hout[:, fc, 0:n], in0=xs, scalar1=gA_ap[:, fc:fc + 1], scalar2=bA_ap[:, fc:fc + 1],
                                                                            op0=ALU.mult, op1=ALU.add)), [Xd[fc], self.gA.d], [Hdeps[fc]])
            self.op("scalar", (lambda e, xs=xs, fc=fc: e.activation(out=xs, in_=xs, func=AF.Identity, scale=g_ap[:, fc:fc + 1], bias=b_ap[:, fc:fc + 1])),
                    [Xd[fc], self.lnp.d], [Xd[fc]])
        stage_a(0)
        for fc in range(8):
            if fc + 1 < 8:
                stage_a(fc + 1)
            stage_b(fc)

    def phase3(self, l):
        last = (l == L - 1)
        m = self.modT
        gs = self.gsc
        EPS2 = LN_EPS / (ALPHA * ALPHA)
        with contextlib.ExitStack() as es:
            Pw = self.sb(es, "Pw", [128, 4, 2, D], BF16)
            Wo = self.sb(es, "Wo", [128, 8, D], BF16)
            w1s = [self.sb(es, "w1s%d" % i, [128, 8, 128], BF16) for i in range(3)]
            w2s = [self.sb(es, "w2s%d" % i, [128, 32, 128], BF16) for i in range(2)]
            gt = [self.sb(es, "gt%d" % i, [128, 4, 512], BF16) for i in range(2)]
            oT = [self.sb(es, "oT%d" % i, [128, 8, 512], BF16) for i in range(2)]
            Abuf = [self.sb(es, "bA%d" % i, [128, 8, 512]) for i in range(1 if last else 2)]
            Adeps = [[Dep() for _ in range(8)] for _ in range(1 if last else 2)]
            if last:
                Abuf, Adeps = [Abuf[0], Abuf[0]], [Adeps[0], Adeps[0]]
            B = self.sb(es, "bB", [128, 8, 512])
            Bd = [Dep() for _ in range(8)]
            E = self.sb(es, "bE", [128, 8, 512], BF16)
            Ed = [Dep() for _ in range(8)]
            Fb = self.sb(es, "bF", [128, 8, 512], BF16) if not last else None
            Fd = [Dep() for _ in range(8)]
            if last:
                ot = [self.sb(es, "ot%d" % i, [128, D]) for i in range(2)]
                Fb = self.sb(es, "bF", [128, 8, 512], BF16)
            hid = self.sb(es, "hid", [128, 32, 512], BF16)
            Hd = [Dep() for _ in range(32)]
            ysb = [self.sb(es, "ysb%d" % i, [128, 4, 512], BF16) for i in range(2)]
            tq = [self.sb(es, "tq%d" % i, [128, 4, 512], BF16) for i in range(2)]
            st = [self.sb(es, "st%d" % i, [128, 512]) for i in range(4)]
            py = [self.ps(es, "py%d" % i, [128, 512]) for i in range(4)]
            pa = [self.ps(es, "pa%d" % i, [128, 512]) for i in range(2)]
            pmean = self.ps(es, "pmean", [128, 512])
            pmsq = self.ps(es, "pmsq", [128, 512])
            self.dma("gpsimd", Pw[:], self.w["w_branch"][l].rearrange("n (mc p) d -> p n mc d", p=128), writes=[Pw])
            self.dma("gpsimd", Wo[:], self.w["w_out"][l].rearrange("(k p) d -> p k d", p=128), writes=[Wo])
            Gv = self.G.rearrange("(nb dc) p t -> dc p nb t", nb=4)
            blocks = BLOCKS[1:] if last else BLOCKS
            pysets = [py, [pa[0], pa[1], pmean, pmsq]]
            cnt = {"npa": 0, "nw1": 0, "nw2": 0, "ng": 0, "ntb": 0}

            def do_block(bi, t0, n):
                s = 1 if t0 < TC else 0
                o_ = oT[bi % 2]
                A = Abuf[bi % 2]
                Ad = Adeps[bi % 2]
                self.dma("sync", o_[:, :, 0:n], self.OT[:, :, t0:t0 + n].rearrange("c p t -> p c t"), writes=[o_])
                self.dma("sync", A[:, :, 0:n], self.xres_src[:, :, t0:t0 + n].rearrange("f p t -> p f t"), writes=Ad)
                for dc in range(8):
                    g_ = gt[cnt["ng"] % 2]
                    cnt["ng"] += 1
                    self.dma("sync", g_[:, :, 0:n], Gv[dc, :, :, t0:t0 + n], writes=[g_])
                    pys = pysets[dc % 2]
                    for nb in range(4):
                        for mc in range(2):
                            self.mm(pys[nb], pys[nb][:, 0:n], Pw, Pw[:, nb, mc, dc * 128:(dc + 1) * 128], o_, o_[:, nb * 2 + mc, 0:n],
                                    start=(mc == 0), stop=(mc == 1))
                    ys_ = ysb[cnt["ntb"] % 2]
                    tq_ = tq[cnt["ntb"] % 2]
                    cnt["ntb"] += 1
                    for nb in range(4):
                        self.I("scalar", "copy", [pys[nb]], [ys_], out=ys_[:, nb, 0:n], in_=pys[nb][:, 0:n])
                    self.I("vector", "tensor_tensor", [ys_, g_], [tq_], out=tq_[:, :, 0:n], in0=ys_[:, :, 0:n], in1=g_[:, :, 0:n], op=ALU.mult)
                    self.I("gpsimd", "tensor_tensor", [tq_], [tq_], out=tq_[:, 0, 0:n], in0=tq_[:, 0, 0:n], in1=tq_[:, 1, 0:n], op=ALU.add)
                    self.I("vector", "tensor_tensor", [tq_], [tq_], out=tq_[:, 2, 0:n], in0=tq_[:, 2, 0:n], in1=tq_[:, 3, 0:n], op=ALU.add)
                    self.op("gpsimd", (lambda e, tq_=tq_, dc=dc: e.tensor_tensor(out=E[:, dc, 0:n], in0=tq_[:, 0, 0:n], in1=tq_[:, 2, 0:n], op=ALU.add)),
                            [tq_.d], [Ed[dc]])
                for d2 in range(8):
                    p_ = pa[cnt["npa"] % 2]
                    cnt["npa"] += 1
                    for dc in range(8):
                        self.op("tensor", (lambda e, p_=p_, dc=dc, d2=d2: e.matmul(p_[:, 0:n], lhsT=Wo[:, dc, d2 * 128:(d2 + 1) * 128], rhs=E[:, dc, 0:n],
                                                                                  start=(dc == 0), stop=(dc == 7))), [Wo.d, Ed[dc]], [p_.d])
                    self.op("vector", (lambda e, p_=p_, d2=d2: e.scalar_tensor_tensor(out=B[:, d2, 0:n], in0=p_[:, 0:n], scalar=gs[:, l, 0, d2, s:s + 1],
                                                                                     in1=A[:, d2, 0:n], op0=ALU.mult, op1=ALU.add)), [p_.d, gs.d, Ad[d2]], [Bd[d2]])
                    self.ln_stats_chunk(B, Bd, d2, n, hid, Hd, hid, Hd, 0, 8)
                    if d2 >= 2:
                        self.ln_stats_mm(d2 - 2, n, hid, Hd, hid, Hd, 0, 8, pmean, pmsq)
                for fc in (6, 7):
                    self.ln_stats_mm(fc, n, hid, Hd, hid, Hd, 0, 8, pmean, pmsq)
                self.ln_finish(B, Bd, n, self.lnp[:, l, 0, :], self.lnp[:, l, 1, :], self.gA[:, l, 0, :, s], self.gA[:, l, 1, :, s], E, Ed, pmean, pmsq, st, EPS2)
                for j in range(32):
                    ws = w1s[cnt["nw1"] % 3]
                    cnt["nw1"] += 1
                    self.dma("sync", ws[:], self.Wb1[j], writes=[ws])
                    p_ = pa[cnt["npa"] % 2]
                    cnt["npa"] += 1
                    for kc in range(8):
                        self.op("tensor", (lambda e, p_=p_, ws=ws, kc=kc: e.matmul(p_[:, 0:n], lhsT=ws[:, kc, :], rhs=E[:, kc, 0:n], start=(kc == 0), stop=(kc == 7))),
                                [ws.d, Ed[kc]], [p_.d])
                    self.op("scalar", (lambda e, p_=p_, j=j: e.activation(out=hid[:, j, 0:n], in_=p_[:, 0:n], func=AF.Relu)), [p_.d], [Hd[j]])
                    self.op("gpsimd", (lambda e, j=j: e.tensor_tensor(out=hid[:, j, 0:n], in0=hid[:, j, 0:n], in1=hid[:, j, 0:n], op=ALU.mult)), [Hd[j]], [Hd[j]])
                for c in range(8):
                    ws = w2s[cnt["nw2"] % 2]
                    cnt["nw2"] += 1
                    self.dma("sync", ws[:], self.Wb2[c], writes=[ws])
                    p_ = pa[cnt["npa"] % 2]
                    cnt["npa"] += 1
                    for j in range(32):
                        self.op("tensor", (lambda e, p_=p_, ws=ws, j=j: e.matmul(p_[:, 0:n], lhsT=ws[:, j, :], rhs=hid[:, j, 0:n], start=(j == 0), stop=(j == 31))),
                                [ws.d, Hd[j]], [p_.d])
                    self.op("vector", (lambda e, p_=p_, c=c: e.scalar_tensor_tensor(out=A[:, c, 0:n], in0=p_[:, 0:n], scalar=gs[:, l, 1, c, s:s + 1],
                                                                                   in1=B[:, c, 0:n], op0=ALU.mult, op1=ALU.add)), [p_.d, gs.d, Bd[c]], [Ad[c]])
                    self.ln_stats_chunk(A, Ad, c, n, E, Ed, Fb, Fd, 0, 0)
                    if c >= 2:
                        self.ln_stats_mm(c - 2, n, E, Ed, Fb, Fd, 0, 0, pmean, pmsq)
                for fc in (6, 7):
                    self.ln_stats_mm(fc, n, E, Ed, Fb, Fd, 0, 0, pmean, pmsq)
                if not last:
                    self.ln_finish(A, Ad, n, self.lnp[:, l, 2, :], self.lnp[:, l, 3, :], self.gA[:, l, 2, :, s], self.gA[:, l, 3, :, s], Fb, Fd, pmean, pmsq, st, EPS2)
                    self.S.dma("scalar", self.xresT[:, :, t0:t0 + n].rearrange("f p t -> p f t"), A[:, :, 0:n], Ad, [])
                    self.S.dma("scalar", self.hT[:, :, t0:t0 + n].rearrange("f p t -> p f t"), Fb[:, :, 0:n], Fd, [])
                else:
                    self.ln_finish(A, Ad, n, self.lnp[:, l, 2, :], self.lnp[:, l, 3, :], None, None, None, None, pmean, pmsq, st, EPS2)
                    for a in range(n // 128):
                        o2 = ot[a % 2]
                        for half in range(2):
                            p_ = py[(a * 2 + half) % 4]
                            for f4 in range(4):
                                fc = half * 4 + f4
                                self.op("tensor", (lambda e, p_=p_, f4=f4, fc=fc, a=a: e.matmul(p_[:, f4 * 128:(f4 + 1) * 128], lhsT=A[:, fc, a * 128:(a + 1) * 128],
                                                                                               rhs=self.ident[:], start=True, stop=True)), [Ad[fc], self.ident.d], [p_.d])
                            self.I("scalar" if half else "vector", "copy" if half else "tensor_copy", [p_], [o2],
                                   out=o2[:, half * 512:(half + 1) * 512], in_=p_[:, :])
                        r0 = t0 - TC + a * 128
                        dop = self.dma("scalar", self.out[r0:r0 + 128, :], o2[:], reads=[o2])
                        dd = Dep()
                        dd.writers = [dop]
                        self.outdeps.append(dd)
            for bi, (t0, n) in enumerate(blocks):
                do_block(bi, t0, n)
            self.S.barrier()

    def _xb(self, es):
        if not hasattr(self, "_xbt") or self._xbt_es is not es:
            self._xbt = self.sb(es, "xbt", [128, 8, 512], BF16)
            self._xbt_es = es
        return self._xbt
    def mixer_lru(self, l):
        W = self.w
        with contextlib.ExitStack() as es:
            ubuf = self.sb(es, "l_u", [128, T + 8])
            xc = self.sb(es, "l_xc", [128, T])
            r = self.sb(es, "l_r", [128, T])
            ig = self.sb(es, "l_i", [128, T])
            tmp = self.sb(es, "l_t", [128, T])
            hf = self.sb(es, "l_hf", [128, T])
            hb = self.sb(es, "l_hb", [128, T])
            ob = self.sb(es, "l_ob", [128, T], BF16)
            cw = self.sb(es, "l_cw", [128, 4])
            cb = self.sb(es, "l_cb", [128, 1])
            prm = self.sb(es, "l_prm", [128, 3, 2])
            cd = self.sb(es, "l_cd", [128, 2, 2])
            e1 = self.sb(es, "l_e1", [128, 2])
            wbd = [self.sb(es, "l_wbd%d" % i, [128, 128]) for i in range(2)]
            pr = [self.ps(es, "l_pr%d" % i, [128, 512]) for i in range(4)]
            npr = 0
            for cp in range(2):
                ch = slice(cp * 128, (cp + 1) * 128)
                self.I("gpsimd", "memset", [], [ubuf], ubuf[:], 0.0)
                self.dma("sync", ubuf[:, 2:2 + TC], self.UF[16 + cp, :, 0:TC], writes=[ubuf])
                self.dma("sync", ubuf[:, 261:261 + TL], self.UF[16 + cp, :, TC:T], writes=[ubuf])
                self.dma("sync", cw[:], W["lru_conv_w"][l, :, ch].rearrange("j c -> c j"), writes=[cw])
                self.dma("sync", cb[:], W["lru_conv_b"][l, ch].rearrange("(c o) -> c o", o=1), writes=[cb])
                for i, nm in enumerate(("lru_ba", "lru_bx", "lru_lambda")):
                    self.dma("sync", prm[:, i, :], W[nm][l, :, ch].rearrange("d c -> c d"), writes=[prm])
                self.I("scalar", "activation", [prm], [e1], out=e1[:], in_=prm[:, 2, :], func=AF.Exp, scale=-1.0)
                self.I("scalar", "activation", [e1], [e1], out=e1[:], in_=e1[:], func=AF.Ln, bias=1.0)
                self.I("vector", "tensor_scalar_mul", [e1], [cd], out=cd[:, 0, :], in0=e1[:], scalar1=-8.0)
                self.I("vector", "tensor_scalar_mul", [e1], [cd], out=cd[:, 1, :], in0=e1[:], scalar1=-16.0)
                for (o0, u0, n) in ((0, 0, TC), (TC, 259, TL)):
                    self.I("vector", "tensor_scalar", [ubuf, cw, cb], [xc], out=xc[:, o0:o0 + n], in0=ubuf[:, u0:u0 + n],
                           scalar1=cw[:, 0:1], scalar2=cb[:, 0:1], op0=ALU.mult, op1=ALU.add)
                    for j in range(1, 4):
                        self.I("vector", "scalar_tensor_tensor", [ubuf, cw, xc], [xc], out=xc[:, o0:o0 + n], in0=ubuf[:, u0 + j:u0 + j + n],
                               scalar=cw[:, j:j + 1], in1=xc[:, o0:o0 + n], op0=ALU.mult, op1=ALU.add)
                for d in range(2):
                    for gi, (nm, dst) in enumerate((("lru_wa", r), ("lru_wx", ig))):
                        wb = wbd[gi]
                        self.I("gpsimd", "memset", [], [wb], wb[:], 0.0)
                        for i in range(2):
                            self.dma("sync", wb[i * 64:(i + 1) * 64, i * 64:(i + 1) * 64], W[nm][l, d, 2 * cp + i], writes=[wb])
                        for (t0, n) in BLOCKS:
                            p_ = pr[npr % 4]
                            npr += 1
                            self.mm(p_, p_[:, 0:n], wb, wb[:], xc, xc[:, t0:t0 + n])
                            self.I("scalar", "activation", [p_, prm], [dst], out=dst[:, t0:t0 + n], in_=p_[:, 0:n], func=AF.Sigmoid,
                                   bias=prm[:, gi, d:d + 1])
                    self.I("scalar", "activation", [r, cd], [tmp], out=tmp[:], in_=r[:], func=AF.Exp, scale=cd[:, 1, d:d + 1])
                    self.I("scalar", "activation", [r, cd], [r], out=r[:], in_=r[:], func=AF.Exp, scale=cd[:, 0, d:d + 1])
                    self.I("scalar", "activation", [tmp], [tmp], out=tmp[:], in_=tmp[:], func=AF.Sqrt, scale=-1.0, bias=1.0)
                    self.I("gpsimd", "tensor_tensor", [tmp, ig], [tmp], out=tmp[:], in0=tmp[:], in1=ig[:], op=ALU.mult)
                    self.I("gpsimd", "tensor_tensor", [tmp, xc], [tmp], out=tmp[:], in0=tmp[:], in1=xc[:], op=ALU.mult)
                    if d == 0:
                        self.I("vector", "tensor_tensor_scan", [r, tmp], [hf], out=hf[:], data0=r[:], data1=tmp[:], initial=0.0,
                               op0=ALU.mult, op1=ALU.add)
                    else:
                        rv = lambda tl, a, b: tl[:, a:b][:, ::-1]
                        self.I("vector", "tensor_tensor_scan", [r, tmp], [hb], out=rv(hb, 0, TC), data0=rv(r, 0, TC), data1=rv(tmp, 0, TC),
                               initial=0.0, op0=ALU.mult, op1=ALU.add)
                        self.I("vector", "tensor_tensor_scan", [r, tmp, hb], [hb], out=rv(hb, TC, T), data0=rv(r, TC, T), data1=rv(tmp, TC, T),
                               initial=hb[:, 0:1], op0=ALU.mult, op1=ALU.add)
                self.I("vector", "tensor_tensor", [hf, hb], [ob], out=ob[:], in0=hf[:], in1=hb[:], op=ALU.add)
                self.dma("sync", self.OT[6 + cp], ob[:], reads=[ob])
            self.S.barrier()

    def load_const(self, es, name, shape=(128, 128), dtype=F32):
        t = self.sb(es, "c_" + name, list(shape), dtype)
        self.dma("sync", t[:], self.cst[name], writes=[t])
        return t

    def mixer_mlstm(self, l):
        W = self.w
        NCH = T // 128
        LN8 = float(np.log(8.0))
        with contextlib.ExitStack() as es:
            trif = self.load_const(es, "trif")
            trib = self.load_const(es, "trib")
            ones = self.load_const(es, "ones")
            gb = self.sb(es, "m_gb", [128, 16])
            z = self.sb(es, "m_z", [128, NCH, 16])
            nlf = self.sb(es, "m_nlf", [128, NCH, 16])
            ncF = self.sb(es, "m_ncF", [128, NCH, 16])
            ncB = self.sb(es, "m_ncB", [128, NCH, 16])
            es_ = [self.sb(es, "m_es%d" % d, [128, NCH, 4]) for d in range(2)]
            eb_ = [self.sb(es, "m_eb%d" % d, [128, NCH, 4]) for d in range(2)]
            eg_ = [self.sb(es, "m_eg%d" % d, [128, NCH, 4]) for d in range(2)]
            ng = self.sb(es, "m_ng", [128, 256])
            qT = self.sb(es, "m_qT", [128, T])
            kTm = [self.sb(es, "m_kTm%d" % i, [128, T]) for i in range(2)]
            vt = self.sb(es, "m_vt", [128, NCH, 128])
            kt = self.sb(es, "m_kt", [128, NCH, 128])
            hd = [self.sb(es, "m_h%d" % d, [128, NCH, 128]) for d in range(2)]
            Cst = self.sb(es, "m_C", [128, 2, 65])
            AT = [self.sb(es, "m_AT%d" % i, [128, 2, 128]) for i in range(2)]
            Vt = [self.sb(es, "m_Vt%d" % i, [128, 2, 65]) for i in range(2)]
            Vt2 = [self.sb(es, "m_Vt2%d" % i, [128, 2, 65]) for i in range(2)]
            sm = [self.sb(es, "m_sm%d" % i, [128, 4, 2]) for i in range(2)]
            red = self.sb(es, "m_red", [128, NCH * 2])
            red2 = self.sb(es, "m_red2", [128, NCH * 2])
            ob = self.sb(es, "m_ob", [128, T], BF16)
            ps_s = [self.ps(es, "m_pss%d" % i, [128, 2, 128]) for i in range(2)]
            ps_n = [self.ps(es, "m_psn%d" % i, [128, 2, 65]) for i in range(2)]
            ps_k = [self.ps(es, "m_psk%d" % i, [128, 2, 65]) for i in range(2)]
            ps_g = [self.ps(es, "m_psg%d" % i, [128, 512]) for i in range(2)]
            self.dma("sync", gb[:], W["mlstm_gate_b"][l].rearrange("a b c -> (a b c)").partition_broadcast(128), writes=[gb])
            self.dma("sync", ng[:], W["mlstm_norm_g"][l].partition_broadcast(128), writes=[ng])
            self.dma("sync", z[:], self.UT[:, 512:528].rearrange("(c p) g -> p c g", p=128), writes=[z])
            self.I("vector", "tensor_tensor", [z, gb], [z], out=z[:], in0=z[:], in1=gb[:].unsqueeze(1).to_broadcast([128, NCH, 16]), op=ALU.add)
            self.I("scalar", "activation", [z], [nlf], out=nlf[:], in_=z[:], func=AF.Exp, scale=-1.0)
            self.I("scalar", "activation", [nlf], [nlf], out=nlf[:], in_=nlf[:], func=AF.Ln, bias=1.0)
            for (tri, dst) in ((trif, ncF), (trib, ncB)):
                for hf_ in range(2):
                    p_ = ps_g[hf_]
                    self.mm(p_, p_[:, 0:272], tri, tri[:], nlf, nlf[:, hf_ * 17:(hf_ + 1) * 17, :])
                    self.I("vector", "tensor_copy", [p_], [dst], out=dst[:, hf_ * 17:(hf_ + 1) * 17, :], in_=p_[:, 0:272])
            for d, ncx in ((0, ncF), (1, ncB)):
                ci, cf = d * 8, d * 8 + 4
                self.I("vector", "tensor_tensor", [z, ncx], [es_[d]], out=es_[d][:], in0=z[:, :, ci:ci + 4], in1=ncx[:, :, cf:cf + 4], op=ALU.add)
                self.I("scalar", "activation", [es_[d]], [es_[d]], out=es_[d][:], in_=es_[d][:], func=AF.Exp, bias=-LN8)
                self.I("scalar", "activation", [ncx], [eb_[d]], out=eb_[d][:], in_=ncx[:, :, cf:cf + 4], func=AF.Exp, scale=-1.0)
            for hf_ in range(2):
                p_ = ps_g[hf_]
                self.mm(p_, p_[:, 0:272], ones, ones[:], nlf, nlf[:, hf_ * 17:(hf_ + 1) * 17, :])
                for d in range(2):
                    self.I("scalar", "activation", [p_], [eg_[d]], out=eg_[d][:, hf_ * 17:(hf_ + 1) * 17, :],
                           in_=p_[:, 0:272].rearrange("p (c g) -> p c g", g=16)[:, :, d * 8 + 4:d * 8 + 8], func=AF.Exp, scale=-1.0)
            order = [list(range(NCH)), [1, 0] + list(range(NCH - 1, 1, -1))]
            nb = 0
            for hp in range(2):
                self.dma("sync", qT[:], self.UF[hp], writes=[qT])
                for hh in range(2):
                    self.I("gpsimd", "memset", [], [kTm[hh]], kTm[hh][:], 0.0)
                    self.dma("sync", kTm[hh][hh * 64:(hh + 1) * 64, :], self.UF[2 + hp, hh * 64:(hh + 1) * 64, :], writes=[kTm[hh]])
                self.dma("sync", vt[:], self.UT[:, hp * 128:(hp + 1) * 128].rearrange("(c p) g -> p c g", p=128), writes=[vt])
                self.dma("sync", kt[:], self.UT[:, 528 + hp * 128:528 + (hp + 1) * 128].rearrange("(c p) g -> p c g", p=128), writes=[kt])
                for d in range(2):
                    mask = trif if d == 0 else trib
                    self.I("vector", "memset", [], [Cst], Cst[:], 0.0)
                    seq = order[d]

                    def stage_a(c, k):
                        cs = slice(c * 128, (c + 1) * 128)
                        pS, A_, V_ = ps_s[k % 2], AT[k % 2], Vt[k % 2]
                        for hh in range(2):
                            self.mm(pS, pS[:, hh, :], kTm[hh], kTm[hh][:, cs], qT, qT[:, cs])
                        self.I("vector", "tensor_tensor", [pS, mask], [A_], out=A_[:], in0=pS[:], in1=mask[:].unsqueeze(1).to_broadcast([128, 2, 128]), op=ALU.mult)
                        esl = es_[d][:, c, 2 * hp:2 * hp + 2]
                        self.I("gpsimd", "tensor_tensor", [vt, es_[d]], [V_], out=V_[:, :, 0:64], in0=vt[:, c, :].rearrange("p (h e) -> p h e", h=2),
                               in1=esl.unsqueeze(2).to_broadcast([128, 2, 64]), op=ALU.mult)
                        self.I("gpsimd", "tensor_copy", [es_[d]], [V_], out=V_[:, :, 64], in_=esl)
                        V2_ = Vt2[k % 2]
                        self.I("gpsimd", "tensor_tensor", [V_, eg_[d]], [V2_], out=V2_[:], in0=V_[:],
                               in1=eg_[d][:, c, 2 * hp:2 * hp + 2].unsqueeze(2).to_broadcast([128, 2, 65]), op=ALU.mult)

                    stage_a(seq[0], nb)
                    for ci, c in enumerate(seq):
                        if ci + 1 < len(seq):
                            stage_a(seq[ci + 1], nb + 1)
                        cs = slice(c * 128, (c + 1) * 128)
                        pN, pK = ps_n[nb % 2], ps_k[nb % 2]
                        A_, V_, s_, V2_ = AT[nb % 2], Vt[nb % 2], sm[nb % 2], Vt2[nb % 2]
                        nb += 1
                        for hh in range(2):
                            self.mm(pN, pN[:, hh, :], A_, A_[:, hh, :], V_, V_[:, hh, :], start=True, stop=False)
                            self.mm(pN, pN[:, hh, :], qT, qT[:, cs], Cst, Cst[:, hh, :], start=False, stop=True)
                        for hh in range(2):
                            self.mm(pK, pK[:, hh, :], kt, kt[:, c, :], V2_, V2_[:, hh, :])
                        for hh in range(2):
                            rows = slice(hh * 64, (hh + 1) * 64)
                            self.I("vector", "scalar_tensor_tensor", [pK, Cst, eg_[d]], [Cst], out=Cst[rows, hh, :], in0=Cst[rows, hh, :],
                                   scalar=eg_[d][rows, c, 2 * hp + hh:2 * hp + hh + 1], in1=pK[rows, hh, :], op0=ALU.mult, op1=ALU.add)
                        ebl = eb_[d][:, c, 2 * hp:2 * hp + 2]
                        self.I("vector", "tensor_tensor", [pN, eb_[d]], [s_], out=s_[:, 0, :], in0=pN[:, :, 64], in1=ebl, op=ALU.mult)
                        self.I("scalar", "activation", [s_], [s_], out=s_[:, 1, :], in_=s_[:, 0, :], func=AF.Abs)
                        self.I("vector", "tensor_scalar_max", [s_], [s_], out=s_[:, 1, :], in0=s_[:, 1, :], scalar1=1.0)
                        self.I("vector", "reciprocal", [s_], [s_], out=s_[:, 2, :], in_=s_[:, 1, :])
                        self.I("gpsimd", "tensor_tensor", [s_, eb_[d]], [s_], out=s_[:, 3, :], in0=s_[:, 2, :], in1=ebl, op=ALU.mult)
                        self.I("vector", "tensor_tensor", [pN, s_], [hd[d]], out=hd[d][:, c, :].rearrange("p (h e) -> p h e", h=2), in0=pN[:, :, 0:64],
                               in1=s_[:, 3, :].unsqueeze(2).to_broadcast([128, 2, 64]), op=ALU.mult)
                hs, sq = hd[0], hd[1]
                hs4 = hs[:].rearrange("p c (h e) -> p (c h) e", h=2)
                sq4 = sq[:].rearrange("p c (h e) -> p (c h) e", h=2)
                NG = NCH * 2
                self.I("vector", "tensor_tensor", [hd[0], hd[1]], [hs], out=hs[:], in0=hd[0][:], in1=hd[1][:], op=ALU.add)
                self.I("vector", "tensor_reduce", [hs], [red], out=red[:], in_=hs4, axis=AX.X, op=ALU.add)
                self.I("vector", "tensor_scalar_mul", [red], [red], out=red[:], in0=red[:], scalar1=-1.0 / 64.0)
                self.I("vector", "tensor_tensor", [hs, red], [hs], out=hs4, in0=hs4, in1=red[:].unsqueeze(2).to_broadcast([128, NG, 64]), op=ALU.add)
                self.I("gpsimd", "tensor_tensor", [hs], [sq], out=sq[:], in0=hs[:], in1=hs[:], op=ALU.mult)
                self.I("vector", "tensor_reduce", [sq], [red2], out=red2[:], in_=sq4, axis=AX.X, op=ALU.add)
                self.rsqrt(red2, red2[:], red2, red2[:], LN_EPS, scale=1.0 / 64.0)
                self.I("vector", "tensor_tensor", [hs, red2], [hs], out=hs4, in0=hs4, in1=red2[:].unsqueeze(2).to_broadcast([128, NG, 64]), op=ALU.mult)
                self.I("gpsimd", "tensor_tensor", [hs, ng], [hs], out=hs[:], in0=hs[:], in1=ng[:, hp * 128:(hp + 1) * 128].unsqueeze(1).to_broadcast([128, NCH, 128]), op=ALU.mult)
                self.dma("sync", sq[:], self.UT[:, 256 + hp * 128:256 + (hp + 1) * 128].rearrange("(c p) g -> p c g", p=128), reads=[], writes=[sq])
                self.I("scalar", "activation", [sq], [sq], out=sq[:], in_=sq[:], func=AF.Sigmoid)
                self.I("vector", "tensor_tensor", [hs, sq], [hs], out=hs[:], in0=hs[:], in1=sq[:], op=ALU.mult)
                for c in range(NCH):
                    p_ = ps_g[(c // 4) % 2]
                    self.mm(p_, p_[:, (c % 4) * 128:(c % 4 + 1) * 128], hs, hs[:, c, :], self.ident, self.ident[:])
                    if c % 4 == 3 or c == NCH - 1:
                        c0 = (c // 4) * 4
                        n = (c - c0 + 1) * 128
                        self.I("scalar", "copy", [p_], [ob], out=ob[:, c0 * 128:c0 * 128 + n], in_=p_[:, 0:n])
                self.dma("sync", self.OT[0 + hp], ob[:], reads=[ob])
            self.S.barrier()

    def mixer_mla(self, l):
        W = self.w
        NCH = T // 128
        SCALE = 96.0 ** -0.5
        RP = slice(64, 96)
        with contextlib.ExitStack() as es:
            ones = self.load_const(es, "ones")
            cqA = self.sb(es, "a_cqA", [128, T])
            cqB = self.sb(es, "a_cqB", [128, T])
            ckv = self.sb(es, "a_ckv", [128, T])
            rq = self.sb(es, "a_rq", [128, T])
            rkv = self.sb(es, "a_rkv", [128, T])
            rkvt = self.sb(es, "a_rkvt", [128, NCH])
            sqt = [self.sb(es, "a_sq%d" % i, [128, 512]) for i in range(2)]
            gq = self.sb(es, "a_gq", [128, 2])
            gkv = self.sb(es, "a_gkv", [128, 1])
            wqA = self.sb(es, "a_wqA", [128, 384])
            wqB = self.sb(es, "a_wqB", [128, 384])
            wqRA = self.sb(es, "a_wqRA", [128, 4, 96])
            wqRB = self.sb(es, "a_wqRB", [128, 4, 96])
            wk = self.sb(es, "a_wk", [128, 256])
            wv = self.sb(es, "a_wv", [128, 256])
            qf = self.sb(es, "a_qf", [128, T], BF16)
            kf = self.sb(es, "a_kf", [128, T], BF16)
            kRs = self.sb(es, "a_kRs", [128, T], BF16)
            Vf = self.sb(es, "a_V", [128, NCH, 4 * 65 + 64], BF16)
            cs_ = [self.sb(es, "a_cs%d" % i, [128, 2, 512]) for i in range(2)]
            tr = [self.sb(es, "a_tr%d" % i, [128, 2, 512]) for i in range(2)]
            PT = [self.sb(es, "a_PT%d" % i, [128, 512], BF16) for i in range(3)]
            oTs = [self.sb(es, "a_oT%d" % i, [128, 512]) for i in range(2)]
            rc = [self.sb(es, "a_rc%d" % i, [128, 4]) for i in range(2)]
            psc = [self.ps(es, "a_psc%d" % i, [128, 512]) for i in range(2)]
            ppo = [self.ps(es, "a_ppo%d" % i, [128, 512]) for i in range(2)]
            ptk = self.ps(es, "a_ptk", [128, 512])
            pj = [self.ps(es, "a_pj%d" % i, [128, 512]) for i in range(2)]
            es2 = contextlib.ExitStack()
            kr = self.sb(es2, "a_kr", [128, T])
            krR = self.sb(es2, "a_krR", [128, T])
            self.dma("sync", cqA[:], self.UF[12], writes=[cqA])
            self.dma("sync", cqB[0:64, :], self.UF[13, 0:64, :], writes=[cqB])
            self.dma("sync", ckv[:], self.UF[14], writes=[ckv])
            self.dma("sync", kr[RP, :], self.UF[15, 0:32, :], writes=[kr])
            self.dma("sync", krR[RP, :], self.UF[15, 32:64, :], writes=[krR])
            self.dma("sync", gq[:, 0:1], W["mla_qn_g"][l, 0:128].rearrange("(c o) -> c o", o=1), writes=[gq])
            self.dma("sync", gq[0:64, 1:2], W["mla_qn_g"][l, 128:192].rearrange("(c o) -> c o", o=1), writes=[gq])
            self.dma("sync", gkv[:], W["mla_kvn_g"][l].rearrange("(c o) -> c o", o=1), writes=[gkv])
            self.dma("sync", wqA[:], W["mla_wuq"][l, 0:128, :], writes=[wqA])
            self.dma("sync", wqB[0:64, :], W["mla_wuq"][l, 128:192, :], writes=[wqB])
            self.dma("sync", wk[:], W["mla_wuk"][l], writes=[wk])
            self.dma("sync", wv[:], W["mla_wuv"][l], writes=[wv])
            self.I("vector", "tensor_scalar_mul", [wqA, gq], [wqA], out=wqA[:], in0=wqA[:], scalar1=gq[:, 0:1])
            self.I("vector", "tensor_scalar_mul", [wqB, gq], [wqB], out=wqB[0:64, :], in0=wqB[0:64, :], scalar1=gq[0:64, 1:2])
            self.I("vector", "tensor_scalar_mul", [wk, gkv], [wk], out=wk[:], in0=wk[:], scalar1=gkv[:, 0:1])
            self.I("vector", "tensor_scalar_mul", [wv, gkv], [wv], out=wv[:], in0=wv[:], scalar1=gkv[:, 0:1])
            for (wsrc, wdst, rows) in ((wqA, wqRA, 128), (wqB, wqRB, 64)):
                self.I("gpsimd", "memset", [], [wdst], wdst[:], 0.0)
                for h in range(4):
                    b0 = h * 96 + 64
                    for (dst, srcc, sgn) in ((0, 8, -1.0), (8, 0, 1.0), (16, 24, -1.0), (24, 16, 1.0)):
                        self.I("vector", "tensor_scalar_mul", [wsrc], [wdst], out=wdst[0:rows, h, 64 + dst:64 + dst + 8],
                               in0=wsrc[0:rows, b0 + srcc:b0 + srcc + 8], scalar1=sgn)
            nj = 0
            for (t0, n) in BLOCKS:
                for which in range(2):
                    p_ = pj[nj % 2]
                    nj += 1
                    if which == 0:
                        s1, s2 = sqt[0], sqt[1]
                        self.I("scalar", "activation", [cqA], [s1], out=s1[:, 0:n], in_=cqA[:, t0:t0 + n], func=AF.Square)
                        self.I("scalar", "activation", [cqB], [s2], out=s2[0:64, 0:n], in_=cqB[0:64, t0:t0 + n], func=AF.Square)
                        self.mm(p_, p_[:, 0:n], ones, ones[:], s1, s1[:, 0:n], start=True, stop=False)
                        self.mm(p_, p_[:, 0:n], ones, ones[0:64, :], s2, s2[0:64, 0:n], start=False, stop=True)
                        self.I("vector", "tensor_scalar", [p_], [rq], out=rq[:, t0:t0 + n], in0=p_[:, 0:n], scalar1=1.0 / 192.0, scalar2=RMS_EPS, op0=ALU.mult, op1=ALU.add)
                    else:
                        s1 = sqt[0]
                        self.I("scalar", "activation", [ckv], [s1], out=s1[:, 0:n], in_=ckv[:, t0:t0 + n], func=AF.Square)
                        self.mm(p_, p_[:, 0:n], ones, ones[:], s1, s1[:, 0:n])
                        self.I("vector", "tensor_scalar", [p_], [rkv], out=rkv[:, t0:t0 + n], in0=p_[:, 0:n], scalar1=1.0 / 128.0, scalar2=RMS_EPS, op0=ALU.mult, op1=ALU.add)
                        p2 = pj[nj % 2]
                        nj += 1
                        for a in range(n // 128):
                            self.mm(p2, p2[:, a:a + 1], s1, s1[:, a * 128:(a + 1) * 128], ones, ones[:, 0:1])
                        c0 = t0 // 128
                        self.I("vector", "tensor_scalar", [p2], [rkvt], out=rkvt[:, c0:c0 + n // 128], in0=p2[:, 0:n // 128], scalar1=1.0 / 128.0, scalar2=RMS_EPS, op0=ALU.mult, op1=ALU.add)
            for tl_ in (rq, rkv, rkvt):
                self.rsqrt(tl_, tl_[:], tl_, tl_[:], 0.0)
            self.I("gpsimd", "memset", [], [Vf], Vf[:], 1.0)
            self.I("gpsimd", "memset", [], [qf], qf[96:128, :], 0.0)
            self.I("gpsimd", "memset", [], [kf], kf[96:128, :], 0.0)
            for c in range(NCH):
                p_ = pj[nj % 2]
                nj += 1
                self.mm(p_, p_[:, 0:256], ckv, ckv[:, c * 128:(c + 1) * 128], wv, wv[:])
                self.I("vector", "tensor_scalar_mul", [p_, rkvt], [Vf], out=Vf[:, c, 0:260].rearrange("p (h e) -> p h e", h=4)[:, :, 0:64],
                       in0=p_[:, 0:256].rearrange("p (h e) -> p h e", h=4), scalar1=rkvt[:, c:c + 1])
            for bi, (t0, n) in enumerate(BLOCKS):
                if t0 < TC:
                    self.I("vector", "tensor_copy", [kr], [kRs], out=kRs[RP, t0:t0 + n], in_=kr[RP, t0:t0 + n])
                    continue
                c_ = cs_[bi % 2]
                t_ = tr[bi % 2]
                self.dma("sync", c_[RP, 0, 0:n], self.cst["cosT"][:, t0 - TC:t0 - TC + n], writes=[c_])
                self.dma("sync", c_[RP, 1, 0:n], self.cst["sinT"][:, t0 - TC:t0 - TC + n], writes=[c_])
                self.I("vector", "tensor_tensor", [kr, c_], [t_], out=t_[RP, 0, 0:n], in0=kr[RP, t0:t0 + n], in1=c_[RP, 0, 0:n], op=ALU.mult)
                self.I("gpsimd", "tensor_tensor", [krR, c_], [t_], out=t_[RP, 1, 0:n], in0=krR[RP, t0:t0 + n], in1=c_[RP, 1, 0:n], op=ALU.mult)
                self.I("vector", "tensor_tensor", [t_], [kRs], out=kRs[RP, t0:t0 + n], in0=t_[RP, 0, 0:n], in1=t_[RP, 1, 0:n], op=ALU.add)
            es2.close()
            otok = self.sb(es, "a_otok", [128, NCH, 128])
            ob = self.sb(es, "a_ob", [128, T], BF16)
            nb = 0
            for hp in range(2):
                for hh in range(2):
                    h = hp * 2 + hh
                    b0 = h * 96
                    for bi, (t0, n) in enumerate(BLOCKS):
                        pq, pk = pj[0], pj[1]
                        self.mm(pq, pq[0:96, 0:n], wqA, wqA[:, b0:b0 + 96], cqA, cqA[:, t0:t0 + n], start=True, stop=False)
                        self.mm(pq, pq[0:96, 0:n], wqB, wqB[0:64, b0:b0 + 96], cqB, cqB[0:64, t0:t0 + n], start=False, stop=True)
                        self.I("vector", "tensor_tensor", [pq, rq], [qf], out=qf[0:64, t0:t0 + n], in0=pq[0:64, 0:n], in1=rq[0:64, t0:t0 + n], op=ALU.mult)
                        self.mm(pk, pk[0:64, 0:n], wk, wk[:, h * 64:(h + 1) * 64], ckv, ckv[:, t0:t0 + n])
                        self.I("vector", "tensor_tensor", [pk, rkv], [kf], out=kf[0:64, t0:t0 + n], in0=pk[0:64, 0:n], in1=rkv[0:64, t0:t0 + n], op=ALU.mult)
                        if t0 < TC:
                            self.I("vector", "tensor_tensor", [pq, rq], [qf], out=qf[RP, t0:t0 + n], in0=pq[RP, 0:n], in1=rq[RP, t0:t0 + n], op=ALU.mult)
                            continue
                        c_ = cs_[bi % 2]
                        t_ = tr[bi % 2]
                        self.dma("sync", c_[RP, 0, 0:n], self.cst["cosT"][:, t0 - TC:t0 - TC + n], writes=[c_])
                        self.dma("sync", c_[RP, 1, 0:n], self.cst["sinT"][:, t0 - TC:t0 - TC + n], writes=[c_])
                        self.I("vector", "tensor_tensor", [pq, c_], [t_], out=t_[RP, 0, 0:n], in0=pq[RP, 0:n], in1=c_[RP, 0, 0:n], op=ALU.mult)
                        p2 = pj[1]
                        self.mm(p2, p2[0:96, 0:n], wqRA, wqRA[:, h, :], cqA, cqA[:, t0:t0 + n], start=True, stop=False)
                        self.mm(p2, p2[0:96, 0:n], wqRB, wqRB[0:64, h, :], cqB, cqB[0:64, t0:t0 + n], start=False, stop=True)
                        self.I("vector", "tensor_tensor", [p2, c_], [t_], out=t_[RP, 1, 0:n], in0=p2[RP, 0:n], in1=c_[RP, 1, 0:n], op=ALU.mult)
                        self.I("gpsimd", "tensor_tensor", [t_], [t_], out=t_[RP, 0, 0:n], in0=t_[RP, 0, 0:n], in1=t_[RP, 1, 0:n], op=ALU.add)
                        self.I("gpsimd", "tensor_tensor", [t_, rq], [qf], out=qf[RP, t0:t0 + n], in0=t_[RP, 0, 0:n], in1=rq[RP, t0:t0 + n], op=ALU.mult)
                    self.I("gpsimd", "tensor_copy", [kRs], [kf], out=kf[RP, :], in_=kRs[RP, :])
                    for bi, (t0, n) in enumerate(BLOCKS):
                        nkt = 2 if t0 < TC else NCH
                        nqs = n // 128
                        po = ppo[bi % 2]
                        def score(kt_):
                            ps__ = psc[(nb + kt_) % 2]
                            self.mm(ps__, ps__[:, 0:n], kf, kf[:, kt_ * 128:(kt_ + 1) * 128], qf, qf[:, t0:t0 + n])
                        score(0)
                        for kt in range(nkt):
                            if kt + 1 < nkt:
                                score(kt + 1)
                            ps_ = psc[(nb + kt) % 2]
                            pt_ = PT[(nb + kt) % 3]
                            self.I("scalar", "activation", [ps_], [pt_], out=pt_[:, 0:n], in_=ps_[:, 0:n], func=AF.Exp, scale=SCALE)
                            self.mm(po, po[:, 0:n], Vf, Vf[:, kt, h * 65:h * 65 + 128], pt_, pt_[:, 0:n], start=(kt == 0), stop=(kt == nkt - 1))
                        nb += nkt
                        oT_ = oTs[bi % 2]
                        self.I("scalar", "copy", [po], [oT_], out=oT_[0:65, 0:n], in_=po[0:65, 0:n])
                        for qs in range(nqs):
                            self.mm(ptk, ptk[:, qs * 65:(qs + 1) * 65], oT_, oT_[0:65, qs * 128:(qs + 1) * 128], self.ident, self.ident[0:65, 0:65])
                        r_ = rc[bi % 2]
                        pk3 = ptk[:, 0:nqs * 65].rearrange("p (q e) -> p q e", e=65)
                        c0 = t0 // 128
                        self.I("vector", "reciprocal", [ptk], [r_], out=r_[:, 0:nqs], in_=pk3[:, :, 64])
                        self.I("vector", "tensor_tensor", [ptk, r_], [otok], out=otok[:, c0:c0 + nqs, hh * 64:(hh + 1) * 64], in0=pk3[:, :, 0:64],
                               in1=r_[:, 0:nqs].unsqueeze(2).to_broadcast([128, nqs, 64]), op=ALU.mult)
                for c in range(NCH):
                    p_ = pj[(c // 4) % 2]
                    self.mm(p_, p_[:, (c % 4) * 128:(c % 4 + 1) * 128], otok, otok[:, c, :], self.ident, self.ident[:])
                    if c % 4 == 3 or c == NCH - 1:
                        c0 = (c // 4) * 4
                        n = (c - c0 + 1) * 128
                        self.I("scalar", "copy", [p_], [ob], out=ob[:, c0 * 128:c0 * 128 + n], in_=p_[:, 0:n])
                self.dma("sync", self.OT[4 + hp], ob[:], reads=[ob])
            self.S.barrier()

    def mixer_rwkv(self, l):
        W = self.w
        NCA = T // 64
        DEC = float(np.exp(-0.5))
        with contextlib.ExitStack() as es:
            su = self.load_const(es, "m_su", (128, 64))
            sl = self.load_const(es, "m_sl", (128, 64))
            uu = self.load_const(es, "m_u", (128, 64))
            ll = self.load_const(es, "m_l", (128, 64))
            idb = self.load_const(es, "idb", (128, 64))
            bones = self.load_const(es, "bones")
            rmask = self.load_const(es, "rmask", (128, 512))
            ones = self.load_const(es, "ones")
            sbf = lambda name, shape, dt=F32: self.sb(es, "r_" + name, shape, dt)
            prm = sbf("prm", [128, 20])
            plw = sbf("plw", [128, 3])
            pla = sbf("pla", [128, 3])
            plg = sbf("plg", [128, 3])
            gnb = sbf("gnb", [128, 64])
            W2p = [sbf("W2p%d" % d, [64, 256]) for d in range(2)]
            A2p = [sbf("A2p%d" % d, [64, 256]) for d in range(2)]
            g2r = sbf("g2r", [128, 256])
            yacc = sbf("yacc", [128, NCA, 64])
            vtk = sbf("vtk", [128, NCA, 64])
            gat = sbf("gat", [128, NCA, 64])
            rks = sbf("rks", [128, NCA])
            red = sbf("red", [128, NCA])
            red2 = sbf("red2", [128, NCA])
            ob = sbf("ob", [128, T], BF16)
            xb = [sbf("xb%d" % i, [128, 516]) for i in range(2)]
            xs = [sbf("xs%d" % i, [128, 512]) for i in range(2)]
            names = ["rs", "ks", "vs", "wl", "al", "gl", "td", "ld", "a", "sg", "cum", "Ep", "Em", "Eq", "kkr", "sq", "rn", "kk", "b", "kd",
                     "Kap", "Kt", "Bt", "Rt", "rk"]
            F = {nm: sbf(nm, [128, 512]) for nm in names}
            X = sbf("X", [128, 8, 128])
            W2 = sbf("W2", [128, 8, 128])
            M = {nm: sbf(nm, [128, 8, 64]) for nm in ["Btok", "NBtok", "Kttok", "Vtok", "A0", "A1", "At0", "At1", "TT0", "TT1", "AkT", "ArT", "AbT",
                                                       "NAbT", "G2", "Hp", "QpT", "Yloc", "ytmp"]}
            ST = [sbf("ST%d" % i, [128, 64]) for i in range(3)]
            pg = [self.ps(es, "r_pg%d" % i, [128, 512]) for i in range(4)]
            pW = [self.ps(es, "r_pW%d" % i, [128, 4, 128]) for i in range(2)]
            pZ = self.ps(es, "r_pZ", [128, 512])
            pY = self.ps(es, "r_pY", [128, 8, 64])
            self._npg = 0

            def nextpg():
                self._npg += 1
                return pg[self._npg % 4]

            def col(ap1d):
                return ap1d.rearrange("(c o) -> c o", o=1)
            HR = [slice(0, 64), slice(64, 128)]
            for d in range(2):
                for (tl_, nm) in ((W2p[d], "rwkv_w2"), (A2p[d], "rwkv_a2")):
                    self.I("gpsimd", "memset", [], [tl_], tl_[:], 0.0)
                    self.dma("sync", tl_[d * 32:(d + 1) * 32, :], W[nm][l, d], writes=[tl_])
            for hh in range(2):
                self.dma("sync", g2r[HR[hh], :], W["rwkv_g2"][l], writes=[g2r])
            mu = W["rwkv_mu"][l]
            for (tl_, rows, off) in ((plw, 64, 768), (pla, 64, 832), (plg, 64, 896)):
                self.I("vector", "memset", [], [tl_], tl_[:], 0.0)
                self.dma("sync", tl_[0:64, 0:1], col(mu[off:off + 64]), writes=[tl_])
                if tl_ is plg:
                    self.dma("sync", tl_[64:128, 0:1], col(mu[off:off + 64]), writes=[tl_])
                self.I("vector", "tensor_scalar", [tl_], [tl_], out=tl_[:, 1:2], in0=tl_[:, 0:1], scalar1=-1.0, scalar2=1.0, op0=ALU.mult, op1=ALU.add)
                self.I("vector", "tensor_scalar_mul", [tl_], [tl_], out=tl_[:, 2:3], in0=tl_[:, 0:1], scalar1=0.5)

            def shift_load(dst, rows, src, t0, n, p3, k):
                has_prev = t0 not in (0, TC)
                has_next = (t0 + n) not in (TC, T)
                lo, hi = int(has_prev), int(has_next)
                x_, s_ = xb[k % 2], xs[k % 2]
                self.dma("sync", x_[0:rows, 1 - lo:n + 1 + hi], src[:, t0 - lo:t0 + n + hi], writes=[x_])
                if not has_prev:
                    self.I("gpsimd", "memset", [], [x_], x_[0:rows, 0:1], 0.0)
                if not has_next:
                    self.I("gpsimd", "memset", [], [x_], x_[0:rows, n + 1:n + 2], 0.0)
                self.I("gpsimd", "tensor_tensor", [x_], [s_], out=s_[0:rows, 0:n], in0=x_[0:rows, 0:n], in1=x_[0:rows, 2:n + 2], op=ALU.add)
                self.I("scalar", "mul", [s_, p3[0]], [s_], out=s_[0:rows, 0:n], in_=s_[0:rows, 0:n], mul=p3[2])
                self.I("vector", "scalar_tensor_tensor", [x_, s_, p3[0]], [dst], out=dst[0:rows, 0:n], in0=x_[0:rows, 1:n + 1], scalar=p3[1],
                       in1=s_[0:rows, 0:n], op0=ALU.mult, op1=ALU.add)

            nld = 0
            for hp in range(2):
                chs = slice(hp * 128, (hp + 1) * 128)
                for j, off in enumerate((0, 256, 512)):
                    self.dma("sync", prm[:, j:j + 1], col(mu[off + hp * 128:off + (hp + 1) * 128]), writes=[prm])
                self.I("vector", "tensor_scalar", [prm], [prm], out=prm[:, 3:6], in0=prm[:, 0:3], scalar1=-1.0, scalar2=1.0, op0=ALU.mult, op1=ALU.add)
                self.I("vector", "tensor_scalar_mul", [prm], [prm], out=prm[:, 6:9], in0=prm[:, 0:3], scalar1=0.5)
                for j, (nm, d) in enumerate((("rwkv_w0", 0), ("rwkv_w0", 1), ("rwkv_a0", 0), ("rwkv_a0", 1))):
                    self.dma("sync", prm[:, 9 + j:10 + j], col(W[nm][l, d, chs]), writes=[prm])
                self.dma("sync", prm[:, 13:14], col(W["rwkv_kk"][l, chs]), writes=[prm])
                self.dma("sync", prm[:, 14:15], col(W["rwkv_ka"][l, chs]), writes=[prm])
                self.dma("sync", prm[:, 16:17], col(W["rwkv_rk"][l].rearrange("h e -> (h e)")[chs]), writes=[prm])
                self.I("vector", "tensor_scalar", [prm], [prm], out=prm[:, 15:16], in0=prm[:, 14:15], scalar1=-1.0, scalar2=1.0, op0=ALU.mult, op1=ALU.add)
                for hh in range(2):
                    h = hp * 2 + hh
                    self.dma("sync", gnb[HR[hh], :], W["rwkv_gn_g"][l, h * 64:(h + 1) * 64].partition_broadcast(64), writes=[gnb])
                for d in range(2):
                    self.I("vector", "memset", [], [ST[0]], ST[0][:], 0.0)
                    nst = 0
                    blocks = BLOCKS if d == 0 else [BLOCKS[0]] + BLOCKS[:0:-1]
                    for (t0, n) in blocks:
                        ncb = n // 64
                        gc0 = t0 // 64
                        v3 = lambda tl_, a=0, b=64: tl_[:, 0:n].rearrange("p (c t) -> p c t", t=64)
                        for (nm, idx, j) in (("rs", 4 + hp, 0), ("ks", 6 + hp, 1), ("vs", 8 + hp, 2)):
                            shift_load(F[nm], 128, self.UF[idx], t0, n, (prm, prm[:, 3 + j:4 + j], prm[:, 6 + j:7 + j]), nld)
                            nld += 1
                        shift_load(F["wl"], 64, self.UF[10, 0:64, :], t0, n, (plw, plw[0:64, 1:2], plw[0:64, 2:3]), nld)
                        nld += 1
                        shift_load(F["al"], 64, self.UF[10, 64:128, :], t0, n, (pla, pla[0:64, 1:2], pla[0:64, 2:3]), nld)
                        nld += 1
                        if d == 0:
                            for hh in range(2):
                                x_ = xb[nld % 2]
                                pass
                            has_prev = t0 not in (0, TC)
                            has_next = (t0 + n) not in (TC, T)
                            lo, hi = int(has_prev), int(has_next)
                            x_, s_ = xb[nld % 2], xs[nld % 2]
                            nld += 1
                            for hh in range(2):
                                self.dma("sync", x_[HR[hh], 1 - lo:n + 1 + hi], self.UF[11, 0:64, t0 - lo:t0 + n + hi], writes=[x_])
                            if not has_prev:
                                self.I("gpsimd", "memset", [], [x_], x_[:, 0:1], 0.0)
                            if not has_next:
                                self.I("gpsimd", "memset", [], [x_], x_[:, n + 1:n + 2], 0.0)
                            self.I("gpsimd", "tensor_tensor", [x_], [s_], out=s_[:, 0:n], in0=x_[:, 0:n], in1=x_[:, 2:n + 2], op=ALU.add)
                            self.I("scalar", "mul", [s_, plg], [s_], out=s_[:, 0:n], in_=s_[:, 0:n], mul=plg[:, 2:3])
                            self.I("vector", "scalar_tensor_tensor", [x_, s_, plg], [F["gl"]], out=F["gl"][:, 0:n], in0=x_[:, 1:n + 1], scalar=plg[:, 1:2],
                                   in1=s_[:, 0:n], op0=ALU.mult, op1=ALU.add)
                            self.I("scalar", "activation", [F["gl"]], [F["sg"]], out=F["sg"][:, 0:n], in_=F["gl"][:, 0:n], func=AF.Sigmoid)
                        self.I("scalar", "activation", [F["wl"]], [F["td"]], out=F["td"][0:64, 0:n], in_=F["wl"][0:64, 0:n], func=AF.Tanh)
                        p_ = nextpg()
                        self.mm(p_, p_[:, 0:n], W2p[d], W2p[d][:, chs], F["td"], F["td"][0:64, 0:n])
                        self.I("scalar", "activation", [p_, prm], [F["ld"]], out=F["ld"][:, 0:n], in_=p_[:, 0:n], func=AF.Sigmoid, bias=prm[:, 9 + d:10 + d])
                        self.I("vector", "tensor_scalar_mul", [F["ld"]], [F["ld"]], out=F["ld"][:, 0:n], in0=F["ld"][:, 0:n], scalar1=-DEC)
                        p_ = nextpg()
                        self.mm(p_, p_[:, 0:n], A2p[d], A2p[d][:, chs], F["al"], F["al"][0:64, 0:n])
                        self.I("scalar", "activation", [p_, prm], [F["a"]], out=F["a"][:, 0:n], in_=p_[:, 0:n], func=AF.Sigmoid, bias=prm[:, 11 + d:12 + d])
                        rv = (lambda ap: ap) if d == 0 else (lambda ap: ap[:, ::-1])
                        self.I("vector", "tensor_tensor_scan", [F["ld"], rmask], [F["cum"]], out=rv(F["cum"][:, 0:n]), data0=rmask[:, 0:n], data1=rv(F["ld"][:, 0:n]),
                               initial=0.0, op0=ALU.mult, op1=ALU.add)
                        self.I("vector", "tensor_tensor", [F["cum"], F["ld"]], [F["Eq"]], out=F["Eq"][:, 0:n], in0=F["cum"][:, 0:n], in1=F["ld"][:, 0:n], op=ALU.subtract)
                        self.I("scalar", "activation", [F["cum"]], [F["Ep"]], out=F["Ep"][:, 0:n], in_=F["cum"][:, 0:n], func=AF.Exp)
                        self.I("scalar", "activation", [F["cum"]], [F["Em"]], out=F["Em"][:, 0:n], in_=F["cum"][:, 0:n], func=AF.Exp, scale=-1.0)
                        self.I("scalar", "activation", [F["Eq"]], [F["Eq"]], out=F["Eq"][:, 0:n], in_=F["Eq"][:, 0:n], func=AF.Exp)
                        self.I("vector", "tensor_scalar_mul", [F["ks"], prm], [F["kkr"]], out=F["kkr"][:, 0:n], in0=F["ks"][:, 0:n], scalar1=prm[:, 13:14])
                        self.I("gpsimd", "tensor_tensor", [F["kkr"]], [F["sq"]], out=F["sq"][:, 0:n], in0=F["kkr"][:, 0:n], in1=F["kkr"][:, 0:n], op=ALU.mult)
                        p_ = nextpg()
                        self.mm(p_, p_[:, 0:n], bones, bones[:], F["sq"], F["sq"][:, 0:n])
                        self.I("vector", "tensor_scalar_max", [p_], [F["rn"]], out=F["rn"][:, 0:n], in0=p_[:, 0:n], scalar1=1e-12)
                        self.rsqrt(F["rn"], F["rn"][:, 0:n], F["rn"], F["rn"][:, 0:n], 0.0)
                        self.I("gpsimd", "tensor_tensor", [F["kkr"], F["rn"]], [F["kk"]], out=F["kk"][:, 0:n], in0=F["kkr"][:, 0:n], in1=F["rn"][:, 0:n], op=ALU.mult)
                        self.I("gpsimd", "tensor_tensor", [F["kk"], F["a"]], [F["b"]], out=F["b"][:, 0:n], in0=F["kk"][:, 0:n], in1=F["a"][:, 0:n], op=ALU.mult)
                        self.I("vector", "tensor_scalar", [F["a"], prm], [F["kd"]], out=F["kd"][:, 0:n], in0=F["a"][:, 0:n], scalar1=prm[:, 14:15], scalar2=prm[:, 15:16],
                               op0=ALU.mult, op1=ALU.add)
                        self.I("gpsimd", "tensor_tensor", [F["kd"], F["ks"]], [F["kd"]], out=F["kd"][:, 0:n], in0=F["kd"][:, 0:n], in1=F["ks"][:, 0:n], op=ALU.mult)
                        self.I("vector", "tensor_tensor", [F["kk"], F["Eq"]], [F["Kap"]], out=F["Kap"][:, 0:n], in0=F["kk"][:, 0:n], in1=F["Eq"][:, 0:n], op=ALU.mult)
                        self.I("gpsimd", "tensor_tensor", [F["kd"], F["Em"]], [F["Kt"]], out=F["Kt"][:, 0:n], in0=F["kd"][:, 0:n], in1=F["Em"][:, 0:n], op=ALU.mult)
                        self.I("vector", "tensor_tensor", [F["b"], F["Em"]], [F["Bt"]], out=F["Bt"][:, 0:n], in0=F["b"][:, 0:n], in1=F["Em"][:, 0:n], op=ALU.mult)
                        self.I("gpsimd", "tensor_tensor", [F["rs"], F["Ep"]], [F["Rt"]], out=F["Rt"][:, 0:n], in0=F["rs"][:, 0:n], in1=F["Ep"][:, 0:n], op=ALU.mult)

                        def gram(dst_ps, L_, R_, nn=64):
                            for c in range(ncb):
                                for hh in range(2):
                                    self.mm(dst_ps, dst_ps[HR[hh], c * 64:c * 64 + 64], L_, L_[HR[hh], c * 64:(c + 1) * 64], R_, R_[HR[hh], c * 64:(c + 1) * 64])

                        def tmm(dst_ps, width, L_, lw_, R_, rw_, off_l=0, off_r=0, start=True, stop=True):
                            for c in range(ncb):
                                for hh in range(2):
                                    self.mm(dst_ps, dst_ps[HR[hh], c * width:c * width + rw_], L_, L_[HR[hh], c, off_l:off_l + lw_], R_, R_[HR[hh], c, off_r:off_r + rw_],
                                            start=start, stop=stop)
                        def tmm2(dst_ps, L1, R1, o1, L2, R2, o2):
                            for c in range(ncb):
                                for hh in range(2):
                                    self.mm(dst_ps, dst_ps[HR[hh], c * 64:c * 64 + 64], L1, L1[HR[hh], c, 0:64], R1, R1[HR[hh], c, o1:o1 + 64], start=True, stop=False)
                                    self.mm(dst_ps, dst_ps[HR[hh], c * 64:c * 64 + 64], L2, L2[HR[hh], c, 0:64], R2, R2[HR[hh], c, o2:o2 + 64], start=False, stop=True)
                        p3 = lambda p_: p_[:, 0:ncb * 64].rearrange("p (c t) -> p c t", t=64)
                        bc = lambda tl_: tl_[:].unsqueeze(1).to_broadcast([128, ncb, 64])
                        m3 = lambda nm: M[nm][:, 0:ncb, :]
                        for (src, kind) in ((F["Kap"], "X"), (F["Bt"], "B"), (F["Kt"], "Kt"), (F["vs"], "V")):
                            p_ = nextpg()
                            for c in range(ncb):
                                for hh in range(2):
                                    self.mm(p_, p_[HR[hh], c * 64:(c + 1) * 64], src, src[HR[hh], c * 64:(c + 1) * 64], self.ident, self.ident[HR[hh], HR[hh]])
                            if kind == "X":
                                self.I("vector", "tensor_copy", [p_], [X], out=X[:, 0:ncb, 0:64], in_=p3(p_))
                            elif kind == "B":
                                self.I("vector", "tensor_copy", [p_], [M["Btok"]], out=m3("Btok"), in_=p3(p_))
                                self.I("scalar", "mul", [M["Btok"]], [M["NBtok"]], out=m3("NBtok"), in_=m3("Btok"), mul=-1.0)
                            elif kind == "Kt":
                                self.I("vector", "tensor_copy", [p_], [M["Kttok"]], out=m3("Kttok"), in_=p3(p_))
                            else:
                                self.I("vector", "tensor_copy", [p_], [M["Vtok"]], out=m3("Vtok"), in_=p3(p_))
                                if d == 0:
                                    self.I("gpsimd", "tensor_copy", [M["Vtok"]], [vtk], out=vtk[:, gc0:gc0 + ncb, :], in_=m3("Vtok"))
                        if d == 0:
                            self.I("gpsimd", "tensor_tensor", [F["rs"], F["ks"]], [F["rk"]], out=F["rk"][:, 0:n], in0=F["rs"][:, 0:n], in1=F["ks"][:, 0:n], op=ALU.mult)
                            self.I("vector", "tensor_scalar_mul", [F["rk"], prm], [F["rk"]], out=F["rk"][:, 0:n], in0=F["rk"][:, 0:n], scalar1=prm[:, 16:17])
                            p_ = nextpg()
                            for c in range(ncb):
                                for hh in range(2):
                                    self.mm(p_, p_[HR[hh], c:c + 1], F["rk"], F["rk"][HR[hh], c * 64:(c + 1) * 64], ones, ones[HR[hh], 0:1])
                            self.I("vector", "tensor_copy", [p_], [rks], out=rks[:, gc0:gc0 + ncb], in_=p_[:, 0:ncb])
                            p_ = nextpg()
                            for c in range(ncb):
                                for hh in range(2):
                                    h = hp * 2 + hh
                                    self.mm(p_, p_[HR[hh], c * 64:(c + 1) * 64], F["sg"], F["sg"][HR[hh], c * 64:(c + 1) * 64], g2r, g2r[HR[hh], h * 64:(h + 1) * 64])
                            self.I("vector", "tensor_copy", [p_], [gat], out=gat[:, gc0:gc0 + ncb, :], in_=p3(p_))
                        su_, sl_, uu_ = (su, sl, uu) if d == 0 else (sl, su, ll)
                        p_ = nextpg()
                        gram(p_, F["Bt"], F["Kap"])
                        self.I("vector", "scalar_tensor_tensor", [p_, su_], [M["At0"]], out=m3("At0"), in0=p3(p_), scalar=-1.0, in1=bc(su_), op0=ALU.mult, op1=ALU.mult)
                        p_ = nextpg()
                        gram(p_, F["Kap"], F["Bt"])
                        self.I("vector", "scalar_tensor_tensor", [p_, sl_], [M["A0"]], out=m3("A0"), in0=p3(p_), scalar=-1.0, in1=bc(sl_), op0=ALU.mult, op1=ALU.mult)
                        p_ = nextpg()
                        gram(p_, F["Kt"], F["Kap"])
                        self.I("vector", "tensor_tensor", [p_, su_], [M["AkT"]], out=m3("AkT"), in0=p3(p_), in1=bc(su_), op=ALU.mult)
                        p_ = nextpg()
                        gram(p_, F["Kt"], F["Rt"])
                        self.I("vector", "tensor_tensor", [p_, uu_], [M["ArT"]], out=m3("ArT"), in0=p3(p_), in1=bc(uu_), op=ALU.mult)
                        p_ = nextpg()
                        gram(p_, F["Bt"], F["Rt"])
                        self.I("vector", "tensor_tensor", [p_, uu_], [M["AbT"]], out=m3("AbT"), in0=p3(p_), in1=bc(uu_), op=ALU.mult)
                        self.I("scalar", "mul", [M["AbT"]], [M["NAbT"]], out=m3("NAbT"), in_=m3("AbT"), mul=-1.0)
                        self.I("gpsimd", "tensor_tensor", [M["At0"], idb], [M["TT0"]], out=m3("TT0"), in0=m3("At0"), in1=bc(idb), op=ALU.add)
                        Ac, Atc, Tc = "A0", "At0", "TT0"
                        for j in range(5):
                            An, Atn, Tn = ("A1", "At1", "TT1") if j % 2 == 0 else ("A0", "At0", "TT0")
                            p_ = nextpg()
                            tmm(p_, 64, M[Atc], 64, M[Ac], 64)
                            self.I("vector", "tensor_copy", [p_], [M[An]], out=m3(An), in_=p3(p_))
                            if j < 4:
                                p_ = nextpg()
                                tmm(p_, 64, M[Ac], 64, M[Atc], 64)
                                self.I("scalar", "copy", [p_], [M[Atn]], out=m3(Atn), in_=p3(p_))
                            p_ = nextpg()
                            tmm(p_, 64, M[An], 64, M[Tc], 64)
                            self.I("vector", "tensor_tensor", [p_, M[Tc]], [M[Tn]], out=m3(Tn), in0=p3(p_), in1=m3(Tc), op=ALU.add)
                            Ac, Atc, Tc = An, Atn, Tn
                        p_ = nextpg()
                        tmm(p_, 64, M["AkT"], 64, M["Vtok"], 64)
                        self.I("vector", "tensor_copy", [p_], [X], out=X[:, 0:ncb, 64:128], in_=p3(p_))
                        for c in range(ncb):
                            for hh in range(2):
                                pw_ = pW[c // 4]
                                self.mm(pw_, pw_[HR[hh], c % 4, :], M[Tc], M[Tc][HR[hh], c, :], X, X[HR[hh], c, :])
                        for half in range((ncb + 3) // 4):
                            cc = min(4, ncb - half * 4)
                            self.I("vector" if half == 0 else "scalar", "tensor_copy" if half == 0 else "copy", [pW[half]], [W2],
                                   out=W2[:, half * 4:half * 4 + cc, :], in_=pW[half][:, 0:cc, :])
                        p_ = nextpg()
                        tmm(p_, 64, W2, 64, M["Btok"], 64)
                        self.I("vector", "scalar_tensor_tensor", [p_, idb], [M["G2"]], out=m3("G2"), in0=p3(p_), scalar=-1.0, in1=bc(idb), op0=ALU.mult, op1=ALU.add)
                        p_ = nextpg()
                        tmm2(p_, M["Kttok"], M["Vtok"], 0, M["NBtok"], W2, 64)
                        pos = 63 if d == 0 else 0
                        pC = F["Ep"][:, 0:n].rearrange("p (c t) -> p c t", t=64)[:, :, pos]
                        self.I("vector", "tensor_tensor", [p_, F["Ep"]], [M["Hp"]], out=m3("Hp"), in0=p3(p_), in1=pC.unsqueeze(2).to_broadcast([128, ncb, 64]), op=ALU.mult)
                        p_ = nextpg()
                        tmm(p_, 64, W2, 64, M["AbT"], 64)
                        self.I("vector", "scalar_tensor_tensor", [p_, F["Rt"]], [M["QpT"]], out=m3("QpT"), in0=p3(p_), scalar=-1.0, in1=v3(F["Rt"]), op0=ALU.mult, op1=ALU.add)
                        p_ = nextpg()
                        tmm2(p_, M["ArT"], M["Vtok"], 0, M["NAbT"], W2, 64)
                        self.I("scalar", "copy", [p_], [M["Yloc"]], out=m3("Yloc"), in_=p3(p_))
                        corder = range(ncb) if d == 0 else range(ncb - 1, -1, -1)
                        for c in corder:
                            S0 = ST[nst % 3]
                            S1 = ST[(nst + 1) % 3]
                            nst += 1
                            for hh in range(2):
                                self.mm(pZ, pZ[HR[hh], 0:64], M["G2"], M["G2"][HR[hh], c, :], S0, S0[HR[hh], :])
                            for hh in range(2):
                                self.mm(pY, pY[HR[hh], c, :], M["QpT"], M["QpT"][HR[hh], c, :], S0, S0[HR[hh], :])
                            self.I("vector", "scalar_tensor_tensor", [pZ, F["Ep"], M["Hp"]], [S1], out=S1[:], in0=pZ[:, 0:64], scalar=F["Ep"][:, c * 64 + pos:c * 64 + pos + 1],
                                   in1=M["Hp"][:, c, :], op0=ALU.mult, op1=ALU.add)
                        if d == 0:
                            self.I("vector", "tensor_tensor", [pY, M["Yloc"]], [yacc], out=yacc[:, gc0:gc0 + ncb, :], in0=pY[:, 0:ncb, :], in1=m3("Yloc"), op=ALU.add)
                        else:
                            self.I("vector", "tensor_tensor", [pY, M["Yloc"]], [M["ytmp"]], out=m3("ytmp"), in0=pY[:, 0:ncb, :], in1=m3("Yloc"), op=ALU.add)
                            self.I("gpsimd", "tensor_tensor", [yacc, M["ytmp"]], [yacc], out=yacc[:, gc0:gc0 + ncb, :], in0=yacc[:, gc0:gc0 + ncb, :], in1=m3("ytmp"), op=ALU.add)
                yb = lambda ap: ap.unsqueeze(2).to_broadcast([128, NCA, 64])
                self.I("vector", "tensor_reduce", [yacc], [red], out=red[:], in_=yacc[:], axis=AX.X, op=ALU.add)
                self.I("vector", "tensor_scalar_mul", [red], [red], out=red[:], in0=red[:], scalar1=-1.0 / 64.0)
                self.I("vector", "tensor_tensor", [yacc, red], [yacc], out=yacc[:], in0=yacc[:], in1=yb(red[:]), op=ALU.add)
                ysq = yacc
                self.I("gpsimd", "tensor_tensor", [yacc], [vtk if False else self._tmp_big(es)], out=self._tmp_big(es)[:], in0=yacc[:], in1=yacc[:], op=ALU.mult)
                tb = self._tmp_big(es)
                self.I("vector", "tensor_reduce", [tb], [red2], out=red2[:], in_=tb[:], axis=AX.X, op=ALU.add)
                self.rsqrt(red2, red2[:], red2, red2[:], 64e-5, scale=1.0 / 64.0)
                self.I("vector", "tensor_tensor", [yacc, red2], [yacc], out=yacc[:], in0=yacc[:], in1=yb(red2[:]), op=ALU.mult)
                self.I("gpsimd", "tensor_tensor", [yacc, gnb], [yacc], out=yacc[:], in0=yacc[:], in1=gnb[:].unsqueeze(1).to_broadcast([128, NCA, 64]), op=ALU.mult)
                self.I("vector", "tensor_tensor", [vtk, rks], [tb], out=tb[:], in0=vtk[:], in1=yb(rks[:]), op=ALU.mult)
                self.I("gpsimd", "tensor_tensor", [yacc, tb], [yacc], out=yacc[:], in0=yacc[:], in1=tb[:], op=ALU.add)
                self.I("vector", "tensor_tensor", [yacc, gat], [yacc], out=yacc[:], in0=yacc[:], in1=gat[:], op=ALU.mult)
                for g8 in range((NCA + 7) // 8):
                    p_ = nextpg()
                    cc = min(8, NCA - g8 * 8)
                    for c in range(cc):
                        gc = g8 * 8 + c
                        for hh in range(2):
                            self.mm(p_, p_[HR[hh], c * 64:(c + 1) * 64], yacc, yacc[HR[hh], gc, :], self.ident, self.ident[HR[hh], HR[hh]])
                    self.I("scalar", "copy", [p_], [ob], out=ob[:, g8 * 512:g8 * 512 + cc * 64], in_=p_[:, 0:cc * 64])
                self.dma("sync", self.OT[2 + hp], ob[:], reads=[ob])
            self.S.barrier()

    def _tmp_big(self, es):
        if getattr(self, "_tb_es", None) is not es:
            self._tb = self.sb(es, "r_tb", [128, T // 64, 64])
            self._tb_es = es
        return self._tb


def build_program(kinds=None, plan=None, layers=(0, 1, 2, 3)):
    b = Builder(kinds=kinds, layers=layers)
    b.declare()
    with b.es:
        b.globals_()
        plan = plan or ["full"]
        if plan == ["full"]:
            b.phase_mod()
            b.phase_x0()
            for l in layers:
                b.phase1(l)
                b.weight_prep(l)
                b.mixer_lru(l)
                b.mixer_mlstm(l)
                b.mixer_mla(l)
                b.mixer_rwkv(l)
                b.phase3(l)
        else:
            for item in plan:
                name, *args = item if isinstance(item, (tuple, list)) else (item,)
                getattr(b, name)(*args)
        b.S.finish_on("sync", b.outdeps)
        b.S.emit()
    return b.nc


def make_consts():
    c = {}
    c["ident"] = np.eye(128, dtype=np.float32)
    c["trif"] = np.triu(np.ones((128, 128), np.float32))
    c["trib"] = np.tril(np.ones((128, 128), np.float32))
    c["ones"] = np.ones((128, 128), np.float32)
    e64 = np.eye(64, dtype=np.float32)
    o64 = np.ones((64, 64), np.float32)
    c["m_su"] = np.concatenate([np.triu(o64, 1)] * 2, 0)
    c["m_sl"] = np.concatenate([np.tril(o64, -1)] * 2, 0)
    c["m_u"] = np.concatenate([np.triu(o64, 0)] * 2, 0)
    c["m_l"] = np.concatenate([np.tril(o64, 0)] * 2, 0)
    c["idb"] = np.concatenate([e64, e64], 0)
    c["bones"] = np.kron(np.eye(2, dtype=np.float32), o64)
    rm = np.ones((128, 512), np.float32)
    rm[:, ::64] = 0.0
    c["rmask"] = rm
    inv = (1.0 / (10000.0 ** (np.arange(8, dtype=np.float32) / 8.0))).astype(np.float32)
    tt = np.arange(4096)
    ang_r = ((tt // 64).astype(np.float32)[:, None] * inv).astype(np.float32)
    ang_c = ((tt % 64).astype(np.float32)[:, None] * inv).astype(np.float32)
    c["cosT"] = np.ascontiguousarray(np.concatenate([np.cos(ang_r), np.cos(ang_r), np.cos(ang_c), np.cos(ang_c)], 1).T.astype(np.float32))
    c["sinT"] = np.ascontiguousarray(np.concatenate([np.sin(ang_r), np.sin(ang_r), np.sin(ang_c), np.sin(ang_c)], 1).T.astype(np.float32))
    return {"k_" + k: v for k, v in c.items()}


_PROG = {}


def kernel(**inputs):
    consts = make_consts()
    if "full" not in _PROG:
        _PROG["full"] = build_program()
    nc = _PROG["full"]
    in_maps = []
    for b in range(4):
        m = {"x": np.ascontiguousarray(inputs["x"][b]), "ctx": np.ascontiguousarray(inputs["ctx"][b]),
             "c": np.ascontiguousarray(inputs["c"][b]), "c_ctx": np.ascontiguousarray(inputs["c_ctx"])}
        for k in WEIGHT_SHAPES:
            m[k] = np.ascontiguousarray(inputs[k])
        m.update(consts)
        in_maps.append(m)
    res = run_bass_kernel_spmd(nc, in_maps, core_ids=list(range(4)))
    return np.stack([r["out"] for r in res.results], 0).astype(np.float32)
```
